# Optimizing a Trainium2 kernel written in Bass

```python
import math
import jax, jax.numpy as jnp
from jax import lax
import numpy as np

D_MODEL = 1024
BATCH = 2
SEQ = 16384
DEPTH = 2
DEC_BATCH = 2
DEC_SEQ = 8192
PAST_LEN = 128

N_META = 16
GRID_W = 64
QUERY_BLOCK = 128
ROPE_THETA = 10000.0
NORM_EPS = 1e-6
A_HEADS = 8
A_KV_HEADS = 2
A_HEAD_DIM = 64
B_HEADS = 4
B_HEAD_DIM = 64
B_V_DIM = 2 * B_HEAD_DIM
C_HEADS = 4
C_NOPE = 128
C_ROPE = 64
C_V = 128
C_Q_LORA = 256
C_KV_LORA = 128
N_BRANCH = 3
MIX_W = 512
D_FF = 4 * D_MODEL
IN_SPLITS = (
    A_HEADS * A_HEAD_DIM, A_KV_HEADS * A_HEAD_DIM, A_KV_HEADS * A_HEAD_DIM,
    2 * B_HEADS * B_HEAD_DIM, 2 * B_HEADS * B_HEAD_DIM, B_HEADS * B_V_DIM,
    C_Q_LORA, C_KV_LORA, C_ROPE,
    N_BRANCH * D_MODEL,
)
IN_COLS = sum(IN_SPLITS)

kernel_name = 'hybrid_gqa_diff_mla_encoder'

F32 = jnp.float32


def rms_norm(x, g):
    xf = x.astype(F32)
    y = xf * lax.rsqrt(jnp.mean(xf * xf, axis=-1, keepdims=True) + NORM_EPS)
    return (y * g.astype(F32)).astype(x.dtype)


def rope(x, pos):
    d = x.shape[-1]
    half = d // 2
    inv = ROPE_THETA ** (-2.0 * jnp.arange(half, dtype=F32) / d)
    ang = pos[:, None] * inv[None, :]
    cos = jnp.cos(ang)[:, None, :]
    sin = jnp.sin(ang)[:, None, :]
    xf = x.astype(F32)
    x1, x2 = xf[..., :half], xf[..., half:]
    return jnp.concatenate([x1 * cos - x2 * sin, x2 * cos + x1 * sin], axis=-1).astype(x.dtype)


def axial_rope(x, row, col):
    h = x.shape[-1] // 2
    return jnp.concatenate([rope(x[..., :h], row), rope(x[..., h:], col)], axis=-1)


def token_positions(n_tokens):
    rows = n_tokens // GRID_W
    row = jnp.concatenate([jnp.full((N_META,), -1.0, F32),
                           jnp.repeat(jnp.arange(rows, dtype=F32), GRID_W)])
    col = jnp.concatenate([jnp.arange(N_META, dtype=F32),
                           jnp.tile(jnp.arange(GRID_W, dtype=F32), rows)])
    lin = jnp.arange(N_META + n_tokens, dtype=F32)
    return row, col, lin


def sweep_query_blocks(block_fn, q_args):
    head = block_fn(*[a[:, :N_META] for a in q_args])

    def split(a):
        rest = a[:, N_META:]
        nblk = rest.shape[1] // QUERY_BLOCK
        rest = rest.reshape(rest.shape[0], nblk, QUERY_BLOCK, *rest.shape[2:])
        return jnp.moveaxis(rest, 1, 0)

    body = lax.map(lambda args: block_fn(*args), tuple(split(a) for a in q_args))
    body = jnp.moveaxis(body, 0, 1)
    body = body.reshape(body.shape[0], -1, *body.shape[3:])
    return jnp.concatenate([head, body], axis=1)


def mixer_gqa(q, k, v, row, col, g_q, g_k):
    bsz, n, _ = q.shape
    grp = A_HEADS // A_KV_HEADS
    q = axial_rope(rms_norm(q.reshape(bsz, n, A_HEADS, A_HEAD_DIM), g_q), row, col)
    k = axial_rope(rms_norm(k.reshape(bsz, n, A_KV_HEADS, A_HEAD_DIM), g_k), row, col)
    v = v.reshape(bsz, n, A_KV_HEADS, A_HEAD_DIM)
    scale = A_HEAD_DIM ** -0.5

    def block(qb):
        qg = qb.reshape(qb.shape[0], qb.shape[1], A_KV_HEADS, grp, A_HEAD_DIM)
        s = jnp.einsum('bqhgd,bkhd->bhgqk', qg, k, preferred_element_type=F32) * scale
        p = jax.nn.softmax(s, axis=-1).astype(v.dtype)
        o = jnp.einsum('bhgqk,bkhd->bqhgd', p, v)
        return o.reshape(qb.shape[0], qb.shape[1], A_HEADS, A_HEAD_DIM)

    o = sweep_query_blocks(block, (q,))
    return o.reshape(bsz, n, A_HEADS * A_HEAD_DIM)


def mixer_diff(q, k, v, lin, g_q, g_k, lq1, lk1, lq2, lk2, g_sub, lam_init):
    bsz, n, _ = q.shape
    q = rms_norm(q.reshape(bsz, n, B_HEADS, 2, B_HEAD_DIM), g_q)
    k = rms_norm(k.reshape(bsz, n, B_HEADS, 2, B_HEAD_DIM), g_k)
    q1, q2 = q[..., 0, :], q[..., 1, :]
    k1, k2 = k[..., 0, :], k[..., 1, :]
    v = v.reshape(bsz, n, B_HEADS, B_V_DIM)
    lam = (jnp.exp(jnp.sum(lq1.astype(F32) * lk1.astype(F32)))
           - jnp.exp(jnp.sum(lq2.astype(F32) * lk2.astype(F32))) + lam_init)
    slopes = 2.0 ** (-8.0 * jnp.arange(1, B_HEADS + 1, dtype=F32) / B_HEADS)
    scale = B_HEAD_DIM ** -0.5

    def block(q1b, q2b, qpos):
        dist = jnp.abs(qpos[0][:, None] - lin[None, :])
        bias = -slopes[:, None, None] * dist[None]
        s1 = jnp.einsum('bqhd,bkhd->bhqk', q1b, k1, preferred_element_type=F32) * scale + bias
        s2 = jnp.einsum('bqhd,bkhd->bhqk', q2b, k2, preferred_element_type=F32) * scale + bias
        pdiff = jax.nn.softmax(s1, axis=-1) - lam * jax.nn.softmax(s2, axis=-1)
        return jnp.einsum('bhqk,bkhe->bqhe', pdiff.astype(v.dtype), v)

    o = sweep_query_blocks(block, (q1, q2, lin[None]))
    o = rms_norm(o, g_sub) * (1.0 - lam_init)
    return o.reshape(bsz, n, B_HEADS * B_V_DIM)


def mixer_mla(cq, ckv, kpe, lin, g_qa, w_qb, g_kva, w_kvb, g_q, g_k):
    bsz, n, _ = cq.shape
    q = (rms_norm(cq, g_qa) @ w_qb).reshape(bsz, n, C_HEADS, C_NOPE + C_ROPE)
    kv = (rms_norm(ckv, g_kva) @ w_kvb).reshape(bsz, n, C_HEADS, C_NOPE + C_V)
    k_nope, v = kv[..., :C_NOPE], kv[..., C_NOPE:]
    kpe = jnp.broadcast_to(kpe[:, :, None, :], (bsz, n, C_HEADS, C_ROPE))
    k = jnp.concatenate([k_nope, kpe], axis=-1)
    q = rms_norm(q, g_q)
    k = rms_norm(k, g_k)
    q = jnp.concatenate([q[..., :C_NOPE], rope(q[..., C_NOPE:], lin)], axis=-1)
    k = jnp.concatenate([k[..., :C_NOPE], rope(k[..., C_NOPE:], lin)], axis=-1)
    scale = (C_NOPE + C_ROPE) ** -0.5

    def block(qb):
        s = jnp.einsum('bqhd,bkhd->bhqk', qb, k, preferred_element_type=F32) * scale
        p = jax.nn.softmax(s, axis=-1).astype(v.dtype)
        return jnp.einsum('bhqk,bkhe->bqhe', p, v)

    o = sweep_query_blocks(block, (q,))
    return o.reshape(bsz, n, C_HEADS * C_V)


def layer(x, pos, l, p):
    row, col, lin = pos
    bsz, n, _ = x.shape
    h = rms_norm(x, p['attn_norm_g'][l])
    z = h @ p['w_in'][l]
    cuts = [int(c) for c in np.cumsum(IN_SPLITS)[:-1]]
    qa, ka, va, qb, kb, vb, cq, ckv, ckpe, gates = jnp.split(z, cuts, axis=-1)
    lam_init = 0.8 - 0.6 * math.exp(-0.3 * l)

    oa = mixer_gqa(qa, ka, va, row, col, p['a_q_norm_g'][l], p['a_k_norm_g'][l])
    ob = mixer_diff(qb, kb, vb, lin, p['b_q_norm_g'][l], p['b_k_norm_g'][l],
                    p['b_lambda_q1'][l], p['b_lambda_k1'][l], p['b_lambda_q2'][l],
                    p['b_lambda_k2'][l], p['b_subln_g'][l], lam_init)
    oc = mixer_mla(cq, ckv, ckpe, lin, p['c_q_a_norm_g'][l], p['c_w_q_b'][l],
                   p['c_kv_a_norm_g'][l], p['c_w_kv_b'][l], p['c_q_norm_g'][l], p['c_k_norm_g'][l])

    g = jax.nn.sigmoid((gates + p['b_gate'][l]).astype(F32)).astype(x.dtype)
    g = g.reshape(bsz, n, N_BRANCH, D_MODEL)
    merged = (g[:, :, 0] * (oa @ p['w_branch_a'][l])
              + g[:, :, 1] * (ob @ p['w_branch_b'][l])
              + g[:, :, 2] * (oc @ p['w_branch_c'][l]))
    x = x + merged @ p['w_out'][l]

    h2 = rms_norm(x, p['mlp_norm_g'][l])
    x = x + jnp.square(jax.nn.relu(h2 @ p['w_up'][l])) @ p['w_down'][l]
    return x


def trunk(x, meta_tokens, p):
    bsz, n_tokens, _ = x.shape
    pos = token_positions(n_tokens)
    meta = jnp.broadcast_to(meta_tokens.astype(x.dtype)[None], (bsz, N_META, D_MODEL))
    h = jnp.concatenate([meta, x], axis=1)
    for l in range(DEPTH):
        h = layer(h, pos, l, p)
    return h[:, N_META:]


def setup_inputs(seed: int = 0) -> dict:
    key = jax.random.key(seed)
    ks = jax.random.split(key, 32)

    def nrm(k, shape, scale):
        return scale * jax.random.normal(k, shape, F32)

    def gain(k, shape):
        return 1.0 + 0.02 * jax.random.normal(k, shape, F32)

    L = DEPTH
    return {
        'x_prompt': nrm(ks[0], (BATCH, SEQ, D_MODEL), 1.0),
        'x_sample': nrm(ks[1], (DEC_BATCH, DEC_SEQ, D_MODEL), 1.0),
        'meta_tokens': nrm(ks[2], (N_META, D_MODEL), 1.0),
        'attn_norm_g': gain(ks[3], (L, D_MODEL)),
        'w_in': nrm(ks[4], (L, D_MODEL, IN_COLS), D_MODEL ** -0.5),
        'b_gate': nrm(ks[5], (L, N_BRANCH * D_MODEL), 0.02),
        'a_q_norm_g': gain(ks[6], (L, A_HEAD_DIM)),
        'a_k_norm_g': gain(ks[7], (L, A_HEAD_DIM)),
        'b_q_norm_g': gain(ks[8], (L, B_HEAD_DIM)),
        'b_k_norm_g': gain(ks[9], (L, B_HEAD_DIM)),
        'b_lambda_q1': nrm(ks[10], (L, B_HEAD_DIM), 0.1),
        'b_lambda_k1': nrm(ks[11], (L, B_HEAD_DIM), 0.1),
        'b_lambda_q2': nrm(ks[12], (L, B_HEAD_DIM), 0.1),
        'b_lambda_k2': nrm(ks[13], (L, B_HEAD_DIM), 0.1),
        'b_subln_g': gain(ks[14], (L, B_V_DIM)),
        'c_q_a_norm_g': gain(ks[15], (L, C_Q_LORA)),
        'c_w_q_b': nrm(ks[16], (L, C_Q_LORA, C_HEADS * (C_NOPE + C_ROPE)), C_Q_LORA ** -0.5),
        'c_kv_a_norm_g': gain(ks[17], (L, C_KV_LORA)),
        'c_w_kv_b': nrm(ks[18], (L, C_KV_LORA, C_HEADS * (C_NOPE + C_V)), C_KV_LORA ** -0.5),
        'c_q_norm_g': gain(ks[19], (L, C_NOPE + C_ROPE)),
        'c_k_norm_g': gain(ks[20], (L, C_NOPE + C_ROPE)),
        'w_branch_a': nrm(ks[21], (L, A_HEADS * A_HEAD_DIM, D_MODEL), (A_HEADS * A_HEAD_DIM) ** -0.5),
        'w_branch_b': nrm(ks[22], (L, B_HEADS * B_V_DIM, D_MODEL), (B_HEADS * B_V_DIM) ** -0.5),
        'w_branch_c': nrm(ks[23], (L, C_HEADS * C_V, D_MODEL), (C_HEADS * C_V) ** -0.5),
        'w_out': nrm(ks[24], (L, D_MODEL, D_MODEL), D_MODEL ** -0.5),
        'mlp_norm_g': gain(ks[25], (L, D_MODEL)),
        'w_up': nrm(ks[26], (L, D_MODEL, D_FF), D_MODEL ** -0.5),
        'w_down': nrm(ks[27], (L, D_FF, D_MODEL), D_FF ** -0.5),
    }


def reference(x_prompt, x_sample, meta_tokens, attn_norm_g, w_in, b_gate,
              a_q_norm_g, a_k_norm_g, b_q_norm_g, b_k_norm_g,
              b_lambda_q1, b_lambda_k1, b_lambda_q2, b_lambda_k2, b_subln_g,
              c_q_a_norm_g, c_w_q_b, c_kv_a_norm_g, c_w_kv_b, c_q_norm_g, c_k_norm_g,
              w_branch_a, w_branch_b, w_branch_c, w_out, mlp_norm_g, w_up, w_down):
    p = dict(attn_norm_g=attn_norm_g, w_in=w_in, b_gate=b_gate,
             a_q_norm_g=a_q_norm_g, a_k_norm_g=a_k_norm_g,
             b_q_norm_g=b_q_norm_g, b_k_norm_g=b_k_norm_g,
             b_lambda_q1=b_lambda_q1, b_lambda_k1=b_lambda_k1,
             b_lambda_q2=b_lambda_q2, b_lambda_k2=b_lambda_k2, b_subln_g=b_subln_g,
             c_q_a_norm_g=c_q_a_norm_g, c_w_q_b=c_w_q_b,
             c_kv_a_norm_g=c_kv_a_norm_g, c_w_kv_b=c_w_kv_b,
             c_q_norm_g=c_q_norm_g, c_k_norm_g=c_k_norm_g,
             w_branch_a=w_branch_a, w_branch_b=w_branch_b, w_branch_c=w_branch_c,
             w_out=w_out, mlp_norm_g=mlp_norm_g, w_up=w_up, w_down=w_down)
    y_prompt = trunk(x_prompt, meta_tokens, p)
    y_sample = trunk(x_sample, meta_tokens, p)
    return (y_prompt, y_sample)
```

```python
import math
from contextlib import ExitStack
import numpy as np
import concourse.bass as bass
import concourse.mybir as mybir
from concourse.bass_utils import run_bass_kernel_spmd

F32 = mybir.dt.float32
BF16 = mybir.dt.bfloat16
AF = mybir.ActivationFunctionType
ALU = mybir.AluOpType
AX = mybir.AxisListType

NCORE = 8
D = 1024
NMETA = 16
DEPTH = 2
EPS = 1e-6
GRID_W = 64
THETA = 10000.0
NQKV = 2752
NAUG = 24
ENGS = ["tensor", "vector", "scalar", "gpsimd", "sync"]


class _Rec:
    def __getattr__(self, name):
        def f(*a, **kw):
            return (name, a, kw)
        return f


_REC = _Rec()


class DSem:
    def __init__(self, h):
        self.h = h
        self.count = 0


class Prog:
    def __init__(self, nc, es):
        self.nc = nc
        self.es = es
        self.q = {e: [] for e in ENGS}
        self.tl = {e: es.enter_context(nc.semaphore("tl_" + e)) for e in ENGS}
        self.seq = {e: 0 for e in ENGS}
        self.waited = {}
        self.nsem = 0
        self.dsems = []
        self.ep_arrive = es.enter_context(nc.semaphore("ep_arrive"))
        self.ep_go = es.enter_context(nc.semaphore("ep_go"))
        self.nbar = 0

    def dsem(self):
        self.nsem += 1
        d = DSem(self.es.enter_context(self.nc.semaphore("ds%d" % self.nsem)))
        self.dsems.append(d)
        return d

    def reset_barrier(self):
        toks = [(self.tl[e], self.seq[e]) for e in ENGS if self.seq[e] > 0]
        toks += [(d.h, d.count) for d in self.dsems if d.count > 0]
        for e in ENGS:
            self.wait_only(e, toks)
        self.nbar += 1
        for e in ("tensor", "vector", "scalar"):
            self.tl[e] = self.es.enter_context(self.nc.semaphore("tl_%s_%d" % (e, self.nbar)))
            self.seq[e] = 0

    def _waits(self, eng, deps):
        waits = []
        for d in deps:
            if d is None:
                continue
            if isinstance(d, list):
                waits += self._waits(eng, d)
                continue
            sem, val = d
            key = (eng, id(sem))
            if self.waited.get(key, 0) >= val:
                continue
            self.waited[key] = val
            waits.append((sem, val))
        return waits

    def op(self, eng, fn, deps=()):
        waits = self._waits(eng, deps)
        self.seq[eng] += 1
        self.q[eng].append((fn(_REC), waits, self.tl[eng], 1))
        return (self.tl[eng], self.seq[eng])

    def dma(self, eng, out, in_, sem, deps=()):
        waits = self._waits(eng, deps)
        sem.count += 16
        self.q[eng].append((("dma_start", (), dict(out=out, in_=in_)), waits, sem.h, 16))
        return (sem.h, sem.count)

    def wait_only(self, eng, deps):
        waits = self._waits(eng, deps)
        if waits:
            self.q[eng].append((None, waits, None, 0))

    def replay(self, block):
        for eng in ENGS:
            items = self.q[eng]

            def body(e, items=items):
                for fn, waits, sem, inc in items:
                    for (sm, v) in waits:
                        e.wait_ge(sm, v)
                    if fn is not None:
                        name, a, kw = fn
                        ins = getattr(e, name)(*a, **kw)
                        if sem is not None:
                            ins.then_inc(sem, inc)
            getattr(block, eng)(body)


def chunks(total, size):
    return [(o, min(size, total - o)) for o in range(0, total, size)]


def build_program(NP, NS, KBLK=2048, QG=2048, QC=512, debug=False, stop=10 ** 9):
    NSEQ = [NP, NS]
    RQ = [NP // 4, NS // 4]
    KB_ = [min(KBLK, RQ[s]) for s in range(2)]
    QG_ = [min(QG, RQ[s]) for s in range(2)]
    TOK = NP + NS
    OFF = [0, NP]
    NMT = 2 * NMETA
    TALL = TOK + NMT
    TOKQ = RQ[0] + RQ[1]
    nc = bass.Bass("TRN2", target_bir_lowering=False)
    es = ExitStack()

    def din(name, shape, dt=F32):
        return nc.dram_tensor(name, list(shape), dt, kind="ExternalInput").ap()

    def dint(name, shape, dt):
        return nc.dram_tensor(name, list(shape), dt, kind=("ExternalOutput" if debug else "Internal")).ap()

    x_in = din("x_in", [TOK, D])
    meta_in = din("meta_in", [NMT, D])
    PNAMES = [("b_gate", 3072), ("attn_norm_g", 1024), ("mlp_norm_g", 1024), ("a_q_norm_g", 64), ("a_k_norm_g", 64),
              ("b_q_norm_g", 64), ("b_k_norm_g", 64), ("b_lambda_q1", 64), ("b_lambda_k1", 64), ("b_lambda_q2", 64),
              ("b_lambda_k2", 64), ("b_subln_g", 128), ("c_q_a_norm_g", 256), ("c_kv_a_norm_g", 128),
              ("c_q_norm_g", 192), ("c_k_norm_g", 192)]
    PTOT = sum(n for _, n in PNAMES)
    LD = 1
    params = din("params", [LD, PTOT])
    lamc = din("lamc", [1, 2])
    pv = {}
    o_ = 0
    for nm_, n_ in PNAMES:
        pv[nm_] = params[:, o_:o_ + n_]
        o_ += n_
    b_gate, attn_g, mlp_g = pv["b_gate"], pv["attn_norm_g"], pv["mlp_norm_g"]
    g64 = pv
    b_sub, c_qa_g, c_kva_g, c_q_g, c_k_g = pv["b_subln_g"], pv["c_q_a_norm_g"], pv["c_kv_a_norm_g"], pv["c_q_norm_g"], pv["c_k_norm_g"]
    w_in_d = din("w_in", [LD * D, 5824])
    w_up_d = din("w_up", [LD * D, 4096])
    w_down_d = din("w_down", [LD * 4096, D])
    w_out_d = din("w_out", [LD * D, D])
    w_br_d = [din("w_branch_" + n, [LD * 512, D]) for n in "abc"]
    c_wqb_d = din("c_w_q_b", [LD * 256, 768])
    c_wkvb_d = din("c_w_kv_b", [LD * 128, 1024])
    w_in = [w_in_d[l * D:(l + 1) * D, :] for l in range(LD)]
    w_up = [w_up_d[l * D:(l + 1) * D, :] for l in range(LD)]
    w_down = [w_down_d[l * 4096:(l + 1) * 4096, :] for l in range(LD)]
    w_out = [w_out_d[l * D:(l + 1) * D, :] for l in range(LD)]
    w_br = [[w_br_d[i][l * 512:(l + 1) * 512, :] for l in range(LD)] for i in range(3)]
    c_wqb = [c_wqb_d[l * 256:(l + 1) * 256, :] for l in range(LD)]
    c_wkvb = [c_wkvb_d[l * 128:(l + 1) * 128, :] for l in range(LD)]
    ident_in = din("ident_in", [128, 128])
    ropeA = din("ropeA", [TALL, 128])
    ropeC = din("ropeC", [TALL, 128])
    qaug_f = [din("qaug%d" % s, [4 * NAUG, NSEQ[s] + 2 * NMETA]) for s in range(2)]
    kaug_f = [din("kaug%d" % s, [4 * NAUG, NSEQ[s] + NMETA]) for s in range(2)]
    kaugl_f = [din("kaugl%d" % s, [8 * NAUG, RQ[s]]) for s in range(2)]
    qaug_b = [dint("qaugb%d" % s, [4 * NAUG, NSEQ[s] + 2 * NMETA], BF16) for s in range(2)]
    kaug_b = [dint("kaugb%d" % s, [4 * NAUG, NSEQ[s] + NMETA], BF16) for s in range(2)]
    kaugl_b = [dint("kauglb%d" % s, [8 * NAUG, RQ[s]], BF16) for s in range(2)]
    qaug = [qaug_b[s].rearrange("(h r) t -> h r t", r=NAUG) for s in range(2)]
    kaug = [kaug_b[s].rearrange("(h r) t -> h r t", r=NAUG) for s in range(2)]
    kaugl = [kaugl_b[s].rearrange("(h g r) t -> h g r t", g=2, r=NAUG) for s in range(2)]
    y_out = nc.dram_tensor("y_out", [TOKQ + NMT, D], F32, kind="ExternalOutput").ap()

    xmid = dint("xmid", [TALL, D], F32)
    x1buf = dint("x1buf", [TALL, D], F32)
    kvl = [dint("kvl%d" % s, [2560, NSEQ[s]], BF16) for s in range(2)]
    kvm = [dint("kvm%d" % s, [2560, NMETA], BF16) for s in range(2)]
    qloc = [dint("qloc%d" % s, [14 * 128, NSEQ[s] + NMETA], BF16) for s in range(2)]
    oA = [dint("oA%d" % s, [512, NSEQ[s] + NMETA], BF16) for s in range(2)]
    oC = [dint("oC%d" % s, [512, NSEQ[s] + NMETA], BF16) for s in range(2)]
    oB = [dint("oB%d" % s, [1024, NSEQ[s] + NMETA], F32) for s in range(2)]

    def sb(name, shape, dt=F32):
        return es.enter_context(nc.sbuf_tensor(name, list(shape), dt))

    P = Prog(nc, es)
    RMAX = max(KB_ + QG_)
    TQM = RMAX + 2 * NMETA
    ident = sb("ident", [128, 128], BF16)
    identf = sb("identf", [128, 128])
    ones_bf = sb("ones_bf", [128, 1], BF16)
    ones_f = sb("ones_f", [128, 128])
    mean_f = sb("mean_f", [128, 128])
    WBIG = sb("wbig", [128, 64 * 1024], BF16)
    wstg = [sb("wstg%d" % i, [128, 1024]) for i in range(2)]
    WORK = sb("work", [128, 14336], F32)
    gT = sb("gT", [128, 8])
    gT2 = sb("gT2", [128, 8])
    bgT = sb("bgT", [128, 24])
    gsub = sb("gsub", [128, 1])
    lamt = sb("lamt", [128, 8])
    lrow = sb("lrow", [1, 4 * 64 + 8])
    gains = WBIG[:, 57344:57344 + 7168].bitcast(F32)
    pst = es.enter_context(nc.psum_tensor("pst", [128, 1024], BF16))
    ps = es.enter_context(nc.psum_tensor("ps", [128, 7, 512], F32))
    pstf = pst[:, :].bitcast(F32)

    dsem_w = [P.dsem(), P.dsem()]
    dsem_x = [P.dsem(), P.dsem()]
    dsem_misc = P.dsem()
    dsem_st = P.dsem()
    dsem_kv = [P.dsem() for _ in range(3)]
    dsem_q = [P.dsem(), P.dsem()]
    dsem_o = P.dsem()
    dsem_tab = [P.dsem(), P.dsem()]
    state = {"wstg_free": [None, None], "wn": 0}

    def work(off, shape, dt=F32):
        n = int(np.prod(shape[1:]))
        if dt == F32:
            v = WORK[:, off:off + n]
        else:
            v = WORK[:, off:off + (n + 1) // 2].bitcast(BF16)[:, 0:n]
        if len(shape) == 3:
            v = v.rearrange("p (a b) -> p a b", b=shape[2])
        elif len(shape) == 4:
            v = v.rearrange("p (a b c) -> p a b c", b=shape[2], c=shape[3])
        return v[0:shape[0]]

    def wview(off, KC, C):
        return WBIG[:, off:off + KC * C].rearrange("p (k c) -> p k c", c=C)

    def load_w(dst, src, deps=()):
        KC, C = dst.shape[1], dst.shape[2]
        toks = {}
        for kc in range(KC):
            for (c0, w) in chunks(C, 1024):
                i = state["wn"] % 2
                state["wn"] += 1
                t = P.dma("sync", wstg[i][:, 0:w], src[kc * 128:(kc + 1) * 128, c0:c0 + w], dsem_w[i],
                          deps=[state["wstg_free"][i]])
                eng = "gpsimd" if (state["wn"] % 2) else "vector"
                ct = P.op(eng, lambda e, o=dst[:, kc, c0:c0 + w], s=wstg[i][:, 0:w]: e.tensor_copy(out=o, in_=s),
                          deps=[t] + list(deps))
                state["wstg_free"][i] = ct
                toks[eng] = ct
        return list(toks.values())

    def bcast_row(src_row, n):
        return bass.AP(src_row.tensor, src_row.offset, [[0, 128], [1, n]])

    t0 = P.dma("sync", identf[:], ident_in, dsem_misc)
    t_id = P.op("vector", lambda e: e.tensor_copy(out=ident[:], in_=identf[:]), deps=[t0])
    P.op("vector", lambda e: e.memset(ones_bf[:], 1.0))
    P.op("vector", lambda e: e.memset(ones_f[:], 1.0))
    t_const = P.op("vector", lambda e: e.memset(mean_f[:], 1.0 / 128.0))
    cstg = [work(0, [128, 1024], BF16), work(512, [128, 1024], BF16)]
    cfree = [None, None]
    cn = 0
    import os
    tabs = list(zip(qaug_f, qaug_b)) + list(zip(kaug_f, kaug_b)) + list(zip(kaugl_f, kaugl_b))
    if os.environ.get('K_SKIP_TAB'):
        tabs = tabs[:int(os.environ['K_SKIP_TAB']) - 1]
    for (srcT, dstT) in tabs:
        nr, ncol = srcT.shape
        for (r0, rh) in chunks(nr, 128):
            for (c0, w) in chunks(ncol, 1024):
                i = cn % 2
                cn += 1
                t = P.dma("sync", wstg[i][0:rh, 0:w], srcT[r0:r0 + rh, c0:c0 + w], dsem_w[i], deps=[state["wstg_free"][i]])
                ct = P.op("vector", lambda e, i=i, rh=rh, w=w: e.tensor_copy(out=cstg[i][0:rh, 0:w], in_=wstg[i][0:rh, 0:w]), deps=[t, cfree[i]])
                state["wstg_free"][i] = ct
                cfree[i] = P.dma("gpsimd", dstT[r0:r0 + rh, c0:c0 + w], cstg[i][0:rh, 0:w], dsem_tab[i], deps=[ct])
    phase_done = [t_const, t_id, cfree[0], cfree[1]]
    pcount = {"n": 0}

    def skip_phase():
        pcount["n"] += 1
        return pcount["n"] > stop

    def barrier(tokens):
        for e in ENGS:
            P.wait_only(e, tokens)

    def rmsnorm_T(xt, g_t, hT, tcol, ntile_deps, scratch_off):
        junk = work(scratch_off, [128, 1024])
        xn = work(scratch_off + 1024, [128, 1024], BF16)
        st = work(scratch_off + 1536, [128, 4])
        a = P.op("scalar", lambda e: e.activation(out=junk, in_=xt, func=AF.Square, accum_out=st[:, 0:1]),
                 deps=ntile_deps)
        b = P.op("scalar", lambda e: e.activation(out=st[:, 1:2], in_=st[:, 0:1], func=AF.Sqrt,
                                                  scale=1.0 / D, bias=EPS), deps=[a])
        c = P.op("vector", lambda e: e.reciprocal(out=st[:, 2:3], in_=st[:, 1:2]), deps=[b] + list(ntile_deps))
        d = P.op("vector", lambda e: e.tensor_scalar(out=xn, in0=xt, scalar1=st[:, 2:3], scalar2=None,
                                                     op0=ALU.mult), deps=[c])
        last = None
        for k in range(8):
            last = P.op("tensor", lambda e, k=k: e.transpose(out=pst[:, k * 128:(k + 1) * 128],
                                                             in_=xn[:, k * 128:(k + 1) * 128], identity=ident[:]),
                        deps=[d] + list(ntile_deps))
        f = P.op("vector", lambda e: e.tensor_tensor(
            out=hT[:, :, tcol:tcol + 128], in0=pst[:, :].rearrange("p (k t) -> p k t", t=128),
            in1=g_t[:, :].unsqueeze(2).to_broadcast([128, 8, 128]), op=ALU.mult), deps=[last])
        return f

    def phase_p1(l, xsrc_reg, xsrc_meta):
        nonlocal phase_done
        if skip_phase():
            return
        P.reset_barrier()
        state["wstg_free"] = [None, None]
        pd = []
        Wq = wview(0, 8, NQKV)
        Wqb = wview(8 * NQKV, 2, 768)
        Wkvb = wview(8 * NQKV + 2 * 768, 1, 1024)
        wt = load_w(Wq, w_in[l][:, 0:NQKV], deps=pd)
        wt += load_w(Wqb, c_wqb[l], deps=pd)
        wt += load_w(Wkvb, c_wkvb[l], deps=pd)
        gt = []
        gt.append(P.dma("sync", gT[:], attn_g[l].rearrange("(k p) -> p k", p=128), dsem_misc, deps=pd))
        col = 0
        for nm, rep in [("a_q_norm_g", 8), ("a_k_norm_g", 2), ("b_q_norm_g", 8), ("b_k_norm_g", 8)]:
            for r in range(rep):
                gt.append(P.dma("sync", gains[:, col:col + 64], bcast_row(g64[nm][l:l + 1, :], 64), dsem_misc, deps=pd))
                col += 64
        gt.append(P.dma("sync", gains[:, col:col + 256], bcast_row(c_qa_g[l:l + 1, :], 256), dsem_misc, deps=pd)); col += 256
        gt.append(P.dma("sync", gains[:, col:col + 128], bcast_row(c_kva_g[l:l + 1, :], 128), dsem_misc, deps=pd)); col += 128
        for r in range(4):
            gt.append(P.dma("sync", gains[:, col:col + 192], bcast_row(c_q_g[l:l + 1, :], 192), dsem_misc, deps=pd)); col += 192
        for r in range(4):
            gt.append(P.dma("sync", gains[:, col:col + 192], bcast_row(c_k_g[l:l + 1, :], 192), dsem_misc, deps=pd)); col += 192
        setup = wt + [gt[-1]]
        G1 = gains[:, 0:640]
        G2 = gains[:, 640:1664]
        GCQA = gains[:, 1664:1920]
        GCKVA = gains[:, 1920:2048]

        xt = [work(0, [128, 1024]), work(1024, [128, 1024])]
        hT = work(2048, [128, 8, 128], BF16)
        z = work(2560, [128, NQKV])
        tmp = work(5312, [128, NQKV])
        zb = work(8064, [128, NQKV], BF16)
        ssq = work(9440, [128, 40])
        rst = work(9480, [128, 40])
        rope = work(9520, [128, 256])
        cqn = work(9776, [128, 384], BF16)
        cT = work(9968, [128, 3, 128], BF16)
        zcb = work(11696, [128, 2, 4, 128], BF16)
        zrb = work(12208, [128, 2, 256], BF16)
        vst = work(12464, [128, 1152], BF16)
        kst = WBIG[:, 40 * 1024:40 * 1024 + 11 * 512].rearrange("p (b t) -> p b t", t=512)
        qst = WBIG[:, 40 * 1024 + 11 * 512:40 * 1024 + 25 * 512].rearrange("p (b t) -> p b t", t=512)

        tile_list = []
        for s in range(2):
            nt = NSEQ[s] // 128
            gsz = min(4, nt)
            for t in range(nt):
                tile_list.append((s, OFF[s] + t * 128, 128, t % gsz, (t % gsz) == gsz - 1, t * 128, gsz))
        tile_list.append((None, 0, NMT, 0, True, 0, 1))

        prev_tile_done = {"tok": None}
        xfree = [None, None]
        st_prev = {"k": None, "q": None, "v": None}
        for ti, (s, row0, ntok, gcol, glast, tok0, gsz) in enumerate(tile_list):
            i = ti % 2
            X = xt[i]
            ptd = prev_tile_done["tok"]
            ld = []
            if s is None:
                mz = P.op("gpsimd", lambda e, X=X: e.memset(X, 0.0), deps=[xfree[i]] + pd)
                ld.append(P.dma("sync", X[0:NMT, :], xsrc_meta, dsem_x[i], deps=[mz]))
                ld.append(P.dma("sync", rope[0:NMT, 0:128], ropeA[TOK:TALL, :], dsem_x[i], deps=[ptd] + pd))
                ld.append(P.dma("sync", rope[0:NMT, 128:256], ropeC[TOK:TALL, :], dsem_x[i], deps=[ptd] + pd))
            else:
                ld.append(P.dma("sync", X, xsrc_reg[row0:row0 + 128, :], dsem_x[i], deps=[xfree[i]] + pd))
                ld.append(P.dma("sync", rope[:, 0:128], ropeA[row0:row0 + 128, :], dsem_x[i], deps=[ptd] + pd))
                ld.append(P.dma("sync", rope[:, 128:256], ropeC[row0:row0 + 128, :], dsem_x[i], deps=[ptd] + pd))
            ldt = ld[-1]
            f = rmsnorm_T(X, gT, hT, 0, [ldt, ptd] + setup, 5312)
            xfree[i] = f
            colch = [(0, 512), (512, 256), (768, 512), (1280, 512), (1792, 512), (2304, 448)]
            ev = []
            for bi, (c0, w) in enumerate(colch):
                mm = None
                for k in range(8):
                    mm = P.op("tensor", lambda e, bi=bi, c0=c0, w=w, k=k: e.matmul(
                        ps[:, bi, 0:w], lhsT=hT[:, k, :], rhs=Wq[:, k, c0:c0 + w], start=(k == 0), stop=(k == 7)),
                        deps=[f, ptd])
                ev.append(P.op("scalar", lambda e, bi=bi, c0=c0, w=w: e.activation(
                    out=z[:, c0:c0 + w], in_=ps[:, bi, 0:w], func=AF.Copy), deps=[mm, ptd]))
            evl = ev[-1]
            a1 = P.op("gpsimd", lambda e: e.tensor_tensor(out=tmp[:, :], in0=z[:, :], in1=z[:, :], op=ALU.mult), deps=[evl, f])
            r1 = P.op("vector", lambda e: e.tensor_reduce(out=ssq[:, 0:10], in_=tmp[:, 0:640].rearrange("p (h d) -> p h d", d=64), axis=AX.X, op=ALU.add), deps=[a1, ptd])
            r2 = P.op("vector", lambda e: e.tensor_reduce(out=ssq[:, 10:26], in_=tmp[:, 768:1792].rearrange("p (h d) -> p h d", d=64), axis=AX.X, op=ALU.add), deps=[a1])
            r3 = P.op("vector", lambda e: e.tensor_reduce(out=ssq[:, 26:27], in_=tmp[:, 2304:2560], axis=AX.X, op=ALU.add), deps=[a1])
            r4 = P.op("vector", lambda e: e.tensor_reduce(out=ssq[:, 27:28], in_=tmp[:, 2560:2688], axis=AX.X, op=ALU.add), deps=[a1])
            s1 = P.op("scalar", lambda e: e.activation(out=rst[:, 0:26], in_=ssq[:, 0:26], func=AF.Sqrt, scale=1.0 / 64, bias=EPS), deps=[r1, r2, ptd])
            s2 = P.op("scalar", lambda e: e.activation(out=rst[:, 26:27], in_=ssq[:, 26:27], func=AF.Sqrt, scale=1.0 / 256, bias=EPS), deps=[r3])
            s3 = P.op("scalar", lambda e: e.activation(out=rst[:, 27:28], in_=ssq[:, 27:28], func=AF.Sqrt, scale=1.0 / 128, bias=EPS), deps=[r4])
            rc = P.op("vector", lambda e: e.reciprocal(out=rst[:, 0:28], in_=rst[:, 0:28]), deps=[s1, s2, s3])
            n1 = P.op("vector", lambda e: e.tensor_tensor(out=z[:, 0:640].rearrange("p (h d) -> p h d", d=64), in0=z[:, 0:640].rearrange("p (h d) -> p h d", d=64), in1=rst[:, 0:10].unsqueeze(2).to_broadcast([128, 10, 64]), op=ALU.mult), deps=[rc, a1])
            n2 = P.op("vector", lambda e: e.tensor_tensor(out=z[:, 768:1792].rearrange("p (h d) -> p h d", d=64), in0=z[:, 768:1792].rearrange("p (h d) -> p h d", d=64), in1=rst[:, 10:26].unsqueeze(2).to_broadcast([128, 16, 64]), op=ALU.mult), deps=[rc, a1])
            n3 = P.op("vector", lambda e: e.tensor_scalar(out=z[:, 2304:2560], in0=z[:, 2304:2560], scalar1=rst[:, 26:27], scalar2=None, op0=ALU.mult), deps=[rc, a1])
            n4 = P.op("vector", lambda e: e.tensor_scalar(out=z[:, 2560:2688], in0=z[:, 2560:2688], scalar1=rst[:, 27:28], scalar2=None, op0=ALU.mult), deps=[rc, a1])
            g1 = P.op("gpsimd", lambda e: e.tensor_tensor(out=z[:, 0:640], in0=z[:, 0:640], in1=G1, op=ALU.mult), deps=[n1])
            g2 = P.op("gpsimd", lambda e: e.tensor_tensor(out=zb[:, 768:1792], in0=z[:, 768:1792], in1=G2, op=ALU.mult), deps=[n2, ptd])
            g3 = P.op("gpsimd", lambda e: e.tensor_tensor(out=cqn[:, 0:256], in0=z[:, 2304:2560], in1=GCQA, op=ALU.mult), deps=[n3, ptd])
            g4 = P.op("gpsimd", lambda e: e.tensor_tensor(out=cqn[:, 256:384], in0=z[:, 2560:2688], in1=GCKVA, op=ALU.mult), deps=[n4, ptd])
            zv = z[:, 0:640].rearrange("p (h r a d) -> p h r a d", r=2, a=2, d=16)
            tv = tmp[:, 0:640].rearrange("p (h r a d) -> p h r a d", r=2, a=2, d=16)
            sinv = rope[:, 64:128].rearrange("p (r a d) -> p r a d", r=2, a=2)
            ra = P.op("vector", lambda e: e.tensor_tensor(out=tmp[:, 768:1408].rearrange("p (h d) -> p h d", d=64), in0=z[:, 0:640].rearrange("p (h d) -> p h d", d=64), in1=rope[:, 0:64].unsqueeze(1).to_broadcast([128, 10, 64]), op=ALU.mult), deps=[g1, ldt, r2, r3, r4])
            rb = []
            for hh in range(2):
                rb.append(P.op("vector", lambda e, hh=hh: e.tensor_tensor(
                    out=tv[:, :, :, hh, :], in0=zv[:, :, :, 1 - hh, :],
                    in1=sinv[:, :, hh, :].unsqueeze(1).to_broadcast([128, 10, 2, 16]),
                    op=ALU.mult), deps=[g1, ldt, r1]))
            rd = P.op("vector", lambda e: e.tensor_tensor(out=zb[:, 0:640], in0=tmp[:, 768:1408], in1=tmp[:, 0:640], op=ALU.add), deps=[ra] + rb + [ptd])
            v1 = P.op("scalar", lambda e: e.activation(out=vst[:, 0:128], in_=z[:, 640:768], func=AF.Copy), deps=[evl, st_prev["v"]])
            v2 = P.op("scalar", lambda e: e.activation(out=vst[:, 128:640], in_=z[:, 1792:2304], func=AF.Copy), deps=[evl])
            tr = None
            for k in range(3):
                tr = P.op("tensor", lambda e, k=k: e.transpose(out=pst[:, k * 128:(k + 1) * 128], in_=cqn[:, k * 128:(k + 1) * 128], identity=ident[:]), deps=[g3, g4, f])
            ctc = P.op("vector", lambda e: e.tensor_copy(out=cT[:, :, :], in_=pst[:, 0:384].rearrange("p (k t) -> p k t", t=128)), deps=[tr, ptd])
            mm = None
            for j in range(2):
                for k in range(2):
                    mm = P.op("tensor", lambda e, j=j, k=k: e.matmul(ps[:, j, 0:384], lhsT=cT[:, k, :], rhs=Wqb[:, k, j * 384:(j + 1) * 384], start=(k == 0), stop=(k == 1)), deps=[ctc, evl])
            for j in range(2):
                mm = P.op("tensor", lambda e, j=j: e.matmul(ps[:, 2 + j, :], lhsT=cT[:, 2, :], rhs=Wkvb[:, 0, j * 512:(j + 1) * 512], start=True, stop=True), deps=[ctc, evl])
            zqk = WORK[:, 10160:10160 + 1536]
            zq = zqk[:, 0:768]
            zk = zqk[:, 768:1536]
            e1 = P.op("scalar", lambda e: e.activation(out=zq.rearrange("p (j c) -> p j c", c=384), in_=ps[:, 0:2, 0:384], func=AF.Copy), deps=[mm, ptd])
            kvv = ps[:, 2:4, :].rearrange("p j (h c) -> p (j h) c", c=256)
            e2 = P.op("scalar", lambda e: e.activation(out=zk.rearrange("p (h c) -> p h c", c=192)[:, :, 0:128], in_=kvv[:, :, 0:128], func=AF.Copy), deps=[mm, ptd])
            e3 = P.op("scalar", lambda e: e.activation(out=vst[:, 640:1152].rearrange("p (h c) -> p h c", c=128), in_=kvv[:, :, 128:256], func=AF.Copy), deps=[mm])
            e4 = P.op("gpsimd", lambda e: e.tensor_copy(out=zk.rearrange("p (h c) -> p h c", c=192)[:, :, 128:192], in_=z[:, 2688:2752].unsqueeze(1).to_broadcast([128, 4, 64])), deps=[evl, ptd, g4])
            a2 = P.op("gpsimd", lambda e: e.tensor_tensor(out=tmp[:, 0:1536], in0=zqk, in1=zqk, op=ALU.mult), deps=[e1, e2, e4, rd])
            r5 = P.op("vector", lambda e: e.tensor_reduce(out=ssq[:, 28:36], in_=tmp[:, 0:1536].rearrange("p (h d) -> p h d", d=192), axis=AX.X, op=ALU.add), deps=[a2])
            s5 = P.op("scalar", lambda e: e.activation(out=rst[:, 28:36], in_=ssq[:, 28:36], func=AF.Sqrt, scale=1.0 / 192, bias=EPS), deps=[r5])
            rc5 = P.op("vector", lambda e: e.reciprocal(out=rst[:, 28:36], in_=rst[:, 28:36]), deps=[s5])
            n5 = P.op("vector", lambda e: e.tensor_tensor(out=zqk.rearrange("p (h d) -> p h d", d=192), in0=zqk.rearrange("p (h d) -> p h d", d=192), in1=rst[:, 28:36].unsqueeze(2).to_broadcast([128, 8, 192]), op=ALU.mult), deps=[rc5, a2])
            g5 = P.op("gpsimd", lambda e: e.tensor_tensor(out=zqk, in0=zqk, in1=gains[:, 2048:3584], op=ALU.mult), deps=[n5])
            zqk4 = zqk.rearrange("p (h d) -> p h d", d=192)
            c1 = P.op("scalar", lambda e: e.activation(out=zcb[:, :, :, :].rearrange("p a h d -> p (a h) d"), in_=zqk4[:, :, 0:128], func=AF.Copy), deps=[g5, ptd])
            rp = zqk4[:, :, 128:192].rearrange("p h (a d) -> p h a d", a=2)
            t64 = tmp[:, 0:512].rearrange("p (h d) -> p h d", d=64)
            u64 = tmp[:, 512:1024].rearrange("p (h a d) -> p h a d", a=2, d=32)
            sinc = rope[:, 192:256].rearrange("p (a d) -> p a d", a=2)
            qa_ = P.op("vector", lambda e: e.tensor_tensor(out=t64, in0=zqk4[:, :, 128:192], in1=rope[:, 128:192].unsqueeze(1).to_broadcast([128, 8, 64]), op=ALU.mult), deps=[g5, ldt, r5])
            qb_ = []
            for hh in range(2):
                qb_.append(P.op("vector", lambda e, hh=hh: e.tensor_tensor(out=u64[:, :, hh, :], in0=rp[:, :, 1 - hh, :], in1=sinc[:, hh, :].unsqueeze(1).to_broadcast([128, 8, 32]), op=ALU.mult), deps=[g5, ldt, r5]))
            qd_ = P.op("vector", lambda e: e.tensor_tensor(out=zrb[:, :, :].rearrange("p a (h d) -> p (a h) d", d=64), in0=t64, in1=tmp[:, 512:1024].rearrange("p (h d) -> p h d", d=64), op=ALU.add), deps=[qa_] + qb_ + [ptd])
            srcs = [("k", 0, zb[:, 512:640])]
            for b in range(4):
                srcs.append(("k", 1 + b, zb[:, 1280 + b * 128:1280 + (b + 1) * 128]))
            for b in range(4):
                srcs.append(("k", 5 + b, zcb[:, 1, b, :]))
            for b in range(2):
                srcs.append(("k", 9 + b, zrb[:, 1, b * 128:(b + 1) * 128]))
            for b in range(4):
                srcs.append(("q", b, zb[:, b * 128:(b + 1) * 128]))
            for b in range(4):
                srcs.append(("q", 4 + b, zb[:, 768 + b * 128:768 + (b + 1) * 128]))
            for b in range(4):
                srcs.append(("q", 8 + b, zcb[:, 0, b, :]))
            for b in range(2):
                srcs.append(("q", 12 + b, zrb[:, 0, b * 128:(b + 1) * 128]))
            alld = [rd, g2, c1, qd_]
            cp_last = ctc
            cps = []
            for r0 in range(0, len(srcs), 8):
                grp = srcs[r0:r0 + 8]
                tr = None
                for j, (dst, blk, src) in enumerate(grp):
                    tr = P.op("tensor", lambda e, j=j, src=src: e.transpose(out=pst[:, j * 128:(j + 1) * 128], in_=src, identity=ident[:]), deps=alld + [cp_last])
                j = 0
                while j < len(grp):
                    j2 = j
                    while j2 + 1 < len(grp) and grp[j2 + 1][0] == grp[j][0] and grp[j2 + 1][1] == grp[j2][1] + 1:
                        j2 += 1
                    dstt = kst if grp[j][0] == "k" else qst
                    b0 = grp[j][1]
                    nb = j2 - j + 1
                    cp_last = P.op("vector", lambda e, dstt=dstt, b0=b0, nb=nb, j=j: e.tensor_copy(
                        out=dstt[:, b0:b0 + nb, gcol * 128:(gcol + 1) * 128],
                        in_=pst[:, j * 128:(j + nb) * 128].rearrange("p (b t) -> p b t", t=128)),
                        deps=[tr, st_prev["k"], st_prev["q"]])
                    cps.append(cp_last)
                    j = j2 + 1
            prev_tile_done["tok"] = [cp_last, qd_, rd, g2, c1, v1, v2, e3]
            if s is None:
                for s4 in range(2):
                    sk = P.dma("gpsimd", kvm[s4][0:1408, :].rearrange("(b p) t -> p b t", p=128), kst[:, :, s4 * 16:(s4 + 1) * 16], dsem_st, deps=cps)
                    sq_ = P.dma("gpsimd", qloc[s4][:, NSEQ[s4]:NSEQ[s4] + NMETA].rearrange("(b p) t -> p b t", p=128), qst[:, :, s4 * 16:(s4 + 1) * 16], dsem_st, deps=cps)
                    sv = P.dma("gpsimd", bass.AP(kvm[s4].tensor, 1408 * NMETA, [[1152, NMETA], [1, 1152]]), vst[s4 * 16:(s4 + 1) * 16, :], dsem_st, deps=[v1, v2, e3])
                st_prev["k"], st_prev["q"], st_prev["v"] = sk, sq_, sv
            else:
                sv = P.dma("gpsimd", bass.AP(kvl[s].tensor, 1408 * NSEQ[s] + tok0 * 1152, [[1152, 128], [1, 1152]]), vst[:, :], dsem_st, deps=[v1, v2, e3])
                st_prev["v"] = sv
                if glast:
                    gw = gsz * 128
                    g0 = tok0 - (gsz - 1) * 128
                    sk = P.dma("gpsimd", kvl[s][0:1408, g0:g0 + gw].rearrange("(b p) t -> p b t", p=128), kst[:, :, 0:gw], dsem_st, deps=cps)
                    sq_ = P.dma("gpsimd", qloc[s][:, g0:g0 + gw].rearrange("(b p) t -> p b t", p=128), qst[:, :, 0:gw], dsem_st, deps=cps)
                    st_prev["k"], st_prev["q"] = sk, sq_
        phase_done = [prev_tile_done["tok"], (dsem_st.h, dsem_st.count)]

    def phase_att(l, s, qcol0, qn, qquarter, is_meta):
        nonlocal phase_done
        if skip_phase():
            return
        P.reset_barrier()
        state["wstg_free"] = [None, None]
        pd = []
        N = NSEQ[s]
        Rq = RQ[s]
        KBs = KB_[s]
        if is_meta:
            qchunks = [(0, NMETA)]
            TQ = NMETA
        else:
            qchunks = chunks(qn, QC)
            TQ = qn
        NCH = len(qchunks)
        kbuf = [WBIG[:, (i * 2) * RMAX:(i * 2 + 2) * RMAX].rearrange("p (m t) -> p m t", t=RMAX) for i in range(3)]
        vo = 6 * RMAX
        VT = max(1, RMAX // 128)
        vbuf = [WBIG[:, vo + i * VT * 130: vo + (i + 1) * VT * 130].rearrange("p (t c) -> p t c", c=130) for i in range(3)]
        qo = vo + 3 * VT * 130
        qbuf = [WBIG[:, qo + i * 4 * TQM: qo + (i + 1) * 4 * TQM].rearrange("p (m t) -> p m t", t=TQM) for i in range(2)]
        po = qo + 8 * TQM
        pbuf = WBIG[:, po:po + 4 * 512].rearrange("p (i t) -> p i t", t=512)
        assert po + 2048 + 1536 <= 64 * 1024
        oacc = work(0, [128, 4, NCH, 512])
        sacc = work(2 * NCH * 512, [1, 2, NCH, 512])
        assert 4 * NCH * 512 <= 12800
        tmpA = WORK[:, 12800:13312]
        tmpB = WORK[:, 13312:13824]
        rcp = WORK[:, 13824:14336]
        ostA = WBIG[:, po + 2048:po + 2048 + 512]
        ostB = WBIG[:, po + 2560:po + 2560 + 1024].bitcast(F32)

        t_ones = [P.op("gpsimd", lambda e, i=i: e.memset(vbuf[i][:, :, 128:129], 1.0), deps=pd) for i in range(3)]
        passes = [("A", 0), ("A", 1)] + [("B", h) for h in range(4)] + [("C", h) for h in range(4)]
        S = {"step": 0, "unit": 0, "pv_tok": [], "blk": 0, "exp_tok": [], "ep_free": None, "ost_free": [None, None],
             "bank6": None, "last_evac": {}, "evacs": [], "epn": 0, "last_store": None}
        slot_free = [None, None, None]
        qb_free = [None, None]
        kblocks = [(c0, w, c0 // Rq, "g") for (c0, w) in chunks(N, KBs)] + [(N, NMETA, 4, "m")]

        for pi, (kind, idx) in enumerate(passes):
            qi = pi % 2
            Q = qbuf[qi]
            nm = {"A": 4, "B": 2, "C": 1}[kind]
            qz = P.op("gpsimd", lambda e, Q=Q: e.memset(Q[:, :, :], 0.0), deps=[qb_free[qi]] + pd)
            qsrc0 = N if is_meta else qcol0
            qt = []
            if kind == "A":
                for m in range(4):
                    h = idx * 4 + m
                    qt.append(P.dma("sync", Q[idx * 64:(idx + 1) * 64, m, 0:TQ], qloc[s][h * 64:(h + 1) * 64, qsrc0:qsrc0 + TQ], dsem_q[qi], deps=[qz]))
            elif kind == "B":
                for m in range(2):
                    mp = idx * 2 + m
                    qt.append(P.dma("sync", Q[0:64, m, 0:TQ], qloc[s][512 + mp * 64:512 + (mp + 1) * 64, qsrc0:qsrc0 + TQ], dsem_q[qi], deps=[qz]))
                    if is_meta:
                        qt.append(P.dma("sync", Q[0:64, m, TQ:2 * TQ], qloc[s][512 + mp * 64:512 + (mp + 1) * 64, qsrc0:qsrc0 + TQ], dsem_q[qi], deps=[qz]))
                        qt.append(P.dma("sync", Q[64:64 + NAUG, m, 0:2 * TQ], qaug[s][idx, :, N:N + 2 * NMETA], dsem_q[qi], deps=[qz]))
                    else:
                        qt.append(P.dma("sync", Q[64:64 + NAUG, m, 0:TQ], qaug[s][idx, :, qcol0:qcol0 + TQ], dsem_q[qi], deps=[qz]))
            else:
                qt.append(P.dma("sync", Q[:, 0, 0:TQ], qloc[s][1024 + idx * 128:1024 + (idx + 1) * 128, qsrc0:qsrc0 + TQ], dsem_q[qi], deps=[qz]))
                hh = idx % 2
                qt.append(P.dma("sync", Q[hh * 64:(hh + 1) * 64, 1, 0:TQ], qloc[s][1536 + idx * 64:1536 + (idx + 1) * 64, qsrc0:qsrc0 + TQ], dsem_q[qi], deps=[qz]))
            qtok = qt[-1]
            blocks = []
            for (c0, w, bq, bk) in kblocks:
                if kind == "B" and bk == "g" and (not is_meta) and bq == qquarter:
                    blocks.append((c0, w, bq, "l", 0))
                    blocks.append((c0, w, bq, "l", 1))
                else:
                    blocks.append((c0, w, bq, bk, None))
            first = {}
            for (kc0, nkeys, bq, bk, lm) in blocks:
                si = S["blk"] % 3
                S["blk"] += 1
                KB, VB = kbuf[si], vbuf[si]
                sem = dsem_kv[si]
                dps = [slot_free[si]] + pd + [t_ones[si]]
                ntb = max(1, nkeys // 128)
                if bk == "m":
                    ksrc, kcs = kvm[s], 0
                else:
                    ksrc, kcs = kvl[s], kc0
                nrow = ksrc.shape[1]

                def vload(c0, w, dcol):
                    base = 1408 * nrow + kcs * 1152
                    if bk == "m":
                        return P.dma("sync", VB[0:NMETA, 0, dcol:dcol + w], bass.AP(ksrc.tensor, base + c0, [[1152, NMETA], [1, w]]), sem, deps=dps)
                    return P.dma("sync", VB[:, 0:ntb, dcol:dcol + w], bass.AP(ksrc.tensor, base + c0, [[1152, 128], [128 * 1152, ntb], [1, w]]), sem, deps=dps)
                kt = []
                if kind == "A":
                    kt.append(P.dma("sync", KB[:, 0, 0:nkeys], ksrc[0:128, kcs:kcs + nkeys], sem, deps=dps))
                    kt.append(vload(idx * 64, 64, 64))
                elif kind == "B":
                    for m in range(2):
                        mp = idx * 2 + (m if bk != "l" else lm)
                        kt.append(P.dma("sync", KB[0:64, m, 0:nkeys], ksrc[128 + mp * 64:128 + (mp + 1) * 64, kcs:kcs + nkeys], sem, deps=dps))
                        if bk == "l":
                            rel = kc0 - bq * Rq
                            kt.append(P.dma("sync", KB[64:64 + NAUG, m, 0:nkeys], kaugl[s][idx, m, :, rel:rel + nkeys], sem, deps=dps))
                        else:
                            kt.append(P.dma("sync", KB[64:64 + NAUG, m, 0:nkeys], kaug[s][idx, :, kc0:kc0 + nkeys], sem, deps=dps))
                    kt.append(vload(128 + idx * 128, 128, 0))
                else:
                    kt.append(P.dma("sync", KB[:, 0, 0:nkeys], ksrc[640 + idx * 128:640 + (idx + 1) * 128, kcs:kcs + nkeys], sem, deps=dps))
                    kt.append(P.dma("sync", KB[:, 1, 0:nkeys], ksrc[1152 + (idx // 2) * 128:1152 + (idx // 2 + 1) * 128, kcs:kcs + nkeys], sem, deps=dps))
                    kt.append(vload(640 + idx * 128, 128, 0))
                ktok = kt[-1]
                ktiles = [(0, NMETA)] if bk == "m" else [(t * 128, 128) for t in range(ntb)]
                for m in ([lm] if bk == "l" else list(range(nm))):
                    for ci, (q0, qw) in enumerate(qchunks):
                        qrel = (qcol0 - qquarter * Rq + q0) if not is_meta else 0
                        krel = kc0 - bq * Rq if bk != "m" else 0
                        fb = (m, ci) not in first
                        first[(m, ci)] = True
                        run_unit(S, kind, m, ci, q0, qw, is_meta, bk, KB, VB, Q, ktiles, ktok, qtok,
                                 oacc, sacc, pbuf, tmpA, tmpB, fb, qrel, krel)
                slot_free[si] = S["pv_tok"][-1]
            qb_free[qi] = S["pv_tok"][-1]
            for m in range(nm):
                for ci, (q0, qw) in enumerate(qchunks):
                    M = 64 if kind == "A" else 128
                    if kind == "A":
                        lrow_ap = oacc[64:65, m, ci, 0:qw]
                        onesl = ones_f[64:65, 0:M]
                    else:
                        lrow_ap = sacc[0:1, m, ci, 0:qw]
                        onesl = ones_f[0:1, 0:M]
                    evac = S["last_evac"][(m, ci)]
                    bc = P.op("tensor", lambda e, lrow_ap=lrow_ap, onesl=onesl, M=M, qw=qw: e.matmul(
                        ps[0:M, 6, 0:qw], lhsT=onesl, rhs=lrow_ap, start=True, stop=True), deps=[evac, S["bank6"]])
                    r1 = P.op("vector", lambda e, M=M, qw=qw: e.reciprocal(out=rcp[0:M, 0:qw], in_=ps[0:M, 6, 0:qw]), deps=[bc, S["ep_free"]])
                    S["bank6"] = r1
                    dcol = (N if is_meta else qcol0) + q0
                    if kind == "B":
                        ost, key = ostB, 0
                        dst = oB[s][(idx * 2 + m) * 128:(idx * 2 + m + 1) * 128, dcol:dcol + qw]
                    elif kind == "A":
                        ost, key = ostA, 1
                        h = idx * 4 + m
                        dst = oA[s][h * 64:(h + 1) * 64, dcol:dcol + qw]
                    else:
                        ost, key = ostA, 1
                        dst = oC[s][idx * 128:(idx + 1) * 128, dcol:dcol + qw]
                    r2 = P.op("vector", lambda e, M=M, qw=qw, m=m, ci=ci, ost=ost: e.tensor_tensor(
                        out=ost[0:M, 0:qw], in0=oacc[0:M, m, ci, 0:qw], in1=rcp[0:M, 0:qw], op=ALU.mult),
                        deps=[r1, S["ost_free"][key]])
                    S["ep_free"] = r2
                    stt = P.dma("gpsimd", dst, ost[0:M, 0:qw], dsem_o, deps=[r2])
                    S["ost_free"][key] = stt
        phase_done = [S["pv_tok"][-1], (dsem_o.h, dsem_o.count), S["ep_free"]]

    def run_unit(S, kind, m, ci, q0, qw, is_meta, bk, KB, VB, Q, ktiles, ktok, qtok,
                 oacc, sacc, pbuf, tmpA, tmpB, first_block, qrel, krel):
        scale = {"A": 0.125, "B": 0.125, "C": 192 ** -0.5}[kind]
        shift = {"A": -8.0, "B": -8.0, "C": -(192 ** 0.5)}[kind]
        u = S["unit"]
        S["unit"] += 1
        ob = 3 + (u % 2)
        sum_ps = ps[0:1, 5, :] if (u % 2 == 0) else pstf[0:1, :]
        M = 65 if kind == "A" else 128
        evac_dep = S["evacs"][-2] if len(S["evacs"]) >= 2 else None
        nt = len(ktiles)
        KA = 64 + NAUG
        pv = None
        for ti, (k0, kw) in enumerate(ktiles):
            i = S["step"]
            S["step"] += 1
            sbk = i % 3
            pslot = i % 4
            diag = False
            var = 0
            if kind == "B":
                if bk == "m" and is_meta:
                    diag = True
                elif bk == "l":
                    if krel + k0 + kw <= qrel:
                        var = 0
                    elif krel + k0 >= qrel + qw:
                        var = 1
                    else:
                        diag = True
            sfree = S["exp_tok"][i - 3] if i >= 3 else None

            def qk(bank, variant, xd=()):
                xd = list(xd)
                if kind == "A":
                    return P.op("tensor", lambda e: e.matmul(ps[0:kw, bank, 0:qw], lhsT=KB[:, 0, k0:k0 + kw], rhs=Q[:, m, q0:q0 + qw], start=True, stop=True), deps=[ktok, qtok, sfree] + xd)
                if kind == "C":
                    P.op("tensor", lambda e: e.matmul(ps[0:kw, bank, 0:qw], lhsT=KB[:, 0, k0:k0 + kw], rhs=Q[:, 0, q0:q0 + qw], start=True, stop=False), deps=[ktok, qtok, sfree] + xd)
                    return P.op("tensor", lambda e: e.matmul(ps[0:kw, bank, 0:qw], lhsT=KB[:, 1, k0:k0 + kw], rhs=Q[:, 1, q0:q0 + qw], start=False, stop=True), deps=[])
                if bk == "l":
                    kslice = KB[0:KA, variant, k0:k0 + kw]
                    qcol = q0
                elif bk == "m" and is_meta:
                    kslice = KB[0:KA, m, k0:k0 + kw]
                    qcol = q0 + (NMETA if variant == 1 else 0)
                else:
                    kslice = KB[0:KA, m, k0:k0 + kw]
                    qcol = q0
                return P.op("tensor", lambda e: e.matmul(ps[0:kw, bank, 0:qw], lhsT=kslice, rhs=Q[0:KA, m, qcol:qcol + qw], start=True, stop=True), deps=[ktok, qtok, sfree] + xd)
            pfree = S["pv_tok"][i - 4] if i >= 4 else None
            if not diag:
                mm = qk(sbk, var)
                ex = P.op("scalar", lambda e: e.activation(out=pbuf[0:kw, pslot, 0:qw], in_=ps[0:kw, sbk, 0:qw], func=AF.Exp, scale=scale, bias=shift), deps=[mm, pfree])
            else:
                mm0 = qk(sbk, 0)
                mm1 = qk(6, 1, [S.get("diag_free"), S.get("bank6")])
                cA = P.op("scalar", lambda e: e.activation(out=tmpA[0:kw, 0:qw], in_=ps[0:kw, sbk, 0:qw], func=AF.Copy), deps=[mm0, S.get("diag_free")])
                mn = P.op("vector", lambda e: e.tensor_tensor(out=tmpB[0:kw, 0:qw], in0=ps[0:kw, 6, 0:qw], in1=tmpA[0:kw, 0:qw], op=ALU.min), deps=[cA, mm1, S.get("diag_free2")])
                S["diag_free"] = mn
                S["bank6"] = mn
                ex = P.op("scalar", lambda e: e.activation(out=pbuf[0:kw, pslot, 0:qw], in_=tmpB[0:kw, 0:qw], func=AF.Exp, scale=scale, bias=shift), deps=[mn, pfree])
                S["diag_free2"] = ex
            S["exp_tok"].append(ex)
            if kind == "A":
                lhs = VB[0:kw, k0 // 128, 64:129]
            else:
                lhs = VB[0:kw, k0 // 128, 0:128]
            pv = P.op("tensor", lambda e: e.matmul(ps[0:M, ob, 0:qw], lhsT=lhs, rhs=pbuf[0:kw, pslot, 0:qw], start=(ti == 0), stop=(ti == nt - 1)), deps=[ex, evac_dep if ti == 0 else None])
            if kind != "A":
                pv = P.op("tensor", lambda e: e.matmul(sum_ps[:, 0:qw], lhsT=ones_bf[0:kw, 0:1], rhs=pbuf[0:kw, pslot, 0:qw], start=(ti == 0), stop=(ti == nt - 1)), deps=[])
            S["pv_tok"].append(pv)
        prev = S["last_evac"].get((m, ci))
        if first_block:
            ev = P.op("vector", lambda e: e.tensor_copy(out=oacc[0:M, m, ci, 0:qw], in_=ps[0:M, ob, 0:qw]), deps=[pv, S["ep_free"]])
            if kind != "A":
                ev = P.op("vector", lambda e: e.tensor_copy(out=sacc[0:1, m, ci, 0:qw], in_=sum_ps[:, 0:qw]), deps=[pv, S["ep_free"]])
        else:
            ev = P.op("vector", lambda e: e.tensor_tensor(out=oacc[0:M, m, ci, 0:qw], in0=ps[0:M, ob, 0:qw], in1=oacc[0:M, m, ci, 0:qw], op=ALU.add), deps=[pv, prev])
            if kind != "A":
                ev = P.op("vector", lambda e: e.tensor_tensor(out=sacc[0:1, m, ci, 0:qw], in0=sum_ps[:, 0:qw], in1=sacc[0:1, m, ci, 0:qw], op=ALU.add), deps=[pv, prev])
        S["last_evac"][(m, ci)] = ev
        S["evacs"].append(ev)

    def phase_p5(l, groups, xsrc_reg, xsrc_meta, dst_fn):
        nonlocal phase_done
        if skip_phase():
            return
        P.reset_barrier()
        state["wstg_free"] = [None, None]
        pd = []
        tl0 = P.dma("sync", lrow[0:1, 262:264], lamc, dsem_misc, deps=pd)
        tl1 = P.dma("sync", lamt[:, 1:2], bcast_row(lamc[0:1, 1:2], 1), dsem_misc, deps=pd)
        Wg = wview(0, 8, 3072)
        Wb = [wview(8 * 3072 + i * 4 * 1024, 4, 1024) for i in range(3)]
        Wo = wview(8 * 3072 + 12 * 1024, 8, 1024)
        wt = load_w(Wg, w_in[l][:, NQKV:5824], deps=pd)
        for i in range(3):
            wt += load_w(Wb[i], w_br[i][l], deps=pd)
        wt += load_w(Wo, w_out[l], deps=pd)
        t1 = P.dma("sync", gT[:], attn_g[l].rearrange("(k p) -> p k", p=128), dsem_misc, deps=pd)
        t1 = P.dma("sync", gT2[:], mlp_g[l].rearrange("(k p) -> p k", p=128), dsem_misc, deps=pd)
        t1 = P.dma("sync", bgT[:], b_gate[l].rearrange("(k p) -> p k", p=128), dsem_misc, deps=pd)
        t1 = P.dma("sync", gsub[:], b_sub[l].rearrange("(p o) -> p o", o=1), dsem_misc, deps=pd)
        for j, nm in enumerate(["b_lambda_q1", "b_lambda_k1", "b_lambda_q2", "b_lambda_k2"]):
            t1 = P.dma("sync", lrow[0:1, j * 64:(j + 1) * 64], g64[nm][l:l + 1, :], dsem_misc, deps=pd)
        la = P.op("vector", lambda e: e.tensor_tensor(out=lrow[0:1, 0:64], in0=lrow[0:1, 0:64], in1=lrow[0:1, 64:128], op=ALU.mult), deps=[t1])
        lb = P.op("vector", lambda e: e.tensor_tensor(out=lrow[0:1, 128:192], in0=lrow[0:1, 128:192], in1=lrow[0:1, 192:256], op=ALU.mult), deps=[t1])
        lc = P.op("vector", lambda e: e.tensor_reduce(out=lrow[0:1, 256:257], in_=lrow[0:1, 0:64], axis=AX.X, op=ALU.add), deps=[la])
        ld_ = P.op("vector", lambda e: e.tensor_reduce(out=lrow[0:1, 257:258], in_=lrow[0:1, 128:192], axis=AX.X, op=ALU.add), deps=[lb])
        le = P.op("scalar", lambda e: e.activation(out=lrow[0:1, 258:260], in_=lrow[0:1, 256:258], func=AF.Exp), deps=[lc, ld_])
        lf = P.op("vector", lambda e: e.tensor_tensor(out=lrow[0:1, 260:261], in0=lrow[0:1, 259:260], in1=lrow[0:1, 258:259], op=ALU.subtract), deps=[le])
        lg = P.op("vector", lambda e: e.tensor_scalar(out=lrow[0:1, 261:262], in0=lrow[0:1, 260:261], scalar1=lrow[0:1, 262:263], scalar2=None, op0=ALU.add), deps=[lf, tl0])
        lh = P.op("tensor", lambda e: e.matmul(ps[:, 6, 0:1], lhsT=ones_f[0:1, :], rhs=lrow[0:1, 261:262], start=True, stop=True), deps=[lg] + pd)
        li = P.op("vector", lambda e: e.tensor_copy(out=lamt[:, 0:1], in_=ps[:, 6, 0:1]), deps=[lh])
        lj = P.op("vector", lambda e: e.tensor_scalar(out=gsub[:], in0=gsub[:], scalar1=lamt[:, 1:2], scalar2=None, op0=ALU.mult), deps=[t1, tl1])
        setup = wt + [li, lj]

        GW = 256
        xt = work(0, [128, 2, 1024])
        hT = work(2048, [128, 8, GW], BF16)
        gTt = work(3072, [128, 24, GW], BF16)
        oAs = work(6144, [128, 4, GW], BF16)
        oCs = work(6656, [128, 4, GW], BF16)
        oBs = work(7168, [128, 8, GW])
        dd = work(9216, [128, GW])
        sqv = work(9472, [128, GW])
        obn = work(9728, [128, 4, GW], BF16)
        mT = work(10240, [128, 8, GW], BF16)
        macc = work(11264, [128, GW])
        mtmp = work(11520, [128, GW])
        x1t = work(11776, [128, 1024])
        prevg = None
        st1 = None
        for gi, (s, row0, ntok, tok0) in enumerate(groups):
            ntile = (ntok + 127) // 128
            pg = prevg
            ld = []
            if s is None:
                mz = P.op("gpsimd", lambda e: e.memset(xt[:, 0, :], 0.0), deps=[pg] + pd)
                ld.append(P.dma("sync", xt[0:NMT, 0, :], xsrc_meta, dsem_x[0], deps=[mz]))
            else:
                for t in range(ntile):
                    ld.append(P.dma("sync", xt[:, t, :], xsrc_reg[row0 + t * 128:row0 + (t + 1) * 128, :], dsem_x[0], deps=[pg] + pd))
            fl = []
            for t in range(ntile):
                fl.append(rmsnorm_T(xt[:, t, :], gT, hT, t * 128, [ld[-1], pg] + setup + fl, 11776))
            W_ = ntile * 128
            ol = []
            if s is None:
                mz2 = P.op("gpsimd", lambda e: e.memset(WORK[:, 6144:9216], 0.0), deps=[pg] + pd)
                for s4 in range(2):
                    c0 = NSEQ[s4]
                    ol.append(P.dma("sync", oAs[:, :, s4 * 16:(s4 + 1) * 16], oA[s4][:, c0:c0 + NMETA].rearrange("(b p) t -> p b t", p=128), dsem_x[1], deps=[mz2]))
                    ol.append(P.dma("sync", oCs[:, :, s4 * 16:(s4 + 1) * 16], oC[s4][:, c0:c0 + NMETA].rearrange("(b p) t -> p b t", p=128), dsem_x[1], deps=[mz2]))
                    ol.append(P.dma("sync", oBs[:, :, s4 * 16:(s4 + 1) * 16], oB[s4][:, c0:c0 + NMETA].rearrange("(b p) t -> p b t", p=128), dsem_x[1], deps=[mz2]))
            else:
                ol.append(P.dma("sync", oAs[:, :, 0:ntok], oA[s][:, tok0:tok0 + ntok].rearrange("(b p) t -> p b t", p=128), dsem_x[1], deps=[pg] + pd))
                ol.append(P.dma("sync", oCs[:, :, 0:ntok], oC[s][:, tok0:tok0 + ntok].rearrange("(b p) t -> p b t", p=128), dsem_x[1], deps=[pg] + pd))
                ol.append(P.dma("sync", oBs[:, :, 0:ntok], oB[s][:, tok0:tok0 + ntok].rearrange("(b p) t -> p b t", p=128), dsem_x[1], deps=[pg] + pd))
            olt = ol[-1]
            lastb = None
            for h in range(4):
                d1 = P.op("vector", lambda e, h=h: e.scalar_tensor_tensor(out=dd[:, 0:W_], in0=oBs[:, 2 * h + 1, 0:W_], scalar=lamt[:, 0:1], in1=oBs[:, 2 * h, 0:W_], op0=ALU.mult, op1=ALU.add), deps=[olt, lastb] + setup)
                d2 = P.op("scalar", lambda e: e.activation(out=sqv[:, 0:W_], in_=dd[:, 0:W_], func=AF.Square), deps=[d1, lastb])
                d3 = P.op("tensor", lambda e: e.matmul(ps[:, 5, 0:W_], lhsT=mean_f[:, :], rhs=sqv[:, 0:W_], start=True, stop=True), deps=[d2, lastb] + fl)
                d4 = P.op("scalar", lambda e: e.activation(out=sqv[:, 0:W_], in_=ps[:, 5, 0:W_], func=AF.Sqrt, bias=EPS), deps=[d3])
                d5 = P.op("vector", lambda e: e.reciprocal(out=sqv[:, 0:W_], in_=sqv[:, 0:W_]), deps=[d4])
                d6 = P.op("vector", lambda e: e.tensor_tensor(out=dd[:, 0:W_], in0=dd[:, 0:W_], in1=sqv[:, 0:W_], op=ALU.mult), deps=[d5])
                lastb = P.op("vector", lambda e, h=h: e.tensor_scalar(out=obn[:, h, 0:W_], in0=dd[:, 0:W_], scalar1=gsub[:, 0:1], scalar2=None, op0=ALU.mult), deps=[d6, pg])
            gacts = []
            for j in range(24):
                bank = j % 3
                mm = None
                for k in range(8):
                    mm = P.op("tensor", lambda e, j=j, k=k, bank=bank: e.matmul(ps[:, bank, 0:W_], lhsT=Wg[:, k, j * 128:(j + 1) * 128], rhs=hT[:, k, 0:W_], start=(k == 0), stop=(k == 7)), deps=fl + [gacts[j - 3] if j >= 3 else None, pg])
                gacts.append(P.op("scalar", lambda e, j=j, bank=bank: e.activation(out=gTt[:, j, 0:W_], in_=ps[:, bank, 0:W_], func=AF.Sigmoid, bias=bgT[:, j:j + 1]), deps=[mm, pg, lastb]))
            gates_done = gacts[-1]
            srcs = [oAs, obn, oCs]
            ml = None
            for j in range(8):
                mms = []
                for br in range(3):
                    mm = None
                    for k in range(4):
                        mm = P.op("tensor", lambda e, j=j, br=br, k=k: e.matmul(ps[:, br, 0:W_], lhsT=Wb[br][:, k, j * 128:(j + 1) * 128], rhs=srcs[br][:, k, 0:W_], start=(k == 0), stop=(k == 3)), deps=[gates_done, olt, lastb, ml])
                    mms.append(mm)
                a0 = P.op("vector", lambda e, j=j: e.tensor_tensor(out=macc[:, 0:W_], in0=ps[:, 0, 0:W_], in1=gTt[:, j, 0:W_], op=ALU.mult), deps=[mms[0], gates_done, ml])
                a1 = P.op("vector", lambda e, j=j: e.tensor_tensor(out=mtmp[:, 0:W_], in0=ps[:, 1, 0:W_], in1=gTt[:, 8 + j, 0:W_], op=ALU.mult), deps=[mms[1], ml])
                a2 = P.op("vector", lambda e: e.tensor_tensor(out=macc[:, 0:W_], in0=macc[:, 0:W_], in1=mtmp[:, 0:W_], op=ALU.add), deps=[a0, a1])
                a3 = P.op("vector", lambda e, j=j: e.tensor_tensor(out=mtmp[:, 0:W_], in0=ps[:, 2, 0:W_], in1=gTt[:, 16 + j, 0:W_], op=ALU.mult), deps=[mms[2], a2])
                ml = P.op("vector", lambda e, j=j: e.tensor_tensor(out=mT[:, j, 0:W_], in0=macc[:, 0:W_], in1=mtmp[:, 0:W_], op=ALU.add), deps=[a3, pg])
            last = ml
            for t in range(ntile):
                mm = None
                for c in range(2):
                    for k in range(8):
                        mm = P.op("tensor", lambda e, t=t, c=c, k=k: e.matmul(ps[:, 3 + c, :], lhsT=mT[:, k, t * 128:(t + 1) * 128], rhs=Wo[:, k, c * 512:(c + 1) * 512], start=(k == 0), stop=(k == 7)), deps=[ml, last])
                nrow = min(128, ntok - t * 128)
                ad = P.op("vector", lambda e, t=t: e.tensor_tensor(out=x1t[:, :], in0=ps[:, 3:5, :].rearrange("p c n -> p (c n)"), in1=xt[:, t, :], op=ALU.add), deps=[mm, st1])
                st1 = P.dma("gpsimd", x1buf[row0 + t * 128:row0 + t * 128 + nrow, :], x1t[0:nrow, :], dsem_st, deps=[ad])
                last = ad
            prevg = [last, st1]
        P.reset_barrier()
        state["wstg_free"] = [None, None]
        pd = []
        Wu = wview(0, 8, 4096)
        Wd = wview(8 * 4096, 32, 1024)
        wt = load_w(Wu, w_up[l], deps=pd)
        wt += load_w(Wd, w_down[l], deps=pd)
        setup = wt
        xt = work(0, [128, 2, 1024])
        hT = work(2048, [128, 8, GW], BF16)
        aT = work(3072, [128, 32, GW], BF16)
        rl = [work(7168, [128, GW]), work(7424, [128, GW])]
        yt = work(7680, [128, 1024])
        prevg = None
        sto = None
        for gi, (s, row0, ntok, tok0) in enumerate(groups):
            ntile = (ntok + 127) // 128
            pg = prevg
            ld = []
            if s is None:
                mz = P.op("gpsimd", lambda e: e.memset(xt[:, 0, :], 0.0), deps=[pg] + pd)
                ld.append(P.dma("sync", xt[0:NMT, 0, :], x1buf[TOK:TALL, :], dsem_x[0], deps=[mz]))
            else:
                for t in range(ntile):
                    ld.append(P.dma("sync", xt[:, t, :], x1buf[row0 + t * 128:row0 + (t + 1) * 128, :], dsem_x[0], deps=[pg] + pd))
            fl = []
            for t in range(ntile):
                fl.append(rmsnorm_T(xt[:, t, :], gT2, hT, t * 128, [ld[-1], pg] + setup + fl, 8704))
            W_ = ntile * 128
            rq = []
            sq_all = []
            for j in range(32):
                bank = j % 3
                mm = None
                for k in range(8):
                    mm = P.op("tensor", lambda e, j=j, k=k, bank=bank: e.matmul(ps[:, bank, 0:W_], lhsT=Wu[:, k, j * 128:(j + 1) * 128], rhs=hT[:, k, 0:W_], start=(k == 0), stop=(k == 7)), deps=fl + [rq[j - 3] if j >= 3 else None, pg])
                r_ = P.op("scalar", lambda e, j=j, bank=bank: e.activation(out=rl[j % 2][:, 0:W_], in_=ps[:, bank, 0:W_], func=AF.Relu), deps=[mm, sq_all[j - 2] if j >= 2 else None, pg])
                rq.append(r_)
                sq_all.append(P.op("gpsimd" if j % 2 else "vector", lambda e, j=j: e.tensor_tensor(out=aT[:, j, 0:W_], in0=rl[j % 2][:, 0:W_], in1=rl[j % 2][:, 0:W_], op=ALU.mult), deps=[r_, pg]))
            last = None
            for t in range(ntile):
                mm = None
                for c in range(2):
                    for k in range(32):
                        mm = P.op("tensor", lambda e, t=t, c=c, k=k: e.matmul(ps[:, 3 + c, :], lhsT=aT[:, k, t * 128:(t + 1) * 128], rhs=Wd[:, k, c * 512:(c + 1) * 512], start=(k == 0), stop=(k == 31)), deps=[sq_all[-1], sq_all[-2], last])
                nrow = min(128, ntok - t * 128)
                ad = P.op("vector", lambda e, t=t: e.tensor_tensor(out=yt[:, :], in0=ps[:, 3:5, :].rearrange("p c n -> p (c n)"), in1=xt[:, t, :], op=ALU.add), deps=[mm, sto])
                sto = P.dma("gpsimd", dst_fn(gi, t, nrow), yt[0:nrow, :], dsem_st, deps=[ad])
                last = ad
            prevg = [last, sto]
        phase_done = [prevg, (dsem_st.h, dsem_st.count)]

    GW = 256
    l = 0
    phase_p1(l, x_in, meta_in)
    for s in range(2):
        for (g0, gn) in chunks(RQ[s], QG_[s]):
            phase_att(l, s, g0, gn, 0, False)
        phase_att(l, s, NSEQ[s], NMETA, 4, True)
    groups = []
    yoff = [0, RQ[0]]
    for s in range(2):
        for (o, w) in chunks(RQ[s], GW):
            groups.append((s, OFF[s] + o, w, o))
    groups.append((None, TOK, NMT, 0))

    def dst_fn(gi, t, nrow, groups=groups, yoff=yoff):
        s, row0, ntok, tok0 = groups[gi]
        if s is None:
            return y_out[TOKQ:TOKQ + NMT, :]
        r = yoff[s] + tok0 + t * 128
        return y_out[r:r + nrow, :]
    phase_p5(l, groups, x_in, meta_in, dst_fn)
    P.reset_barrier()
    fin = P.dma("sync", y_out[0:1, 0:128], ident_in[0:1, :], dsem_misc) if stop < 10 ** 9 else None
    P.wait_only("sync", [fin])

    with nc.allow_non_contiguous_dma(reason="small strided param/V loads"), nc.Block() as block:
        P.replay(block)
    es.close()
    return nc


def _local_order(c, N):
    r = c % 4
    Rq = N // 4
    order = [r] + [g for g in range(4) if g != r]
    return np.concatenate([np.arange(g * Rq, (g + 1) * Rq) for g in order]), order


def _tables(NP, NS, c):
    f32 = np.float32
    NSEQ = [NP, NS]
    TOK = NP + NS

    def rope_tab(pos, d):
        half = d // 2
        inv = (f32(THETA) ** (f32(-2.0) * np.arange(half, dtype=f32) / f32(d))).astype(f32)
        ang = (pos.astype(f32)[:, None] * inv[None, :]).astype(f32)
        return np.cos(ang).astype(f32), np.sin(ang).astype(f32)

    ropeA = np.zeros((TOK + 2 * NMETA, 128), f32)
    ropeC = np.zeros((TOK + 2 * NMETA, 128), f32)

    def fill(rows, row, col, lin):
        cr, sr = rope_tab(row, 32)
        cc, sc = rope_tab(col, 32)
        ropeA[rows, 0:64] = np.concatenate([cr, cr, cc, cc], axis=1)
        ropeA[rows, 64:128] = np.concatenate([-sr, sr, -sc, sc], axis=1)
        cl, sl = rope_tab(lin, 64)
        ropeC[rows, 0:64] = np.concatenate([cl, cl], axis=1)
        ropeC[rows, 64:128] = np.concatenate([-sl, sl], axis=1)
    off = 0
    m = np.arange(NMETA)
    slopes = [2.0 ** (-8.0 * (h + 1) / 4) for h in range(4)]
    qaug, kaug, kaugl = [], [], []
    for s in range(2):
        N = NSEQ[s]
        Rq = N // 4
        n, order = _local_order(c, N)
        fill(slice(off, off + N), (n // GRID_W).astype(f32), (n % GRID_W).astype(f32), (NMETA + n).astype(f32))
        off += N
        fill(slice(TOK + s * NMETA, TOK + (s + 1) * NMETA), np.full(NMETA, -1.0, f32), m.astype(f32), m.astype(f32))
        pos = NMETA + n
        lq = np.arange(N) // Rq
        gq = np.array(order)[lq]
        irel = np.arange(N) % Rq
        qa = np.zeros((4, NAUG, N + 2 * NMETA), np.float64)
        ka = np.zeros((4, NAUG, N + NMETA), np.float64)
        kl = np.zeros((4, 2, NAUG, Rq), np.float64)
        for h in range(4):
            c8 = 8.0 * slopes[h]
            hi, lo = pos // 128, pos % 128
            for b in range(5):
                r0 = 4 * b
                if b < 4:
                    sig = np.sign(gq - order[b]).astype(np.float64)
                else:
                    sig = np.ones(N)
                qa[h, r0 + 0, 0:N] = sig * (-c8 * 128.0) * hi
                qa[h, r0 + 1, 0:N] = sig * (-c8) * lo
                qa[h, r0 + 2, 0:N] = sig
                qa[h, r0 + 3, 0:N] = sig
                for rep in range(2):
                    sg = -1.0 if (b < 4 or rep == 1) else 1.0
                    cs = slice(N + rep * NMETA, N + (rep + 1) * NMETA)
                    qa[h, r0 + 0, cs] = 0.0
                    qa[h, r0 + 1, cs] = sg * (-c8) * m
                    qa[h, r0 + 2, cs] = sg
                    qa[h, r0 + 3, cs] = sg
                if b < 4:
                    sel = np.where(lq == b)[0]
                    ka[h, r0 + 0, sel] = 1.0
                    ka[h, r0 + 1, sel] = 1.0
                    ka[h, r0 + 2, sel] = c8 * 128.0 * hi[sel]
                    ka[h, r0 + 3, sel] = c8 * lo[sel]
                else:
                    ka[h, r0 + 0, N:N + NMETA] = 1.0
                    ka[h, r0 + 1, N:N + NMETA] = 1.0
                    ka[h, r0 + 2, N:N + NMETA] = 0.0
                    ka[h, r0 + 3, N:N + NMETA] = c8 * m
            qa[h, 20, 0:N] = (-c8 * 128.0) * (irel // 128)
            qa[h, 21, 0:N] = (-c8) * (irel % 128)
            qa[h, 22, 0:N] = 1.0
            qa[h, 23, 0:N] = 1.0
            j = np.arange(Rq)
            for v, sg in ((0, 1.0), (1, -1.0)):
                kl[h, v, 20, :] = sg
                kl[h, v, 21, :] = sg
                kl[h, v, 22, :] = sg * c8 * 128.0 * (j // 128)
                kl[h, v, 23, :] = sg * c8 * (j % 128)
        qaug.append(qa.reshape(4 * NAUG, -1).astype(f32))
        kaug.append(ka.reshape(4 * NAUG, -1).astype(f32))
        kaugl.append(kl.reshape(8 * NAUG, -1).astype(f32))
    return ropeA, ropeC, qaug, kaug, kaugl


_CACHE = {}


def run(inputs, NP, NS, **kw):
    key = (NP, NS, tuple(sorted(kw.items())))
    if key not in _CACHE:
        _CACHE[key] = build_program(NP, NS, **kw)
    nc = _CACHE[key]
    xcur = [np.asarray(inputs["x_prompt"], dtype=np.float32), np.asarray(inputs["x_sample"], dtype=np.float32)]
    meta = np.asarray(inputs["meta_tokens"], dtype=np.float32)
    mcur = [[meta, meta], [meta, meta]]
    PN = ["b_gate", "attn_norm_g", "mlp_norm_g", "a_q_norm_g", "a_k_norm_g", "b_q_norm_g", "b_k_norm_g",
          "b_lambda_q1", "b_lambda_k1", "b_lambda_q2", "b_lambda_k2", "b_subln_g", "c_q_a_norm_g",
          "c_kv_a_norm_g", "c_q_norm_g", "c_k_norm_g"]
    tabs = [_tables(NP, NS, c) for c in range(NCORE)]
    perms = [(_local_order(c, NP)[0], _local_order(c, NS)[0]) for c in range(NCORE)]
    RqP, RqS = NP // 4, NS // 4
    for l in range(DEPTH):
        shared = {}
        shared["params"] = np.ascontiguousarray(np.concatenate([np.asarray(inputs[k], dtype=np.float32)[l:l + 1] for k in PN], axis=1))
        for k in ["w_in", "w_up", "w_down", "w_out", "w_branch_a", "w_branch_b", "w_branch_c", "c_w_q_b", "c_w_kv_b"]:
            shared[k] = np.ascontiguousarray(np.asarray(inputs[k], dtype=np.float32)[l])
        lam_init = 0.8 - 0.6 * math.exp(-0.3 * l)
        shared["lamc"] = np.array([[-lam_init, 1.0 - lam_init]], np.float32)
        shared["ident_in"] = np.eye(128, dtype=np.float32)
        in_maps = []
        for c in range(NCORE):
            b = c // 4
            ropeA, ropeC, qaug, kaug, kaugl = tabs[c]
            d = dict(shared)
            d["x_in"] = np.ascontiguousarray(np.concatenate([xcur[0][b][perms[c][0]], xcur[1][b][perms[c][1]]], axis=0))
            d["meta_in"] = np.ascontiguousarray(np.concatenate([mcur[b][0], mcur[b][1]], axis=0))
            d["ropeA"], d["ropeC"] = ropeA, ropeC
            for s in range(2):
                d["qaug%d" % s] = qaug[s]
                d["kaug%d" % s] = kaug[s]
                d["kaugl%d" % s] = kaugl[s]
            in_maps.append(d)
        res = run_bass_kernel_spmd(nc, in_maps, core_ids=list(range(NCORE)))
        y_prompt = np.zeros((2, NP, D), np.float32)
        y_sample = np.zeros((2, NS, D), np.float32)
        mnew = [[None, None], [None, None]]
        for c in range(NCORE):
            b, r = c // 4, c % 4
            y = res.results[c]["y_out"]
            y_prompt[b, r * RqP:(r + 1) * RqP] = y[0:RqP]
            y_sample[b, r * RqS:(r + 1) * RqS] = y[RqP:RqP + RqS]
            if r == 0:
                mnew[b][0] = np.ascontiguousarray(y[RqP + RqS:RqP + RqS + NMETA])
                mnew[b][1] = np.ascontiguousarray(y[RqP + RqS + NMETA:RqP + RqS + 2 * NMETA])
        xcur = [y_prompt, y_sample]
        mcur = mnew
    return xcur[0], xcur[1]


def kernel(**inputs):
    return run(inputs, inputs["x_prompt"].shape[1], inputs["x_sample"].shape[1])
```

```python
import math
from contextlib import ExitStack
import numpy as np
import concourse.bass as bass
import concourse.mybir as mybir
from concourse.bass_utils import run_bass_kernel_spmd

F32 = mybir.dt.float32
BF16 = mybir.dt.bfloat16
AF = mybir.ActivationFunctionType
ALU = mybir.AluOpType
AX = mybir.AxisListType

NCORE = 8
D = 1024
NMETA = 16
DEPTH = 2
EPS = 1e-6
GRID_W = 64
THETA = 10000.0
NQKV = 2752
NAUG = 24
ENGS = ["tensor", "vector", "scalar", "gpsimd", "sync"]


class _Rec:
    def __getattr__(self, name):
        def f(*a, **kw):
            return (name, a, kw)
        return f


_REC = _Rec()


class DSem:
    def __init__(self, h):
        self.h = h
        self.count = 0


class Prog:
    def __init__(self, nc, es):
        self.nc = nc
        self.es = es
        self.q = {e: [] for e in ENGS}
        self.tl = {e: es.enter_context(nc.semaphore("tl_" + e)) for e in ENGS}
        self.seq = {e: 0 for e in ENGS}
        self.waited = {}
        self.nsem = 0
        self.dsems = []
        self.ep_arrive = es.enter_context(nc.semaphore("ep_arrive"))
        self.ep_go = es.enter_context(nc.semaphore("ep_go"))
        self.nbar = 0

    def dsem(self):
        self.nsem += 1
        d = DSem(self.es.enter_context(self.nc.semaphore("ds%d" % self.nsem)))
        self.dsems.append(d)
        return d

    def reset_barrier(self):
        toks = [(self.tl[e], self.seq[e]) for e in ENGS if self.seq[e] > 0]
        toks += [(d.h, d.count) for d in self.dsems if d.count > 0]
        for e in ENGS:
            self.wait_only(e, toks)
        self.nbar += 1
        for e in ("tensor", "vector", "scalar"):
            self.tl[e] = self.es.enter_context(self.nc.semaphore("tl_%s_%d" % (e, self.nbar)))
            self.seq[e] = 0

    def _waits(self, eng, deps):
        waits = []
        for d in deps:
            if d is None:
                continue
            if isinstance(d, list):
                waits += self._waits(eng, d)
                continue
            sem, val = d
            key = (eng, id(sem))
            if self.waited.get(key, 0) >= val:
                continue
            self.waited[key] = val
            waits.append((sem, val))
        return waits

    def op(self, eng, fn, deps=()):
        waits = self._waits(eng, deps)
        self.seq[eng] += 1
        self.q[eng].append((fn(_REC), waits, self.tl[eng], 1))
        return (self.tl[eng], self.seq[eng])

    def dma(self, eng, out, in_, sem, deps=()):
        waits = self._waits(eng, deps)
        sem.count += 16
        self.q[eng].append((("dma_start", (), dict(out=out, in_=in_)), waits, sem.h, 16))
        return (sem.h, sem.count)

    def wait_only(self, eng, deps):
        waits = self._waits(eng, deps)
        if waits:
            self.q[eng].append((None, waits, None, 0))

    def replay(self, block):
        for eng in ENGS:
            items = self.q[eng]

            def body(e, items=items):
                for fn, waits, sem, inc in items:
                    for (sm, v) in waits:
                        e.wait_ge(sm, v)
                    if fn is not None:
                        name, a, kw = fn
                        ins = getattr(e, name)(*a, **kw)
                        if sem is not None:
                            ins.then_inc(sem, inc)
            getattr(block, eng)(body)


def chunks(total, size):
    return [(o, min(size, total - o)) for o in range(0, total, size)]


def build_program(NP, NS, KBLK=2048, QG=2048, QC=512, debug=False, stop=10 ** 9):
    NSEQ = [NP, NS]
    RQ = [NP // 4, NS // 4]
    KB_ = [min(KBLK, RQ[s]) for s in range(2)]
    QG_ = [min(QG, RQ[s]) for s in range(2)]
    TOK = NP + NS
    OFF = [0, NP]
    NMT = 2 * NMETA
    TALL = TOK + NMT
    TOKQ = RQ[0] + RQ[1]
    nc = bass.Bass("TRN2", target_bir_lowering=False)
    es = ExitStack()

    def din(name, shape, dt=F32):
        return nc.dram_tensor(name, list(shape), dt, kind="ExternalInput").ap()

    def dint(name, shape, dt):
        return nc.dram_tensor(name, list(shape), dt, kind=("ExternalOutput" if debug else "Internal")).ap()

    x_in = din("x_in", [TOK, D])
    meta_in = din("meta_in", [NMT, D])
    PNAMES = [("b_gate", 3072), ("attn_norm_g", 1024), ("mlp_norm_g", 1024), ("a_q_norm_g", 64), ("a_k_norm_g", 64),
              ("b_q_norm_g", 64), ("b_k_norm_g", 64), ("b_lambda_q1", 64), ("b_lambda_k1", 64), ("b_lambda_q2", 64),
              ("b_lambda_k2", 64), ("b_subln_g", 128), ("c_q_a_norm_g", 256), ("c_kv_a_norm_g", 128),
              ("c_q_norm_g", 192), ("c_k_norm_g", 192)]
    PTOT = sum(n for _, n in PNAMES)
    LD = 1
    params = din("params", [LD, PTOT])
    lamc = din("lamc", [1, 2])
    pv = {}
    o_ = 0
    for nm_, n_ in PNAMES:
        pv[nm_] = params[:, o_:o_ + n_]
        o_ += n_
    b_gate, attn_g, mlp_g = pv["b_gate"], pv["attn_norm_g"], pv["mlp_norm_g"]
    g64 = pv
    b_sub, c_qa_g, c_kva_g, c_q_g, c_k_g = pv["b_subln_g"], pv["c_q_a_norm_g"], pv["c_kv_a_norm_g"], pv["c_q_norm_g"], pv["c_k_norm_g"]
    w_in_d = din("w_in", [LD * D, 5824])
    w_up_d = din("w_up", [LD * D, 4096])
    w_down_d = din("w_down", [LD * 4096, D])
    w_out_d = din("w_out", [LD * D, D])
    w_br_d = [din("w_branch_" + n, [LD * 512, D]) for n in "abc"]
    c_wqb_d = din("c_w_q_b", [LD * 256, 768])
    c_wkvb_d = din("c_w_kv_b", [LD * 128, 1024])
    w_in = [w_in_d[l * D:(l + 1) * D, :] for l in range(LD)]
    w_up = [w_up_d[l * D:(l + 1) * D, :] for l in range(LD)]
    w_down = [w_down_d[l * 4096:(l + 1) * 4096, :] for l in range(LD)]
    w_out = [w_out_d[l * D:(l + 1) * D, :] for l in range(LD)]
    w_br = [[w_br_d[i][l * 512:(l + 1) * 512, :] for l in range(LD)] for i in range(3)]
    c_wqb = [c_wqb_d[l * 256:(l + 1) * 256, :] for l in range(LD)]
    c_wkvb = [c_wkvb_d[l * 128:(l + 1) * 128, :] for l in range(LD)]
    ident_in = din("ident_in", [128, 128])
    ropeA = din("ropeA", [TALL, 128])
    ropeC = din("ropeC", [TALL, 128])
    qaug_f = [din("qaug%d" % s, [4 * NAUG, NSEQ[s] + 2 * NMETA]) for s in range(2)]
    kaug_f = [din("kaug%d" % s, [4 * NAUG, NSEQ[s] + NMETA]) for s in range(2)]
    kaugl_f = [din("kaugl%d" % s, [8 * NAUG, RQ[s]]) for s in range(2)]
    qaug_b = [dint("qaugb%d" % s, [4 * NAUG, NSEQ[s] + 2 * NMETA], BF16) for s in range(2)]
    kaug_b = [dint("kaugb%d" % s, [4 * NAUG, NSEQ[s] + NMETA], BF16) for s in range(2)]
    kaugl_b = [dint("kauglb%d" % s, [8 * NAUG, RQ[s]], BF16) for s in range(2)]
    qaug = [qaug_b[s].rearrange("(h r) t -> h r t", r=NAUG) for s in range(2)]
    kaug = [kaug_b[s].rearrange("(h r) t -> h r t", r=NAUG) for s in range(2)]
    kaugl = [kaugl_b[s].rearrange("(h g r) t -> h g r t", g=2, r=NAUG) for s in range(2)]
    y_out = nc.dram_tensor("y_out", [TOKQ + NMT, D], F32, kind="ExternalOutput").ap()

    xmid = dint("xmid", [TALL, D], F32)
    x1buf = dint("x1buf", [TALL, D], F32)
    kvl = [dint("kvl%d" % s, [2560, NSEQ[s]], BF16) for s in range(2)]
    kvm = [dint("kvm%d" % s, [2560, NMETA], BF16) for s in range(2)]
    qloc = [dint("qloc%d" % s, [14 * 128, NSEQ[s] + NMETA], BF16) for s in range(2)]
    oA = [dint("oA%d" % s, [512, NSEQ[s] + NMETA], BF16) for s in range(2)]
    oC = [dint("oC%d" % s, [512, NSEQ[s] + NMETA], BF16) for s in range(2)]
    oB = [dint("oB%d" % s, [1024, NSEQ[s] + NMETA], F32) for s in range(2)]

    def sb(name, shape, dt=F32):
        return es.enter_context(nc.sbuf_tensor(name, list(shape), dt))

    P = Prog(nc, es)
    RMAX = max(KB_ + QG_)
    TQM = RMAX + 2 * NMETA
    ident = sb("ident", [128, 128], BF16)
    identf = sb("identf", [128, 128])
    ones_bf = sb("ones_bf", [128, 1], BF16)
    ones_f = sb("ones_f", [128, 128])
    mean_f = sb("mean_f", [128, 128])
    WBIG = sb("wbig", [128, 64 * 1024], BF16)
    wstg = [sb("wstg%d" % i, [128, 1024]) for i in range(2)]
    WORK = sb("work", [128, 14336], F32)
    gT = sb("gT", [128, 8])
    gT2 = sb("gT2", [128, 8])
    bgT = sb("bgT", [128, 24])
    gsub = sb("gsub", [128, 1])
    lamt = sb("lamt", [128, 8])
    lrow = sb("lrow", [1, 4 * 64 + 8])
    gains = WBIG[:, 57344:57344 + 7168].bitcast(F32)
    pst = es.enter_context(nc.psum_tensor("pst", [128, 1024], BF16))
    ps = es.enter_context(nc.psum_tensor("ps", [128, 7, 512], F32))
    pstf = pst[:, :].bitcast(F32)

    dsem_w = [P.dsem(), P.dsem()]
    dsem_x = [P.dsem(), P.dsem()]
    dsem_misc = P.dsem()
    dsem_st = P.dsem()
    dsem_kv = [P.dsem() for _ in range(3)]
    dsem_q = [P.dsem(), P.dsem()]
    dsem_o = P.dsem()
    dsem_tab = [P.dsem(), P.dsem()]
    state = {"wstg_free": [None, None], "wn": 0}

    def work(off, shape, dt=F32):
        n = int(np.prod(shape[1:]))
        if dt == F32:
            v = WORK[:, off:off + n]
        else:
            v = WORK[:, off:off + (n + 1) // 2].bitcast(BF16)[:, 0:n]
        if len(shape) == 3:
            v = v.rearrange("p (a b) -> p a b", b=shape[2])
        elif len(shape) == 4:
            v = v.rearrange("p (a b c) -> p a b c", b=shape[2], c=shape[3])
        return v[0:shape[0]]

    def wview(off, KC, C):
        return WBIG[:, off:off + KC * C].rearrange("p (k c) -> p k c", c=C)

    def load_w(dst, src, deps=()):
        KC, C = dst.shape[1], dst.shape[2]
        toks = {}
        for kc in range(KC):
            for (c0, w) in chunks(C, 1024):
                i = state["wn"] % 2
                state["wn"] += 1
                t = P.dma("sync", wstg[i][:, 0:w], src[kc * 128:(kc + 1) * 128, c0:c0 + w], dsem_w[i],
                          deps=[state["wstg_free"][i]])
                eng = "gpsimd" if (state["wn"] % 2) else "vector"
                ct = P.op(eng, lambda e, o=dst[:, kc, c0:c0 + w], s=wstg[i][:, 0:w]: e.tensor_copy(out=o, in_=s),
                          deps=[t] + list(deps))
                state["wstg_free"][i] = ct
                toks[eng] = ct
        return list(toks.values())

    def bcast_row(src_row, n):
        return bass.AP(src_row.tensor, src_row.offset, [[0, 128], [1, n]])

    t0 = P.dma("sync", identf[:], ident_in, dsem_misc)
    t_id = P.op("vector", lambda e: e.tensor_copy(out=ident[:], in_=identf[:]), deps=[t0])
    P.op("vector", lambda e: e.memset(ones_bf[:], 1.0))
    P.op("vector", lambda e: e.memset(ones_f[:], 1.0))
    t_const = P.op("vector", lambda e: e.memset(mean_f[:], 1.0 / 128.0))
    cstg = [work(0, [128, 1024], BF16), work(512, [128, 1024], BF16)]
    cfree = [None, None]
    cn = 0
    import os
    tabs = list(zip(qaug_f, qaug_b)) + list(zip(kaug_f, kaug_b)) + list(zip(kaugl_f, kaugl_b))
    if os.environ.get('K_SKIP_TAB'):
        tabs = tabs[:int(os.environ['K_SKIP_TAB']) - 1]
    for (srcT, dstT) in tabs:
        nr, ncol = srcT.shape
        for (r0, rh) in chunks(nr, 128):
            for (c0, w) in chunks(ncol, 1024):
                i = cn % 2
                cn += 1
                t = P.dma("sync", wstg[i][0:rh, 0:w], srcT[r0:r0 + rh, c0:c0 + w], dsem_w[i], deps=[state["wstg_free"][i]])
                ct = P.op("vector", lambda e, i=i, rh=rh, w=w: e.tensor_copy(out=cstg[i][0:rh, 0:w], in_=wstg[i][0:rh, 0:w]), deps=[t, cfree[i]])
                state["wstg_free"][i] = ct
                cfree[i] = P.dma("gpsimd", dstT[r0:r0 + rh, c0:c0 + w], cstg[i][0:rh, 0:w], dsem_tab[i], deps=[ct])
    phase_done = [t_const, t_id, cfree[0], cfree[1]]
    pcount = {"n": 0}

    def skip_phase():
        pcount["n"] += 1
        return pcount["n"] > stop

    def barrier(tokens):
        for e in ENGS:
            P.wait_only(e, tokens)

    def rmsnorm_T(xt, g_t, hT, tcol, ntile_deps, scratch_off):
        junk = work(scratch_off, [128, 1024])
        xn = work(scratch_off + 1024, [128, 1024], BF16)
        st = work(scratch_off + 1536, [128, 4])
        a = P.op("scalar", lambda e: e.activation(out=junk, in_=xt, func=AF.Square, accum_out=st[:, 0:1]),
                 deps=ntile_deps)
        b = P.op("scalar", lambda e: e.activation(out=st[:, 1:2], in_=st[:, 0:1], func=AF.Sqrt,
                                                  scale=1.0 / D, bias=EPS), deps=[a])
        c = P.op("vector", lambda e: e.reciprocal(out=st[:, 2:3], in_=st[:, 1:2]), deps=[b] + list(ntile_deps))
        d = P.op("vector", lambda e: e.tensor_scalar(out=xn, in0=xt, scalar1=st[:, 2:3], scalar2=None,
                                                     op0=ALU.mult), deps=[c])
        last = None
        for k in range(8):
            last = P.op("tensor", lambda e, k=k: e.transpose(out=pst[:, k * 128:(k + 1) * 128],
                                                             in_=xn[:, k * 128:(k + 1) * 128], identity=ident[:]),
                        deps=[d] + list(ntile_deps))
        f = P.op("vector", lambda e: e.tensor_tensor(
            out=hT[:, :, tcol:tcol + 128], in0=pst[:, :].rearrange("p (k t) -> p k t", t=128),
            in1=g_t[:, :].unsqueeze(2).to_broadcast([128, 8, 128]), op=ALU.mult), deps=[last])
        return f

    def phase_p1(l, xsrc_reg, xsrc_meta):
        nonlocal phase_done
        if skip_phase():
            return
        P.reset_barrier()
        state["wstg_free"] = [None, None]
        pd = []
        Wq = wview(0, 8, NQKV)
        Wqb = wview(8 * NQKV, 2, 768)
        Wkvb = wview(8 * NQKV + 2 * 768, 1, 1024)
        wt = load_w(Wq, w_in[l][:, 0:NQKV], deps=pd)
        wt += load_w(Wqb, c_wqb[l], deps=pd)
        wt += load_w(Wkvb, c_wkvb[l], deps=pd)
        gt = []
        gt.append(P.dma("sync", gT[:], attn_g[l].rearrange("(k p) -> p k", p=128), dsem_misc, deps=pd))
        col = 0
        for nm, rep in [("a_q_norm_g", 8), ("a_k_norm_g", 2), ("b_q_norm_g", 8), ("b_k_norm_g", 8)]:
            for r in range(rep):
                gt.append(P.dma("sync", gains[:, col:col + 64], bcast_row(g64[nm][l:l + 1, :], 64), dsem_misc, deps=pd))
                col += 64
        gt.append(P.dma("sync", gains[:, col:col + 256], bcast_row(c_qa_g[l:l + 1, :], 256), dsem_misc, deps=pd)); col += 256
        gt.append(P.dma("sync", gains[:, col:col + 128], bcast_row(c_kva_g[l:l + 1, :], 128), dsem_misc, deps=pd)); col += 128
        for r in range(4):
            gt.append(P.dma("sync", gains[:, col:col + 192], bcast_row(c_q_g[l:l + 1, :], 192), dsem_misc, deps=pd)); col += 192
        for r in range(4):
            gt.append(P.dma("sync", gains[:, col:col + 192], bcast_row(c_k_g[l:l + 1, :], 192), dsem_misc, deps=pd)); col += 192
        setup = wt + [gt[-1]]
        G1 = gains[:, 0:640]
        G2 = gains[:, 640:1664]
        GCQA = gains[:, 1664:1920]
        GCKVA = gains[:, 1920:2048]

        xt = [work(0, [128, 1024]), work(1024, [128, 1024])]
        hT = work(2048, [128, 8, 128], BF16)
        z = work(2560, [128, NQKV])
        tmp = work(5312, [128, NQKV])
        zb = work(8064, [128, NQKV], BF16)
        ssq = work(9440, [128, 40])
        rst = work(9480, [128, 40])
        rope = work(9520, [128, 256])
        cqn = work(9776, [128, 384], BF16)
        cT = work(9968, [128, 3, 128], BF16)
        zcb = work(11696, [128, 2, 4, 128], BF16)
        zrb = work(12208, [128, 2, 256], BF16)
        vst = work(12464, [128, 1152], BF16)
        kst = WBIG[:, 40 * 1024:40 * 1024 + 11 * 512].rearrange("p (b t) -> p b t", t=512)
        qst = WBIG[:, 40 * 1024 + 11 * 512:40 * 1024 + 25 * 512].rearrange("p (b t) -> p b t", t=512)

        tile_list = []
        for s in range(2):
            nt = NSEQ[s] // 128
            gsz = min(4, nt)
            for t in range(nt):
                tile_list.append((s, OFF[s] + t * 128, 128, t % gsz, (t % gsz) == gsz - 1, t * 128, gsz))
        tile_list.append((None, 0, NMT, 0, True, 0, 1))

        prev_tile_done = {"tok": None}
        xfree = [None, None]
        st_prev = {"k": None, "q": None, "v": None}
        for ti, (s, row0, ntok, gcol, glast, tok0, gsz) in enumerate(tile_list):
            i = ti % 2
            X = xt[i]
            ptd = prev_tile_done["tok"]
            ld = []
            if s is None:
                mz = P.op("gpsimd", lambda e, X=X: e.memset(X, 0.0), deps=[xfree[i]] + pd)
                ld.append(P.dma("sync", X[0:NMT, :], xsrc_meta, dsem_x[i], deps=[mz]))
                ld.append(P.dma("sync", rope[0:NMT, 0:128], ropeA[TOK:TALL, :], dsem_x[i], deps=[ptd] + pd))
                ld.append(P.dma("sync", rope[0:NMT, 128:256], ropeC[TOK:TALL, :], dsem_x[i], deps=[ptd] + pd))
            else:
                ld.append(P.dma("sync", X, xsrc_reg[row0:row0 + 128, :], dsem_x[i], deps=[xfree[i]] + pd))
                ld.append(P.dma("sync", rope[:, 0:128], ropeA[row0:row0 + 128, :], dsem_x[i], deps=[ptd] + pd))
                ld.append(P.dma("sync", rope[:, 128:256], ropeC[row0:row0 + 128, :], dsem_x[i], deps=[ptd] + pd))
            ldt = ld[-1]
            f = rmsnorm_T(X, gT, hT, 0, [ldt, ptd] + setup, 5312)
            xfree[i] = f
            colch = [(0, 512), (512, 256), (768, 512), (1280, 512), (1792, 512), (2304, 448)]
            ev = []
            for bi, (c0, w) in enumerate(colch):
                mm = None
                for k in range(8):
                    mm = P.op("tensor", lambda e, bi=bi, c0=c0, w=w, k=k: e.matmul(
                        ps[:, bi, 0:w], lhsT=hT[:, k, :], rhs=Wq[:, k, c0:c0 + w], start=(k == 0), stop=(k == 7)),
                        deps=[f, ptd])
                ev.append(P.op("scalar", lambda e, bi=bi, c0=c0, w=w: e.activation(
                    out=z[:, c0:c0 + w], in_=ps[:, bi, 0:w], func=AF.Copy), deps=[mm, ptd]))
            evl = ev[-1]
            a1 = P.op("gpsimd", lambda e: e.tensor_tensor(out=tmp[:, :], in0=z[:, :], in1=z[:, :], op=ALU.mult), deps=[evl, f])
            r1 = P.op("vector", lambda e: e.tensor_reduce(out=ssq[:, 0:10], in_=tmp[:, 0:640].rearrange("p (h d) -> p h d", d=64), axis=AX.X, op=ALU.add), deps=[a1, ptd])
            r2 = P.op("vector", lambda e: e.tensor_reduce(out=ssq[:, 10:26], in_=tmp[:, 768:1792].rearrange("p (h d) -> p h d", d=64), axis=AX.X, op=ALU.add), deps=[a1])
            r3 = P.op("vector", lambda e: e.tensor_reduce(out=ssq[:, 26:27], in_=tmp[:, 2304:2560], axis=AX.X, op=ALU.add), deps=[a1])
            r4 = P.op("vector", lambda e: e.tensor_reduce(out=ssq[:, 27:28], in_=tmp[:, 2560:2688], axis=AX.X, op=ALU.add), deps=[a1])
            s1 = P.op("scalar", lambda e: e.activation(out=rst[:, 0:26], in_=ssq[:, 0:26], func=AF.Sqrt, scale=1.0 / 64, bias=EPS), deps=[r1, r2, ptd])
            s2 = P.op("scalar", lambda e: e.activation(out=rst[:, 26:27], in_=ssq[:, 26:27], func=AF.Sqrt, scale=1.0 / 256, bias=EPS), deps=[r3])
            s3 = P.op("scalar", lambda e: e.activation(out=rst[:, 27:28], in_=ssq[:, 27:28], func=AF.Sqrt, scale=1.0 / 128, bias=EPS), deps=[r4])
            rc = P.op("vector", lambda e: e.reciprocal(out=rst[:, 0:28], in_=rst[:, 0:28]), deps=[s1, s2, s3])
            n1 = P.op("vector", lambda e: e.tensor_tensor(out=z[:, 0:640].rearrange("p (h d) -> p h d", d=64), in0=z[:, 0:640].rearrange("p (h d) -> p h d", d=64), in1=rst[:, 0:10].unsqueeze(2).to_broadcast([128, 10, 64]), op=ALU.mult), deps=[rc, a1])
            n2 = P.op("vector", lambda e: e.tensor_tensor(out=z[:, 768:1792].rearrange("p (h d) -> p h d", d=64), in0=z[:, 768:1792].rearrange("p (h d) -> p h d", d=64), in1=rst[:, 10:26].unsqueeze(2).to_broadcast([128, 16, 64]), op=ALU.mult), deps=[rc, a1])
            n3 = P.op("vector", lambda e: e.tensor_scalar(out=z[:, 2304:2560], in0=z[:, 2304:2560], scalar1=rst[:, 26:27], scalar2=None, op0=ALU.mult), deps=[rc, a1])
            n4 = P.op("vector", lambda e: e.tensor_scalar(out=z[:, 2560:2688], in0=z[:, 2560:2688], scalar1=rst[:, 27:28], scalar2=None, op0=ALU.mult), deps=[rc, a1])
            g1 = P.op("gpsimd", lambda e: e.tensor_tensor(out=z[:, 0:640], in0=z[:, 0:640], in1=G1, op=ALU.mult), deps=[n1])
            g2 = P.op("gpsimd", lambda e: e.tensor_tensor(out=zb[:, 768:1792], in0=z[:, 768:1792], in1=G2, op=ALU.mult), deps=[n2, ptd])
            g3 = P.op("gpsimd", lambda e: e.tensor_tensor(out=cqn[:, 0:256], in0=z[:, 2304:2560], in1=GCQA, op=ALU.mult), deps=[n3, ptd])
            g4 = P.op("gpsimd", lambda e: e.tensor_tensor(out=cqn[:, 256:384], in0=z[:, 2560:2688], in1=GCKVA, op=ALU.mult), deps=[n4, ptd])
            zv = z[:, 0:640].rearrange("p (h r a d) -> p h r a d", r=2, a=2, d=16)
            tv = tmp[:, 0:640].rearrange("p (h r a d) -> p h r a d", r=2, a=2, d=16)
            sinv = rope[:, 64:128].rearrange("p (r a d) -> p r a d", r=2, a=2)
            ra = P.op("vector", lambda e: e.tensor_tensor(out=tmp[:, 768:1408].rearrange("p (h d) -> p h d", d=64), in0=z[:, 0:640].rearrange("p (h d) -> p h d", d=64), in1=rope[:, 0:64].unsqueeze(1).to_broadcast([128, 10, 64]), op=ALU.mult), deps=[g1, ldt, r2, r3, r4])
            rb = []
            for hh in range(2):
                rb.append(P.op("vector", lambda e, hh=hh: e.tensor_tensor(
                    out=tv[:, :, :, hh, :], in0=zv[:, :, :, 1 - hh, :],
                    in1=sinv[:, :, hh, :].unsqueeze(1).to_broadcast([128, 10, 2, 16]),
                    op=ALU.mult), deps=[g1, ldt, r1]))
            rd = P.op("vector", lambda e: e.tensor_tensor(out=zb[:, 0:640], in0=tmp[:, 768:1408], in1=tmp[:, 0:640], op=ALU.add), deps=[ra] + rb + [ptd])
            v1 = P.op("scalar", lambda e: e.activation(out=vst[:, 0:128], in_=z[:, 640:768], func=AF.Copy), deps=[evl, st_prev["v"]])
            v2 = P.op("scalar", lambda e: e.activation(out=vst[:, 128:640], in_=z[:, 1792:2304], func=AF.Copy), deps=[evl])
            tr = None
            for k in range(3):
                tr = P.op("tensor", lambda e, k=k: e.transpose(out=pst[:, k * 128:(k + 1) * 128], in_=cqn[:, k * 128:(k + 1) * 128], identity=ident[:]), deps=[g3, g4, f])
            ctc = P.op("vector", lambda e: e.tensor_copy(out=cT[:, :, :], in_=pst[:, 0:384].rearrange("p (k t) -> p k t", t=128)), deps=[tr, ptd])
            mm = None
            for j in range(2):
                for k in range(2):
                    mm = P.op("tensor", lambda e, j=j, k=k: e.matmul(ps[:, j, 0:384], lhsT=cT[:, k, :], rhs=Wqb[:, k, j * 384:(j + 1) * 384], start=(k == 0), stop=(k == 1)), deps=[ctc, evl])
            for j in range(2):
                mm = P.op("tensor", lambda e, j=j: e.matmul(ps[:, 2 + j, :], lhsT=cT[:, 2, :], rhs=Wkvb[:, 0, j * 512:(j + 1) * 512], start=True, stop=True), deps=[ctc, evl])
            zqk = WORK[:, 10160:10160 + 1536]
            zq = zqk[:, 0:768]
            zk = zqk[:, 768:1536]
            e1 = P.op("scalar", lambda e: e.activation(out=zq.rearrange("p (j c) -> p j c", c=384), in_=ps[:, 0:2, 0:384], func=AF.Copy), deps=[mm, ptd])
            kvv = ps[:, 2:4, :].rearrange("p j (h c) -> p (j h) c", c=256)
            e2 = P.op("scalar", lambda e: e.activation(out=zk.rearrange("p (h c) -> p h c", c=192)[:, :, 0:128], in_=kvv[:, :, 0:128], func=AF.Copy), deps=[mm, ptd])
            e3 = P.op("scalar", lambda e: e.activation(out=vst[:, 640:1152].rearrange("p (h c) -> p h c", c=128), in_=kvv[:, :, 128:256], func=AF.Copy), deps=[mm])
            e4 = P.op("gpsimd", lambda e: e.tensor_copy(out=zk.rearrange("p (h c) -> p h c", c=192)[:, :, 128:192], in_=z[:, 2688:2752].unsqueeze(1).to_broadcast([128, 4, 64])), deps=[evl, ptd, g4])
            a2 = P.op("gpsimd", lambda e: e.tensor_tensor(out=tmp[:, 0:1536], in0=zqk, in1=zqk, op=ALU.mult), deps=[e1, e2, e4, rd])
            r5 = P.op("vector", lambda e: e.tensor_reduce(out=ssq[:, 28:36], in_=tmp[:, 0:1536].rearrange("p (h d) -> p h d", d=192), axis=AX.X, op=ALU.add), deps=[a2])
            s5 = P.op("scalar", lambda e: e.activation(out=rst[:, 28:36], in_=ssq[:, 28:36], func=AF.Sqrt, scale=1.0 / 192, bias=EPS), deps=[r5])
            rc5 = P.op("vector", lambda e: e.reciprocal(out=rst[:, 28:36], in_=rst[:, 28:36]), deps=[s5])
            n5 = P.op("vector", lambda e: e.tensor_tensor(out=zqk.rearrange("p (h d) -> p h d", d=192), in0=zqk.rearrange("p (h d) -> p h d", d=192), in1=rst[:, 28:36].unsqueeze(2).to_broadcast([128, 8, 192]), op=ALU.mult), deps=[rc5, a2])
            g5 = P.op("gpsimd", lambda e: e.tensor_tensor(out=zqk, in0=zqk, in1=gains[:, 2048:3584], op=ALU.mult), deps=[n5])
            zqk4 = zqk.rearrange("p (h d) -> p h d", d=192)
            c1 = P.op("scalar", lambda e: e.activation(out=zcb[:, :, :, :].rearrange("p a h d -> p (a h) d"), in_=zqk4[:, :, 0:128], func=AF.Copy), deps=[g5, ptd])
            rp = zqk4[:, :, 128:192].rearrange("p h (a d) -> p h a d", a=2)
            t64 = tmp[:, 0:512].rearrange("p (h d) -> p h d", d=64)
            u64 = tmp[:, 512:1024].rearrange("p (h a d) -> p h a d", a=2, d=32)
            sinc = rope[:, 192:256].rearrange("p (a d) -> p a d", a=2)
            qa_ = P.op("vector", lambda e: e.tensor_tensor(out=t64, in0=zqk4[:, :, 128:192], in1=rope[:, 128:192].unsqueeze(1).to_broadcast([128, 8, 64]), op=ALU.mult), deps=[g5, ldt, r5])
            qb_ = []
            for hh in range(2):
                qb_.append(P.op("vector", lambda e, hh=hh: e.tensor_tensor(out=u64[:, :, hh, :], in0=rp[:, :, 1 - hh, :], in1=sinc[:, hh, :].unsqueeze(1).to_broadcast([128, 8, 32]), op=ALU.mult), deps=[g5, ldt, r5]))
            qd_ = P.op("vector", lambda e: e.tensor_tensor(out=zrb[:, :, :].rearrange("p a (h d) -> p (a h) d", d=64), in0=t64, in1=tmp[:, 512:1024].rearrange("p (h d) -> p h d", d=64), op=ALU.add), deps=[qa_] + qb_ + [ptd])
            srcs = [("k", 0, zb[:, 512:640])]
            for b in range(4):
                srcs.append(("k", 1 + b, zb[:, 1280 + b * 128:1280 + (b + 1) * 128]))
            for b in range(4):
                srcs.append(("k", 5 + b, zcb[:, 1, b, :]))
            for b in range(2):
                srcs.append(("k", 9 + b, zrb[:, 1, b * 128:(b + 1) * 128]))
            for b in range(4):
                srcs.append(("q", b, zb[:, b * 128:(b + 1) * 128]))
            for b in range(4):
                srcs.append(("q", 4 + b, zb[:, 768 + b * 128:768 + (b + 1) * 128]))
            for b in range(4):
                srcs.append(("q", 8 + b, zcb[:, 0, b, :]))
            for b in range(2):
                srcs.append(("q", 12 + b, zrb[:, 0, b * 128:(b + 1) * 128]))
            alld = [rd, g2, c1, qd_]
            cp_last = ctc
            cps = []
            for r0 in range(0, len(srcs), 8):
                grp = srcs[r0:r0 + 8]
                tr = None
                for j, (dst, blk, src) in enumerate(grp):
                    tr = P.op("tensor", lambda e, j=j, src=src: e.transpose(out=pst[:, j * 128:(j + 1) * 128], in_=src, identity=ident[:]), deps=alld + [cp_last])
                j = 0
                while j < len(grp):
                    j2 = j
                    while j2 + 1 < len(grp) and grp[j2 + 1][0] == grp[j][0] and grp[j2 + 1][1] == grp[j2][1] + 1:
                        j2 += 1
                    dstt = kst if grp[j][0] == "k" else qst
                    b0 = grp[j][1]
                    nb = j2 - j + 1
                    cp_last = P.op("vector", lambda e, dstt=dstt, b0=b0, nb=nb, j=j: e.tensor_copy(
                        out=dstt[:, b0:b0 + nb, gcol * 128:(gcol + 1) * 128],
                        in_=pst[:, j * 128:(j + nb) * 128].rearrange("p (b t) -> p b t", t=128)),
                        deps=[tr, st_prev["k"], st_prev["q"]])
                    cps.append(cp_last)
                    j = j2 + 1
            prev_tile_done["tok"] = [cp_last, qd_, rd, g2, c1, v1, v2, e3]
            if s is None:
                for s4 in range(2):
                    sk = P.dma("gpsimd", kvm[s4][0:1408, :].rearrange("(b p) t -> p b t", p=128), kst[:, :, s4 * 16:(s4 + 1) * 16], dsem_st, deps=cps)
                    sq_ = P.dma("gpsimd", qloc[s4][:, NSEQ[s4]:NSEQ[s4] + NMETA].rearrange("(b p) t -> p b t", p=128), qst[:, :, s4 * 16:(s4 + 1) * 16], dsem_st, deps=cps)
                    sv = P.dma("gpsimd", bass.AP(kvm[s4].tensor, 1408 * NMETA, [[1152, NMETA], [1, 1152]]), vst[s4 * 16:(s4 + 1) * 16, :], dsem_st, deps=[v1, v2, e3])
                st_prev["k"], st_prev["q"], st_prev["v"] = sk, sq_, sv
            else:
                sv = P.dma("gpsimd", bass.AP(kvl[s].tensor, 1408 * NSEQ[s] + tok0 * 1152, [[1152, 128], [1, 1152]]), vst[:, :], dsem_st, deps=[v1, v2, e3])
                st_prev["v"] = sv
                if glast:
                    gw = gsz * 128
                    g0 = tok0 - (gsz - 1) * 128
                    sk = P.dma("gpsimd", kvl[s][0:1408, g0:g0 + gw].rearrange("(b p) t -> p b t", p=128), kst[:, :, 0:gw], dsem_st, deps=cps)
                    sq_ = P.dma("gpsimd", qloc[s][:, g0:g0 + gw].rearrange("(b p) t -> p b t", p=128), qst[:, :, 0:gw], dsem_st, deps=cps)
                    st_prev["k"], st_prev["q"] = sk, sq_
        phase_done = [prev_tile_done["tok"], (dsem_st.h, dsem_st.count)]

    def phase_att(l, s, qcol0, qn, qquarter, is_meta):
        nonlocal phase_done
        if skip_phase():
            return
        P.reset_barrier()
        state["wstg_free"] = [None, None]
        pd = []
        N = NSEQ[s]
        Rq = RQ[s]
        KBs = KB_[s]
        if is_meta:
            qchunks = [(0, NMETA)]
            TQ = NMETA
        else:
            qchunks = chunks(qn, QC)
            TQ = qn
        NCH = len(qchunks)
        kbuf = [WBIG[:, (i * 2) * RMAX:(i * 2 + 2) * RMAX].rearrange("p (m t) -> p m t", t=RMAX) for i in range(3)]
        vo = 6 * RMAX
        VT = max(1, RMAX // 128)
        vbuf = [WBIG[:, vo + i * VT * 130: vo + (i + 1) * VT * 130].rearrange("p (t c) -> p t c", c=130) for i in range(3)]
        qo = vo + 3 * VT * 130
        qbuf = [WBIG[:, qo + i * 4 * TQM: qo + (i + 1) * 4 * TQM].rearrange("p (m t) -> p m t", t=TQM) for i in range(2)]
        po = qo + 8 * TQM
        pbuf = WBIG[:, po:po + 4 * 512].rearrange("p (i t) -> p i t", t=512)
        assert po + 2048 + 1536 <= 64 * 1024
        oacc = work(0, [128, 4, NCH, 512])
        sacc = work(2 * NCH * 512, [1, 2, NCH, 512])
        assert 4 * NCH * 512 <= 12800
        tmpA = WORK[:, 12800:13312]
        tmpB = WORK[:, 13312:13824]
        rcp = WORK[:, 13824:14336]
        ostA = WBIG[:, po + 2048:po + 2048 + 512]
        ostB = WBIG[:, po + 2560:po + 2560 + 1024].bitcast(F32)

        t_ones = [P.op("gpsimd", lambda e, i=i: e.memset(vbuf[i][:, :, 128:129], 1.0), deps=pd) for i in range(3)]
        passes = [("A", 0), ("A", 1)] + [("B", h) for h in range(4)] + [("C", h) for h in range(4)]
        S = {"step": 0, "unit": 0, "pv_tok": {}, "blk": 0, "exp_tok": {}, "ep_free": None, "ost_free": [None, None],
             "bank6": None, "last_evac": {}, "evac_by_unit": {}, "pending": [], "last_pv": None}
        slot_free = [None, None, None]
        qb_free = [None, None]
        kblocks = [(c0, w, c0 // Rq, "g") for (c0, w) in chunks(N, KBs)] + [(N, NMETA, 4, "m")]

        for pi, (kind, idx) in enumerate(passes):
            qi = pi % 2
            Q = qbuf[qi]
            nm = {"A": 4, "B": 2, "C": 1}[kind]
            qz = P.op("gpsimd", lambda e, Q=Q: e.memset(Q[:, :, :], 0.0), deps=[qb_free[qi]] + pd)
            qsrc0 = N if is_meta else qcol0
            qt = []
            if kind == "A":
                for m in range(4):
                    h = idx * 4 + m
                    qt.append(P.dma("sync", Q[idx * 64:(idx + 1) * 64, m, 0:TQ], qloc[s][h * 64:(h + 1) * 64, qsrc0:qsrc0 + TQ], dsem_q[qi], deps=[qz]))
            elif kind == "B":
                for m in range(2):
                    mp = idx * 2 + m
                    qt.append(P.dma("sync", Q[0:64, m, 0:TQ], qloc[s][512 + mp * 64:512 + (mp + 1) * 64, qsrc0:qsrc0 + TQ], dsem_q[qi], deps=[qz]))
                    if is_meta:
                        qt.append(P.dma("sync", Q[0:64, m, TQ:2 * TQ], qloc[s][512 + mp * 64:512 + (mp + 1) * 64, qsrc0:qsrc0 + TQ], dsem_q[qi], deps=[qz]))
                        qt.append(P.dma("sync", Q[64:64 + NAUG, m, 0:2 * TQ], qaug[s][idx, :, N:N + 2 * NMETA], dsem_q[qi], deps=[qz]))
                    else:
                        qt.append(P.dma("sync", Q[64:64 + NAUG, m, 0:TQ], qaug[s][idx, :, qcol0:qcol0 + TQ], dsem_q[qi], deps=[qz]))
            else:
                qt.append(P.dma("sync", Q[:, 0, 0:TQ], qloc[s][1024 + idx * 128:1024 + (idx + 1) * 128, qsrc0:qsrc0 + TQ], dsem_q[qi], deps=[qz]))
                hh = idx % 2
                qt.append(P.dma("sync", Q[hh * 64:(hh + 1) * 64, 1, 0:TQ], qloc[s][1536 + idx * 64:1536 + (idx + 1) * 64, qsrc0:qsrc0 + TQ], dsem_q[qi], deps=[qz]))
            qtok = qt[-1]
            blocks = []
            for (c0, w, bq, bk) in kblocks:
                if kind == "B" and bk == "g" and (not is_meta) and bq == qquarter:
                    blocks.append((c0, w, bq, "l", 0))
                    blocks.append((c0, w, bq, "l", 1))
                else:
                    blocks.append((c0, w, bq, bk, None))
            first = {}
            for (kc0, nkeys, bq, bk, lm) in blocks:
                si = S["blk"] % 3
                S["blk"] += 1
                KB, VB = kbuf[si], vbuf[si]
                sem = dsem_kv[si]
                dps = [slot_free[si]] + pd + [t_ones[si]]
                ntb = max(1, nkeys // 128)
                if bk == "m":
                    ksrc, kcs = kvm[s], 0
                else:
                    ksrc, kcs = kvl[s], kc0
                nrow = ksrc.shape[1]

                def vload(c0, w, dcol):
                    base = 1408 * nrow + kcs * 1152
                    if bk == "m":
                        return P.dma("sync", VB[0:NMETA, 0, dcol:dcol + w], bass.AP(ksrc.tensor, base + c0, [[1152, NMETA], [1, w]]), sem, deps=dps)
                    return P.dma("sync", VB[:, 0:ntb, dcol:dcol + w], bass.AP(ksrc.tensor, base + c0, [[1152, 128], [128 * 1152, ntb], [1, w]]), sem, deps=dps)
                kt = []
                if kind == "A":
                    kt.append(P.dma("sync", KB[:, 0, 0:nkeys], ksrc[0:128, kcs:kcs + nkeys], sem, deps=dps))
                    kt.append(vload(idx * 64, 64, 64))
                elif kind == "B":
                    for m in range(2):
                        mp = idx * 2 + (m if bk != "l" else lm)
                        kt.append(P.dma("sync", KB[0:64, m, 0:nkeys], ksrc[128 + mp * 64:128 + (mp + 1) * 64, kcs:kcs + nkeys], sem, deps=dps))
                        if bk == "l":
                            rel = kc0 - bq * Rq
                            kt.append(P.dma("sync", KB[64:64 + NAUG, m, 0:nkeys], kaugl[s][idx, m, :, rel:rel + nkeys], sem, deps=dps))
                        else:
                            kt.append(P.dma("sync", KB[64:64 + NAUG, m, 0:nkeys], kaug[s][idx, :, kc0:kc0 + nkeys], sem, deps=dps))
                    kt.append(vload(128 + idx * 128, 128, 0))
                else:
                    kt.append(P.dma("sync", KB[:, 0, 0:nkeys], ksrc[640 + idx * 128:640 + (idx + 1) * 128, kcs:kcs + nkeys], sem, deps=dps))
                    kt.append(P.dma("sync", KB[:, 1, 0:nkeys], ksrc[1152 + (idx // 2) * 128:1152 + (idx // 2 + 1) * 128, kcs:kcs + nkeys], sem, deps=dps))
                    kt.append(vload(640 + idx * 128, 128, 0))
                ktok = kt[-1]
                ktiles = [(0, NMETA)] if bk == "m" else [(t * 128, 128) for t in range(ntb)]
                for m in ([lm] if bk == "l" else list(range(nm))):
                    for ci, (q0, qw) in enumerate(qchunks):
                        qrel = (qcol0 - qquarter * Rq + q0) if not is_meta else 0
                        krel = kc0 - bq * Rq if bk != "m" else 0
                        fb = (m, ci) not in first
                        first[(m, ci)] = True
                        run_unit(S, kind, m, ci, q0, qw, is_meta, bk, KB, VB, Q, ktiles, ktok, qtok,
                                 oacc, sacc, pbuf, tmpA, tmpB, fb, qrel, krel)
                pipe_flush(S)
                slot_free[si] = S["last_pv"]
            qb_free[qi] = S["last_pv"]
            for m in range(nm):
                for ci, (q0, qw) in enumerate(qchunks):
                    M = 64 if kind == "A" else 128
                    if kind == "A":
                        lrow_ap = oacc[64:65, m, ci, 0:qw]
                        onesl = ones_f[64:65, 0:M]
                    else:
                        lrow_ap = sacc[0:1, m, ci, 0:qw]
                        onesl = ones_f[0:1, 0:M]
                    evac = S["last_evac"][(m, ci)]
                    bc = P.op("tensor", lambda e, lrow_ap=lrow_ap, onesl=onesl, M=M, qw=qw: e.matmul(
                        ps[0:M, 6, 0:qw], lhsT=onesl, rhs=lrow_ap, start=True, stop=True), deps=[evac, S["bank6"]])
                    r1 = P.op("vector", lambda e, M=M, qw=qw: e.reciprocal(out=rcp[0:M, 0:qw], in_=ps[0:M, 6, 0:qw]), deps=[bc, S["ep_free"]])
                    S["bank6"] = r1
                    dcol = (N if is_meta else qcol0) + q0
                    if kind == "B":
                        ost, key = ostB, 0
                        dst = oB[s][(idx * 2 + m) * 128:(idx * 2 + m + 1) * 128, dcol:dcol + qw]
                    elif kind == "A":
                        ost, key = ostA, 1
                        h = idx * 4 + m
                        dst = oA[s][h * 64:(h + 1) * 64, dcol:dcol + qw]
                    else:
                        ost, key = ostA, 1
                        dst = oC[s][idx * 128:(idx + 1) * 128, dcol:dcol + qw]
                    r2 = P.op("vector", lambda e, M=M, qw=qw, m=m, ci=ci, ost=ost: e.tensor_tensor(
                        out=ost[0:M, 0:qw], in0=oacc[0:M, m, ci, 0:qw], in1=rcp[0:M, 0:qw], op=ALU.mult),
                        deps=[r1, S["ost_free"][key]])
                    S["ep_free"] = r2
                    stt = P.dma("gpsimd", dst, ost[0:M, 0:qw], dsem_o, deps=[r2])
                    S["ost_free"][key] = stt
        phase_done = [S["last_pv"], (dsem_o.h, dsem_o.count), S["ep_free"]]

    PIPE_D = 2

    def pipe_push(S, back):
        S["pending"].append(back)
        while len(S["pending"]) > PIPE_D:
            S["pending"].pop(0)()

    def pipe_flush(S):
        while S["pending"]:
            S["pending"].pop(0)()

    def run_unit(S, kind, m, ci, q0, qw, is_meta, bk, KB, VB, Q, ktiles, ktok, qtok,
                 oacc, sacc, pbuf, tmpA, tmpB, first_block, qrel, krel):
        scale = {"A": 0.125, "B": 0.125, "C": 192 ** -0.5}[kind]
        shift = {"A": -8.0, "B": -8.0, "C": -(192 ** 0.5)}[kind]
        u = S["unit"]
        S["unit"] += 1
        ob = 3 + (u % 2)
        sum_ps = ps[0:1, 5, :] if (u % 2 == 0) else pstf[0:1, :]
        M = 65 if kind == "A" else 128
        nt = len(ktiles)
        KA = 64 + NAUG
        for ti, (k0, kw) in enumerate(ktiles):
            i = S["step"]
            S["step"] += 1
            sbk = i % 3
            pslot = i % 4
            diag = False
            var = 0
            if kind == "B":
                if bk == "m" and is_meta:
                    diag = True
                elif bk == "l":
                    if krel + k0 + kw <= qrel:
                        var = 0
                    elif krel + k0 >= qrel + qw:
                        var = 1
                    else:
                        diag = True
            sfree = S["exp_tok"].get(i - 3)

            def qk(bank, variant, xd=(), k0=k0, kw=kw):
                xd = list(xd)
                if kind == "A":
                    return P.op("tensor", lambda e: e.matmul(ps[0:kw, bank, 0:qw], lhsT=KB[:, 0, k0:k0 + kw], rhs=Q[:, m, q0:q0 + qw], start=True, stop=True), deps=[ktok, qtok, sfree] + xd)
                if kind == "C":
                    P.op("tensor", lambda e: e.matmul(ps[0:kw, bank, 0:qw], lhsT=KB[:, 0, k0:k0 + kw], rhs=Q[:, 0, q0:q0 + qw], start=True, stop=False), deps=[ktok, qtok, sfree] + xd)
                    return P.op("tensor", lambda e: e.matmul(ps[0:kw, bank, 0:qw], lhsT=KB[:, 1, k0:k0 + kw], rhs=Q[:, 1, q0:q0 + qw], start=False, stop=True), deps=[])
                if bk == "l":
                    kslice = KB[0:KA, variant, k0:k0 + kw]
                    qcol = q0
                elif bk == "m" and is_meta:
                    kslice = KB[0:KA, m, k0:k0 + kw]
                    qcol = q0 + (NMETA if variant == 1 else 0)
                else:
                    kslice = KB[0:KA, m, k0:k0 + kw]
                    qcol = q0
                return P.op("tensor", lambda e: e.matmul(ps[0:kw, bank, 0:qw], lhsT=kslice, rhs=Q[0:KA, m, qcol:qcol + qw], start=True, stop=True), deps=[ktok, qtok, sfree] + xd)
            pfree = S["pv_tok"].get(i - 4)
            if not diag:
                mm = qk(sbk, var)
                ex = P.op("scalar", lambda e: e.activation(out=pbuf[0:kw, pslot, 0:qw], in_=ps[0:kw, sbk, 0:qw], func=AF.Exp, scale=scale, bias=shift), deps=[mm, pfree])
            else:
                mm0 = qk(sbk, 0)
                mm1 = qk(6, 1, [S.get("diag_free"), S.get("bank6")])
                cA = P.op("scalar", lambda e: e.activation(out=tmpA[0:kw, 0:qw], in_=ps[0:kw, sbk, 0:qw], func=AF.Copy), deps=[mm0, S.get("diag_free")])
                mn = P.op("vector", lambda e: e.tensor_tensor(out=tmpB[0:kw, 0:qw], in0=ps[0:kw, 6, 0:qw], in1=tmpA[0:kw, 0:qw], op=ALU.min), deps=[cA, mm1, S.get("diag_free2")])
                S["diag_free"] = mn
                S["bank6"] = mn
                ex = P.op("scalar", lambda e: e.activation(out=pbuf[0:kw, pslot, 0:qw], in_=tmpB[0:kw, 0:qw], func=AF.Exp, scale=scale, bias=shift), deps=[mn, pfree])
                S["diag_free2"] = ex
            S["exp_tok"][i] = ex

            def back(i=i, ti=ti, k0=k0, kw=kw, pslot=pslot, ex=ex):
                if kind == "A":
                    lhs = VB[0:kw, k0 // 128, 64:129]
                else:
                    lhs = VB[0:kw, k0 // 128, 0:128]
                evac_dep = S["evac_by_unit"].get(u - 2) if ti == 0 else None
                pv = P.op("tensor", lambda e: e.matmul(ps[0:M, ob, 0:qw], lhsT=lhs, rhs=pbuf[0:kw, pslot, 0:qw], start=(ti == 0), stop=(ti == nt - 1)), deps=[ex, evac_dep])
                if kind != "A":
                    pv = P.op("tensor", lambda e: e.matmul(sum_ps[:, 0:qw], lhsT=ones_bf[0:kw, 0:1], rhs=pbuf[0:kw, pslot, 0:qw], start=(ti == 0), stop=(ti == nt - 1)), deps=[])
                S["pv_tok"][i] = pv
                S["last_pv"] = pv
                if ti != nt - 1:
                    return
                prev = S["last_evac"].get((m, ci))
                if first_block:
                    ev = P.op("vector", lambda e: e.tensor_copy(out=oacc[0:M, m, ci, 0:qw], in_=ps[0:M, ob, 0:qw]), deps=[pv, S["ep_free"]])
                    if kind != "A":
                        ev = P.op("vector", lambda e: e.tensor_copy(out=sacc[0:1, m, ci, 0:qw], in_=sum_ps[:, 0:qw]), deps=[pv, S["ep_free"]])
                else:
                    ev = P.op("vector", lambda e: e.tensor_tensor(out=oacc[0:M, m, ci, 0:qw], in0=ps[0:M, ob, 0:qw], in1=oacc[0:M, m, ci, 0:qw], op=ALU.add), deps=[pv, prev])
                    if kind != "A":
                        ev = P.op("vector", lambda e: e.tensor_tensor(out=sacc[0:1, m, ci, 0:qw], in0=sum_ps[:, 0:qw], in1=sacc[0:1, m, ci, 0:qw], op=ALU.add), deps=[pv, prev])
                S["last_evac"][(m, ci)] = ev
                S["evac_by_unit"][u] = ev
            pipe_push(S, back)

    def phase_p5(l, groups, xsrc_reg, xsrc_meta, dst_fn):
        nonlocal phase_done
        if skip_phase():
            return
        P.reset_barrier()
        state["wstg_free"] = [None, None]
        pd = []
        tl0 = P.dma("sync", lrow[0:1, 262:264], lamc, dsem_misc, deps=pd)
        tl1 = P.dma("sync", lamt[:, 1:2], bcast_row(lamc[0:1, 1:2], 1), dsem_misc, deps=pd)
        Wg = wview(0, 8, 3072)
        Wb = [wview(8 * 3072 + i * 4 * 1024, 4, 1024) for i in range(3)]
        Wo = wview(8 * 3072 + 12 * 1024, 8, 1024)
        wt = load_w(Wg, w_in[l][:, NQKV:5824], deps=pd)
        for i in range(3):
            wt += load_w(Wb[i], w_br[i][l], deps=pd)
        wt += load_w(Wo, w_out[l], deps=pd)
        t1 = P.dma("sync", gT[:], attn_g[l].rearrange("(k p) -> p k", p=128), dsem_misc, deps=pd)
        t1 = P.dma("sync", gT2[:], mlp_g[l].rearrange("(k p) -> p k", p=128), dsem_misc, deps=pd)
        t1 = P.dma("sync", bgT[:], b_gate[l].rearrange("(k p) -> p k", p=128), dsem_misc, deps=pd)
        t1 = P.dma("sync", gsub[:], b_sub[l].rearrange("(p o) -> p o", o=1), dsem_misc, deps=pd)
        for j, nm in enumerate(["b_lambda_q1", "b_lambda_k1", "b_lambda_q2", "b_lambda_k2"]):
            t1 = P.dma("sync", lrow[0:1, j * 64:(j + 1) * 64], g64[nm][l:l + 1, :], dsem_misc, deps=pd)
        la = P.op("vector", lambda e: e.tensor_tensor(out=lrow[0:1, 0:64], in0=lrow[0:1, 0:64], in1=lrow[0:1, 64:128], op=ALU.mult), deps=[t1])
        lb = P.op("vector", lambda e: e.tensor_tensor(out=lrow[0:1, 128:192], in0=lrow[0:1, 128:192], in1=lrow[0:1, 192:256], op=ALU.mult), deps=[t1])
        lc = P.op("vector", lambda e: e.tensor_reduce(out=lrow[0:1, 256:257], in_=lrow[0:1, 0:64], axis=AX.X, op=ALU.add), deps=[la])
        ld_ = P.op("vector", lambda e: e.tensor_reduce(out=lrow[0:1, 257:258], in_=lrow[0:1, 128:192], axis=AX.X, op=ALU.add), deps=[lb])
        le = P.op("scalar", lambda e: e.activation(out=lrow[0:1, 258:260], in_=lrow[0:1, 256:258], func=AF.Exp), deps=[lc, ld_])
        lf = P.op("vector", lambda e: e.tensor_tensor(out=lrow[0:1, 260:261], in0=lrow[0:1, 259:260], in1=lrow[0:1, 258:259], op=ALU.subtract), deps=[le])
        lg = P.op("vector", lambda e: e.tensor_scalar(out=lrow[0:1, 261:262], in0=lrow[0:1, 260:261], scalar1=lrow[0:1, 262:263], scalar2=None, op0=ALU.add), deps=[lf, tl0])
        lh = P.op("tensor", lambda e: e.matmul(ps[:, 6, 0:1], lhsT=ones_f[0:1, :], rhs=lrow[0:1, 261:262], start=True, stop=True), deps=[lg] + pd)
        li = P.op("vector", lambda e: e.tensor_copy(out=lamt[:, 0:1], in_=ps[:, 6, 0:1]), deps=[lh])
        lj = P.op("vector", lambda e: e.tensor_scalar(out=gsub[:], in0=gsub[:], scalar1=lamt[:, 1:2], scalar2=None, op0=ALU.mult), deps=[t1, tl1])
        setup = wt + [li, lj]

        GW = 256
        xt = work(0, [128, 2, 1024])
        hT = work(2048, [128, 8, GW], BF16)
        gTt = work(3072, [128, 24, GW], BF16)
        oAs = work(6144, [128, 4, GW], BF16)
        oCs = work(6656, [128, 4, GW], BF16)
        oBs = work(7168, [128, 8, GW])
        dd = work(9216, [128, GW])
        sqv = work(9472, [128, GW])
        obn = work(9728, [128, 4, GW], BF16)
        mT = work(10240, [128, 8, GW], BF16)
        macc = work(11264, [128, GW])
        mtmp = work(11520, [128, GW])
        x1t = work(11776, [128, 1024])
        prevg = None
        st1 = None
        for gi, (s, row0, ntok, tok0) in enumerate(groups):
            ntile = (ntok + 127) // 128
            pg = prevg
            ld = []
            if s is None:
                mz = P.op("gpsimd", lambda e: e.memset(xt[:, 0, :], 0.0), deps=[pg] + pd)
                ld.append(P.dma("sync", xt[0:NMT, 0, :], xsrc_meta, dsem_x[0], deps=[mz]))
            else:
                for t in range(ntile):
                    ld.append(P.dma("sync", xt[:, t, :], xsrc_reg[row0 + t * 128:row0 + (t + 1) * 128, :], dsem_x[0], deps=[pg] + pd))
            fl = []
            for t in range(ntile):
                fl.append(rmsnorm_T(xt[:, t, :], gT, hT, t * 128, [ld[-1], pg] + setup + fl, 11776))
            W_ = ntile * 128
            ol = []
            if s is None:
                mz2 = P.op("gpsimd", lambda e: e.memset(WORK[:, 6144:9216], 0.0), deps=[pg] + pd)
                for s4 in range(2):
                    c0 = NSEQ[s4]
                    ol.append(P.dma("sync", oAs[:, :, s4 * 16:(s4 + 1) * 16], oA[s4][:, c0:c0 + NMETA].rearrange("(b p) t -> p b t", p=128), dsem_x[1], deps=[mz2]))
                    ol.append(P.dma("sync", oCs[:, :, s4 * 16:(s4 + 1) * 16], oC[s4][:, c0:c0 + NMETA].rearrange("(b p) t -> p b t", p=128), dsem_x[1], deps=[mz2]))
                    ol.append(P.dma("sync", oBs[:, :, s4 * 16:(s4 + 1) * 16], oB[s4][:, c0:c0 + NMETA].rearrange("(b p) t -> p b t", p=128), dsem_x[1], deps=[mz2]))
            else:
                ol.append(P.dma("sync", oAs[:, :, 0:ntok], oA[s][:, tok0:tok0 + ntok].rearrange("(b p) t -> p b t", p=128), dsem_x[1], deps=[pg] + pd))
                ol.append(P.dma("sync", oCs[:, :, 0:ntok], oC[s][:, tok0:tok0 + ntok].rearrange("(b p) t -> p b t", p=128), dsem_x[1], deps=[pg] + pd))
                ol.append(P.dma("sync", oBs[:, :, 0:ntok], oB[s][:, tok0:tok0 + ntok].rearrange("(b p) t -> p b t", p=128), dsem_x[1], deps=[pg] + pd))
            olt = ol[-1]
            lastb = None
            for h in range(4):
                d1 = P.op("vector", lambda e, h=h: e.scalar_tensor_tensor(out=dd[:, 0:W_], in0=oBs[:, 2 * h + 1, 0:W_], scalar=lamt[:, 0:1], in1=oBs[:, 2 * h, 0:W_], op0=ALU.mult, op1=ALU.add), deps=[olt, lastb] + setup)
                d2 = P.op("scalar", lambda e: e.activation(out=sqv[:, 0:W_], in_=dd[:, 0:W_], func=AF.Square), deps=[d1, lastb])
                d3 = P.op("tensor", lambda e: e.matmul(ps[:, 5, 0:W_], lhsT=mean_f[:, :], rhs=sqv[:, 0:W_], start=True, stop=True), deps=[d2, lastb] + fl)
                d4 = P.op("scalar", lambda e: e.activation(out=sqv[:, 0:W_], in_=ps[:, 5, 0:W_], func=AF.Sqrt, bias=EPS), deps=[d3])
                d5 = P.op("vector", lambda e: e.reciprocal(out=sqv[:, 0:W_], in_=sqv[:, 0:W_]), deps=[d4])
                d6 = P.op("vector", lambda e: e.tensor_tensor(out=dd[:, 0:W_], in0=dd[:, 0:W_], in1=sqv[:, 0:W_], op=ALU.mult), deps=[d5])
                lastb = P.op("vector", lambda e, h=h: e.tensor_scalar(out=obn[:, h, 0:W_], in0=dd[:, 0:W_], scalar1=gsub[:, 0:1], scalar2=None, op0=ALU.mult), deps=[d6, pg])
            gacts = []
            for j in range(24):
                bank = j % 3
                mm = None
                for k in range(8):
                    mm = P.op("tensor", lambda e, j=j, k=k, bank=bank: e.matmul(ps[:, bank, 0:W_], lhsT=Wg[:, k, j * 128:(j + 1) * 128], rhs=hT[:, k, 0:W_], start=(k == 0), stop=(k == 7)), deps=fl + [gacts[j - 3] if j >= 3 else None, pg])
                gacts.append(P.op("scalar", lambda e, j=j, bank=bank: e.activation(out=gTt[:, j, 0:W_], in_=ps[:, bank, 0:W_], func=AF.Sigmoid, bias=bgT[:, j:j + 1]), deps=[mm, pg, lastb]))
            gates_done = gacts[-1]
            srcs = [oAs, obn, oCs]
            ml = None
            for j in range(8):
                mms = []
                for br in range(3):
                    mm = None
                    for k in range(4):
                        mm = P.op("tensor", lambda e, j=j, br=br, k=k: e.matmul(ps[:, br, 0:W_], lhsT=Wb[br][:, k, j * 128:(j + 1) * 128], rhs=srcs[br][:, k, 0:W_], start=(k == 0), stop=(k == 3)), deps=[gates_done, olt, lastb, ml])
                    mms.append(mm)
                a0 = P.op("vector", lambda e, j=j: e.tensor_tensor(out=macc[:, 0:W_], in0=ps[:, 0, 0:W_], in1=gTt[:, j, 0:W_], op=ALU.mult), deps=[mms[0], gates_done, ml])
                a1 = P.op("vector", lambda e, j=j: e.tensor_tensor(out=mtmp[:, 0:W_], in0=ps[:, 1, 0:W_], in1=gTt[:, 8 + j, 0:W_], op=ALU.mult), deps=[mms[1], ml])
                a2 = P.op("vector", lambda e: e.tensor_tensor(out=macc[:, 0:W_], in0=macc[:, 0:W_], in1=mtmp[:, 0:W_], op=ALU.add), deps=[a0, a1])
                a3 = P.op("vector", lambda e, j=j: e.tensor_tensor(out=mtmp[:, 0:W_], in0=ps[:, 2, 0:W_], in1=gTt[:, 16 + j, 0:W_], op=ALU.mult), deps=[mms[2], a2])
                ml = P.op("vector", lambda e, j=j: e.tensor_tensor(out=mT[:, j, 0:W_], in0=macc[:, 0:W_], in1=mtmp[:, 0:W_], op=ALU.add), deps=[a3, pg])
            last = ml
            for t in range(ntile):
                mm = None
                for c in range(2):
                    for k in range(8):
                        mm = P.op("tensor", lambda e, t=t, c=c, k=k: e.matmul(ps[:, 3 + c, :], lhsT=mT[:, k, t * 128:(t + 1) * 128], rhs=Wo[:, k, c * 512:(c + 1) * 512], start=(k == 0), stop=(k == 7)), deps=[ml, last])
                nrow = min(128, ntok - t * 128)
                ad = P.op("vector", lambda e, t=t: e.tensor_tensor(out=x1t[:, :], in0=ps[:, 3:5, :].rearrange("p c n -> p (c n)"), in1=xt[:, t, :], op=ALU.add), deps=[mm, st1])
                st1 = P.dma("gpsimd", x1buf[row0 + t * 128:row0 + t * 128 + nrow, :], x1t[0:nrow, :], dsem_st, deps=[ad])
                last = ad
            prevg = [last, st1]
        P.reset_barrier()
        state["wstg_free"] = [None, None]
        pd = []
        Wu = wview(0, 8, 4096)
        Wd = wview(8 * 4096, 32, 1024)
        wt = load_w(Wu, w_up[l], deps=pd)
        wt += load_w(Wd, w_down[l], deps=pd)
        setup = wt
        xt = work(0, [128, 2, 1024])
        hT = work(2048, [128, 8, GW], BF16)
        aT = work(3072, [128, 32, GW], BF16)
        rl = [work(7168, [128, GW]), work(7424, [128, GW])]
        yt = work(7680, [128, 1024])
        prevg = None
        sto = None
        for gi, (s, row0, ntok, tok0) in enumerate(groups):
            ntile = (ntok + 127) // 128
            pg = prevg
            ld = []
            if s is None:
                mz = P.op("gpsimd", lambda e: e.memset(xt[:, 0, :], 0.0), deps=[pg] + pd)
                ld.append(P.dma("sync", xt[0:NMT, 0, :], x1buf[TOK:TALL, :], dsem_x[0], deps=[mz]))
            else:
                for t in range(ntile):
                    ld.append(P.dma("sync", xt[:, t, :], x1buf[row0 + t * 128:row0 + (t + 1) * 128, :], dsem_x[0], deps=[pg] + pd))
            fl = []
            for t in range(ntile):
                fl.append(rmsnorm_T(xt[:, t, :], gT2, hT, t * 128, [ld[-1], pg] + setup + fl, 8704))
            W_ = ntile * 128
            rq = []
            sq_all = []
            for j in range(32):
                bank = j % 3
                mm = None
                for k in range(8):
                    mm = P.op("tensor", lambda e, j=j, k=k, bank=bank: e.matmul(ps[:, bank, 0:W_], lhsT=Wu[:, k, j * 128:(j + 1) * 128], rhs=hT[:, k, 0:W_], start=(k == 0), stop=(k == 7)), deps=fl + [rq[j - 3] if j >= 3 else None, pg])
                r_ = P.op("scalar", lambda e, j=j, bank=bank: e.activation(out=rl[j % 2][:, 0:W_], in_=ps[:, bank, 0:W_], func=AF.Relu), deps=[mm, sq_all[j - 2] if j >= 2 else None, pg])
                rq.append(r_)
                sq_all.append(P.op("gpsimd" if j % 2 else "vector", lambda e, j=j: e.tensor_tensor(out=aT[:, j, 0:W_], in0=rl[j % 2][:, 0:W_], in1=rl[j % 2][:, 0:W_], op=ALU.mult), deps=[r_, pg]))
            last = None
            for t in range(ntile):
                mm = None
                for c in range(2):
                    for k in range(32):
                        mm = P.op("tensor", lambda e, t=t, c=c, k=k: e.matmul(ps[:, 3 + c, :], lhsT=aT[:, k, t * 128:(t + 1) * 128], rhs=Wd[:, k, c * 512:(c + 1) * 512], start=(k == 0), stop=(k == 31)), deps=[sq_all[-1], sq_all[-2], last])
                nrow = min(128, ntok - t * 128)
                ad = P.op("vector", lambda e, t=t: e.tensor_tensor(out=yt[:, :], in0=ps[:, 3:5, :].rearrange("p c n -> p (c n)"), in1=xt[:, t, :], op=ALU.add), deps=[mm, sto])
                sto = P.dma("gpsimd", dst_fn(gi, t, nrow), yt[0:nrow, :], dsem_st, deps=[ad])
                last = ad
            prevg = [last, sto]
        phase_done = [prevg, (dsem_st.h, dsem_st.count)]

    GW = 256
    l = 0
    phase_p1(l, x_in, meta_in)
    for s in range(2):
        for (g0, gn) in chunks(RQ[s], QG_[s]):
            phase_att(l, s, g0, gn, 0, False)
        phase_att(l, s, NSEQ[s], NMETA, 4, True)
    groups = []
    yoff = [0, RQ[0]]
    for s in range(2):
        for (o, w) in chunks(RQ[s], GW):
            groups.append((s, OFF[s] + o, w, o))
    groups.append((None, TOK, NMT, 0))

    def dst_fn(gi, t, nrow, groups=groups, yoff=yoff):
        s, row0, ntok, tok0 = groups[gi]
        if s is None:
            return y_out[TOKQ:TOKQ + NMT, :]
        r = yoff[s] + tok0 + t * 128
        return y_out[r:r + nrow, :]
    phase_p5(l, groups, x_in, meta_in, dst_fn)
    P.reset_barrier()
    fin = P.dma("sync", y_out[0:1, 0:128], ident_in[0:1, :], dsem_misc) if stop < 10 ** 9 else None
    P.wait_only("sync", [fin])

    with nc.allow_non_contiguous_dma(reason="small strided param/V loads"), nc.Block() as block:
        P.replay(block)
    es.close()
    return nc


def _local_order(c, N):
    r = c % 4
    Rq = N // 4
    order = [r] + [g for g in range(4) if g != r]
    return np.concatenate([np.arange(g * Rq, (g + 1) * Rq) for g in order]), order


def _tables(NP, NS, c):
    f32 = np.float32
    NSEQ = [NP, NS]
    TOK = NP + NS

    def rope_tab(pos, d):
        half = d // 2
        inv = (f32(THETA) ** (f32(-2.0) * np.arange(half, dtype=f32) / f32(d))).astype(f32)
        ang = (pos.astype(f32)[:, None] * inv[None, :]).astype(f32)
        return np.cos(ang).astype(f32), np.sin(ang).astype(f32)

    ropeA = np.zeros((TOK + 2 * NMETA, 128), f32)
    ropeC = np.zeros((TOK + 2 * NMETA, 128), f32)

    def fill(rows, row, col, lin):
        cr, sr = rope_tab(row, 32)
        cc, sc = rope_tab(col, 32)
        ropeA[rows, 0:64] = np.concatenate([cr, cr, cc, cc], axis=1)
        ropeA[rows, 64:128] = np.concatenate([-sr, sr, -sc, sc], axis=1)
        cl, sl = rope_tab(lin, 64)
        ropeC[rows, 0:64] = np.concatenate([cl, cl], axis=1)
        ropeC[rows, 64:128] = np.concatenate([-sl, sl], axis=1)
    off = 0
    m = np.arange(NMETA)
    slopes = [2.0 ** (-8.0 * (h + 1) / 4) for h in range(4)]
    qaug, kaug, kaugl = [], [], []
    for s in range(2):
        N = NSEQ[s]
        Rq = N // 4
        n, order = _local_order(c, N)
        fill(slice(off, off + N), (n // GRID_W).astype(f32), (n % GRID_W).astype(f32), (NMETA + n).astype(f32))
        off += N
        fill(slice(TOK + s * NMETA, TOK + (s + 1) * NMETA), np.full(NMETA, -1.0, f32), m.astype(f32), m.astype(f32))
        pos = NMETA + n
        lq = np.arange(N) // Rq
        gq = np.array(order)[lq]
        irel = np.arange(N) % Rq
        qa = np.zeros((4, NAUG, N + 2 * NMETA), np.float64)
        ka = np.zeros((4, NAUG, N + NMETA), np.float64)
        kl = np.zeros((4, 2, NAUG, Rq), np.float64)
        for h in range(4):
            c8 = 8.0 * slopes[h]
            hi, lo = pos // 128, pos % 128
            for b in range(5):
                r0 = 4 * b
                if b < 4:
                    sig = np.sign(gq - order[b]).astype(np.float64)
                else:
                    sig = np.ones(N)
                qa[h, r0 + 0, 0:N] = sig * (-c8 * 128.0) * hi
                qa[h, r0 + 1, 0:N] = sig * (-c8) * lo
                qa[h, r0 + 2, 0:N] = sig
                qa[h, r0 + 3, 0:N] = sig
                for rep in range(2):
                    sg = -1.0 if (b < 4 or rep == 1) else 1.0
                    cs = slice(N + rep * NMETA, N + (rep + 1) * NMETA)
                    qa[h, r0 + 0, cs] = 0.0
                    qa[h, r0 + 1, cs] = sg * (-c8) * m
                    qa[h, r0 + 2, cs] = sg
                    qa[h, r0 + 3, cs] = sg
                if b < 4:
                    sel = np.where(lq == b)[0]
                    ka[h, r0 + 0, sel] = 1.0
                    ka[h, r0 + 1, sel] = 1.0
                    ka[h, r0 + 2, sel] = c8 * 128.0 * hi[sel]
                    ka[h, r0 + 3, sel] = c8 * lo[sel]
                else:
                    ka[h, r0 + 0, N:N + NMETA] = 1.0
                    ka[h, r0 + 1, N:N + NMETA] = 1.0
                    ka[h, r0 + 2, N:N + NMETA] = 0.0
                    ka[h, r0 + 3, N:N + NMETA] = c8 * m
            qa[h, 20, 0:N] = (-c8 * 128.0) * (irel // 128)
            qa[h, 21, 0:N] = (-c8) * (irel % 128)
            qa[h, 22, 0:N] = 1.0
            qa[h, 23, 0:N] = 1.0
            j = np.arange(Rq)
            for v, sg in ((0, 1.0), (1, -1.0)):
                kl[h, v, 20, :] = sg
                kl[h, v, 21, :] = sg
                kl[h, v, 22, :] = sg * c8 * 128.0 * (j // 128)
                kl[h, v, 23, :] = sg * c8 * (j % 128)
        qaug.append(qa.reshape(4 * NAUG, -1).astype(f32))
        kaug.append(ka.reshape(4 * NAUG, -1).astype(f32))
        kaugl.append(kl.reshape(8 * NAUG, -1).astype(f32))
    return ropeA, ropeC, qaug, kaug, kaugl


_CACHE = {}


def run(inputs, NP, NS, **kw):
    key = (NP, NS, tuple(sorted(kw.items())))
    if key not in _CACHE:
        _CACHE[key] = build_program(NP, NS, **kw)
    nc = _CACHE[key]
    xcur = [np.asarray(inputs["x_prompt"], dtype=np.float32), np.asarray(inputs["x_sample"], dtype=np.float32)]
    meta = np.asarray(inputs["meta_tokens"], dtype=np.float32)
    mcur = [[meta, meta], [meta, meta]]
    PN = ["b_gate", "attn_norm_g", "mlp_norm_g", "a_q_norm_g", "a_k_norm_g", "b_q_norm_g", "b_k_norm_g",
          "b_lambda_q1", "b_lambda_k1", "b_lambda_q2", "b_lambda_k2", "b_subln_g", "c_q_a_norm_g",
          "c_kv_a_norm_g", "c_q_norm_g", "c_k_norm_g"]
    tabs = [_tables(NP, NS, c) for c in range(NCORE)]
    perms = [(_local_order(c, NP)[0], _local_order(c, NS)[0]) for c in range(NCORE)]
    RqP, RqS = NP // 4, NS // 4
    for l in range(DEPTH):
        shared = {}
        shared["params"] = np.ascontiguousarray(np.concatenate([np.asarray(inputs[k], dtype=np.float32)[l:l + 1] for k in PN], axis=1))
        for k in ["w_in", "w_up", "w_down", "w_out", "w_branch_a", "w_branch_b", "w_branch_c", "c_w_q_b", "c_w_kv_b"]:
            shared[k] = np.ascontiguousarray(np.asarray(inputs[k], dtype=np.float32)[l])
        lam_init = 0.8 - 0.6 * math.exp(-0.3 * l)
        shared["lamc"] = np.array([[-lam_init, 1.0 - lam_init]], np.float32)
        shared["ident_in"] = np.eye(128, dtype=np.float32)
        in_maps = []
        for c in range(NCORE):
            b = c // 4
            ropeA, ropeC, qaug, kaug, kaugl = tabs[c]
            d = dict(shared)
            d["x_in"] = np.ascontiguousarray(np.concatenate([xcur[0][b][perms[c][0]], xcur[1][b][perms[c][1]]], axis=0))
            d["meta_in"] = np.ascontiguousarray(np.concatenate([mcur[b][0], mcur[b][1]], axis=0))
            d["ropeA"], d["ropeC"] = ropeA, ropeC
            for s in range(2):
                d["qaug%d" % s] = qaug[s]
                d["kaug%d" % s] = kaug[s]
                d["kaugl%d" % s] = kaugl[s]
            in_maps.append(d)
        res = run_bass_kernel_spmd(nc, in_maps, core_ids=list(range(NCORE)))
        y_prompt = np.zeros((2, NP, D), np.float32)
        y_sample = np.zeros((2, NS, D), np.float32)
        mnew = [[None, None], [None, None]]
        for c in range(NCORE):
            b, r = c // 4, c % 4
            y = res.results[c]["y_out"]
            y_prompt[b, r * RqP:(r + 1) * RqP] = y[0:RqP]
            y_sample[b, r * RqS:(r + 1) * RqS] = y[RqP:RqP + RqS]
            if r == 0:
                mnew[b][0] = np.ascontiguousarray(y[RqP + RqS:RqP + RqS + NMETA])
                mnew[b][1] = np.ascontiguousarray(y[RqP + RqS + NMETA:RqP + RqS + 2 * NMETA])
        xcur = [y_prompt, y_sample]
        mcur = mnew
    return xcur[0], xcur[1]


def kernel(**inputs):
    return run(inputs, inputs["x_prompt"].shape[1], inputs["x_sample"].shape[1])
```

```python
import math
from contextlib import ExitStack
import numpy as np
import concourse.bass as bass
import concourse.mybir as mybir
from concourse.bass_utils import run_bass_kernel_spmd

F32 = mybir.dt.float32
BF16 = mybir.dt.bfloat16
AF = mybir.ActivationFunctionType
ALU = mybir.AluOpType
AX = mybir.AxisListType

NCORE = 8
D = 1024
NMETA = 16
DEPTH = 2
EPS = 1e-6
GRID_W = 64
THETA = 10000.0
NQKV = 2752
NAUG = 24
ENGS = ["tensor", "vector", "scalar", "gpsimd", "sync"]


class _Rec:
    def __getattr__(self, name):
        def f(*a, **kw):
            return (name, a, kw)
        return f


_REC = _Rec()


class DSem:
    def __init__(self, h):
        self.h = h
        self.count = 0


class Prog:
    def __init__(self, nc, es):
        self.nc = nc
        self.es = es
        self.q = {e: [] for e in ENGS}
        self.tl = {e: es.enter_context(nc.semaphore("tl_" + e)) for e in ENGS}
        self.seq = {e: 0 for e in ENGS}
        self.waited = {}
        self.nsem = 0
        self.dsems = []
        self.ep_arrive = es.enter_context(nc.semaphore("ep_arrive"))
        self.ep_go = es.enter_context(nc.semaphore("ep_go"))
        self.nbar = 0

    def dsem(self):
        self.nsem += 1
        d = DSem(self.es.enter_context(self.nc.semaphore("ds%d" % self.nsem)))
        self.dsems.append(d)
        return d

    def reset_barrier(self):
        toks = [(self.tl[e], self.seq[e]) for e in ENGS if self.seq[e] > 0]
        toks += [(d.h, d.count) for d in self.dsems if d.count > 0]
        for e in ENGS:
            self.wait_only(e, toks)
        self.nbar += 1
        for e in ("tensor", "vector", "scalar"):
            self.tl[e] = self.es.enter_context(self.nc.semaphore("tl_%s_%d" % (e, self.nbar)))
            self.seq[e] = 0

    def _waits(self, eng, deps):
        waits = []
        for d in deps:
            if d is None:
                continue
            if isinstance(d, list):
                waits += self._waits(eng, d)
                continue
            sem, val = d
            key = (eng, id(sem))
            if self.waited.get(key, 0) >= val:
                continue
            self.waited[key] = val
            waits.append((sem, val))
        return waits

    def op(self, eng, fn, deps=()):
        waits = self._waits(eng, deps)
        self.seq[eng] += 1
        self.q[eng].append((fn(_REC), waits, self.tl[eng], 1))
        return (self.tl[eng], self.seq[eng])

    def dma(self, eng, out, in_, sem, deps=()):
        waits = self._waits(eng, deps)
        sem.count += 16
        self.q[eng].append((("dma_start", (), dict(out=out, in_=in_)), waits, sem.h, 16))
        return (sem.h, sem.count)

    def wait_only(self, eng, deps):
        waits = self._waits(eng, deps)
        if waits:
            self.q[eng].append((None, waits, None, 0))

    def replay(self, block):
        for eng in ENGS:
            items = self.q[eng]

            def body(e, items=items):
                for fn, waits, sem, inc in items:
                    for (sm, v) in waits:
                        e.wait_ge(sm, v)
                    if fn is not None:
                        name, a, kw = fn
                        ins = getattr(e, name)(*a, **kw)
                        if sem is not None:
                            ins.then_inc(sem, inc)
            getattr(block, eng)(body)


def chunks(total, size):
    return [(o, min(size, total - o)) for o in range(0, total, size)]


def build_program(NP, NS, KBLK=2048, QG=2048, QC=512, debug=False, stop=10 ** 9):
    NSEQ = [NP, NS]
    RQ = [NP // 4, NS // 4]
    KB_ = [min(KBLK, RQ[s]) for s in range(2)]
    QG_ = [min(QG, RQ[s]) for s in range(2)]
    TOK = NP + NS
    OFF = [0, NP]
    NMT = 2 * NMETA
    TALL = TOK + NMT
    TOKQ = RQ[0] + RQ[1]
    nc = bass.Bass("TRN2", target_bir_lowering=False)
    es = ExitStack()

    def din(name, shape, dt=F32):
        return nc.dram_tensor(name, list(shape), dt, kind="ExternalInput").ap()

    def dint(name, shape, dt):
        return nc.dram_tensor(name, list(shape), dt, kind=("ExternalOutput" if debug else "Internal")).ap()

    x_in = din("x_in", [TOK, D])
    meta_in = din("meta_in", [NMT, D])
    PNAMES = [("b_gate", 3072), ("attn_norm_g", 1024), ("mlp_norm_g", 1024), ("a_q_norm_g", 64), ("a_k_norm_g", 64),
              ("b_q_norm_g", 64), ("b_k_norm_g", 64), ("b_lambda_q1", 64), ("b_lambda_k1", 64), ("b_lambda_q2", 64),
              ("b_lambda_k2", 64), ("b_subln_g", 128), ("c_q_a_norm_g", 256), ("c_kv_a_norm_g", 128),
              ("c_q_norm_g", 192), ("c_k_norm_g", 192)]
    PTOT = sum(n for _, n in PNAMES)
    LD = 1
    params = din("params", [LD, PTOT])
    lamc = din("lamc", [1, 2])
    pv = {}
    o_ = 0
    for nm_, n_ in PNAMES:
        pv[nm_] = params[:, o_:o_ + n_]
        o_ += n_
    b_gate, attn_g, mlp_g = pv["b_gate"], pv["attn_norm_g"], pv["mlp_norm_g"]
    g64 = pv
    b_sub, c_qa_g, c_kva_g, c_q_g, c_k_g = pv["b_subln_g"], pv["c_q_a_norm_g"], pv["c_kv_a_norm_g"], pv["c_q_norm_g"], pv["c_k_norm_g"]
    w_in_d = din("w_in", [LD * D, 5824])
    w_up_d = din("w_up", [LD * D, 4096])
    w_down_d = din("w_down", [LD * 4096, D])
    w_out_d = din("w_out", [LD * D, D])
    w_br_d = [din("w_branch_" + n, [LD * 512, D]) for n in "abc"]
    c_wqb_d = din("c_w_q_b", [LD * 256, 768])
    c_wkvb_d = din("c_w_kv_b", [LD * 128, 1024])
    w_in = [w_in_d[l * D:(l + 1) * D, :] for l in range(LD)]
    w_up = [w_up_d[l * D:(l + 1) * D, :] for l in range(LD)]
    w_down = [w_down_d[l * 4096:(l + 1) * 4096, :] for l in range(LD)]
    w_out = [w_out_d[l * D:(l + 1) * D, :] for l in range(LD)]
    w_br = [[w_br_d[i][l * 512:(l + 1) * 512, :] for l in range(LD)] for i in range(3)]
    c_wqb = [c_wqb_d[l * 256:(l + 1) * 256, :] for l in range(LD)]
    c_wkvb = [c_wkvb_d[l * 128:(l + 1) * 128, :] for l in range(LD)]
    ident_in = din("ident_in", [128, 128])
    ropeA = din("ropeA", [TALL, 128])
    ropeC = din("ropeC", [TALL, 128])
    qaug_f = [din("qaug%d" % s, [4 * NAUG, NSEQ[s] + 2 * NMETA]) for s in range(2)]
    kaug_f = [din("kaug%d" % s, [4 * NAUG, NSEQ[s] + NMETA]) for s in range(2)]
    kaugl_f = [din("kaugl%d" % s, [8 * NAUG, RQ[s]]) for s in range(2)]
    qaug_b = [dint("qaugb%d" % s, [4 * NAUG, NSEQ[s] + 2 * NMETA], BF16) for s in range(2)]
    kaug_b = [dint("kaugb%d" % s, [4 * NAUG, NSEQ[s] + NMETA], BF16) for s in range(2)]
    kaugl_b = [dint("kauglb%d" % s, [8 * NAUG, RQ[s]], BF16) for s in range(2)]
    qaug = [qaug_b[s].rearrange("(h r) t -> h r t", r=NAUG) for s in range(2)]
    kaug = [kaug_b[s].rearrange("(h r) t -> h r t", r=NAUG) for s in range(2)]
    kaugl = [kaugl_b[s].rearrange("(h g r) t -> h g r t", g=2, r=NAUG) for s in range(2)]
    y_out = nc.dram_tensor("y_out", [TOKQ + NMT, D], F32, kind="ExternalOutput").ap()

    xmid = dint("xmid", [TALL, D], F32)
    x1buf = dint("x1buf", [TALL, D], F32)
    kvl = [dint("kvl%d" % s, [2560, NSEQ[s]], BF16) for s in range(2)]
    kvm = [dint("kvm%d" % s, [2560, NMETA], BF16) for s in range(2)]
    qloc = [dint("qloc%d" % s, [14 * 128, NSEQ[s] + NMETA], BF16) for s in range(2)]
    oA = [dint("oA%d" % s, [512, NSEQ[s] + NMETA], BF16) for s in range(2)]
    oC = [dint("oC%d" % s, [512, NSEQ[s] + NMETA], BF16) for s in range(2)]
    oB = [dint("oB%d" % s, [1024, NSEQ[s] + NMETA], F32) for s in range(2)]

    def sb(name, shape, dt=F32):
        return es.enter_context(nc.sbuf_tensor(name, list(shape), dt))

    P = Prog(nc, es)
    RMAX = max(KB_ + QG_)
    TQM = RMAX + 2 * NMETA
    ident = sb("ident", [128, 128], BF16)
    identf = sb("identf", [128, 128])
    ones_bf = sb("ones_bf", [128, 1], BF16)
    ones_f = sb("ones_f", [128, 128])
    mean_f = sb("mean_f", [128, 128])
    WBIG = sb("wbig", [128, 64 * 1024], BF16)
    wstg = [sb("wstg%d" % i, [128, 1024]) for i in range(2)]
    WORK = sb("work", [128, 14336], F32)
    gT = sb("gT", [128, 8])
    gT2 = sb("gT2", [128, 8])
    bgT = sb("bgT", [128, 24])
    gsub = sb("gsub", [128, 1])
    lamt = sb("lamt", [128, 8])
    lrow = sb("lrow", [1, 4 * 64 + 8])
    gains = WBIG[:, 57344:57344 + 7168].bitcast(F32)
    pst = es.enter_context(nc.psum_tensor("pst", [128, 1024], BF16))
    ps = es.enter_context(nc.psum_tensor("ps", [128, 7, 512], F32))
    pstf = pst[:, :].bitcast(F32)

    dsem_w = [P.dsem(), P.dsem()]
    dsem_x = [P.dsem(), P.dsem()]
    dsem_misc = P.dsem()
    dsem_st = P.dsem()
    dsem_kv = [P.dsem() for _ in range(3)]
    dsem_q = [P.dsem(), P.dsem()]
    dsem_o = P.dsem()
    dsem_tab = [P.dsem(), P.dsem()]
    state = {"wstg_free": [None, None], "wn": 0}

    def work(off, shape, dt=F32):
        n = int(np.prod(shape[1:]))
        if dt == F32:
            v = WORK[:, off:off + n]
        else:
            v = WORK[:, off:off + (n + 1) // 2].bitcast(BF16)[:, 0:n]
        if len(shape) == 3:
            v = v.rearrange("p (a b) -> p a b", b=shape[2])
        elif len(shape) == 4:
            v = v.rearrange("p (a b c) -> p a b c", b=shape[2], c=shape[3])
        return v[0:shape[0]]

    def wview(off, KC, C):
        return WBIG[:, off:off + KC * C].rearrange("p (k c) -> p k c", c=C)

    def load_w(dst, src, deps=()):
        KC, C = dst.shape[1], dst.shape[2]
        toks = {}
        for kc in range(KC):
            for (c0, w) in chunks(C, 1024):
                i = state["wn"] % 2
                state["wn"] += 1
                t = P.dma("sync", wstg[i][:, 0:w], src[kc * 128:(kc + 1) * 128, c0:c0 + w], dsem_w[i],
                          deps=[state["wstg_free"][i]])
                eng = "gpsimd" if (state["wn"] % 2) else "vector"
                ct = P.op(eng, lambda e, o=dst[:, kc, c0:c0 + w], s=wstg[i][:, 0:w]: e.tensor_copy(out=o, in_=s),
                          deps=[t] + list(deps))
                state["wstg_free"][i] = ct
                toks[eng] = ct
        return list(toks.values())

    def bcast_row(src_row, n):
        return bass.AP(src_row.tensor, src_row.offset, [[0, 128], [1, n]])

    t0 = P.dma("sync", identf[:], ident_in, dsem_misc)
    t_id = P.op("vector", lambda e: e.tensor_copy(out=ident[:], in_=identf[:]), deps=[t0])
    P.op("vector", lambda e: e.memset(ones_bf[:], 1.0))
    P.op("vector", lambda e: e.memset(ones_f[:], 1.0))
    t_const = P.op("vector", lambda e: e.memset(mean_f[:], 1.0 / 128.0))
    cstg = [work(0, [128, 1024], BF16), work(512, [128, 1024], BF16)]
    cfree = [None, None]
    cn = 0
    import os
    tabs = list(zip(qaug_f, qaug_b)) + list(zip(kaug_f, kaug_b)) + list(zip(kaugl_f, kaugl_b))
    if os.environ.get('K_SKIP_TAB'):
        tabs = tabs[:int(os.environ['K_SKIP_TAB']) - 1]
    for (srcT, dstT) in tabs:
        nr, ncol = srcT.shape
        for (r0, rh) in chunks(nr, 128):
            for (c0, w) in chunks(ncol, 1024):
                i = cn % 2
                cn += 1
                t = P.dma("sync", wstg[i][0:rh, 0:w], srcT[r0:r0 + rh, c0:c0 + w], dsem_w[i], deps=[state["wstg_free"][i]])
                ct = P.op("vector", lambda e, i=i, rh=rh, w=w: e.tensor_copy(out=cstg[i][0:rh, 0:w], in_=wstg[i][0:rh, 0:w]), deps=[t, cfree[i]])
                state["wstg_free"][i] = ct
                cfree[i] = P.dma("gpsimd", dstT[r0:r0 + rh, c0:c0 + w], cstg[i][0:rh, 0:w], dsem_tab[i], deps=[ct])
    phase_done = [t_const, t_id, cfree[0], cfree[1]]
    pcount = {"n": 0}

    def skip_phase():
        pcount["n"] += 1
        return pcount["n"] > stop

    def barrier(tokens):
        for e in ENGS:
            P.wait_only(e, tokens)

    def rmsnorm_T(xt, g_t, hT, tcol, ntile_deps, scratch_off):
        junk = work(scratch_off, [128, 1024])
        xn = work(scratch_off + 1024, [128, 1024], BF16)
        st = work(scratch_off + 1536, [128, 4])
        a = P.op("scalar", lambda e: e.activation(out=junk, in_=xt, func=AF.Square, accum_out=st[:, 0:1]),
                 deps=ntile_deps)
        b = P.op("scalar", lambda e: e.activation(out=st[:, 1:2], in_=st[:, 0:1], func=AF.Sqrt,
                                                  scale=1.0 / D, bias=EPS), deps=[a])
        c = P.op("vector", lambda e: e.reciprocal(out=st[:, 2:3], in_=st[:, 1:2]), deps=[b] + list(ntile_deps))
        d = P.op("vector", lambda e: e.tensor_scalar(out=xn, in0=xt, scalar1=st[:, 2:3], scalar2=None,
                                                     op0=ALU.mult), deps=[c])
        last = None
        for k in range(8):
            last = P.op("tensor", lambda e, k=k: e.transpose(out=pst[:, k * 128:(k + 1) * 128],
                                                             in_=xn[:, k * 128:(k + 1) * 128], identity=ident[:]),
                        deps=[d] + list(ntile_deps))
        f = P.op("vector", lambda e: e.tensor_tensor(
            out=hT[:, :, tcol:tcol + 128], in0=pst[:, :].rearrange("p (k t) -> p k t", t=128),
            in1=g_t[:, :].unsqueeze(2).to_broadcast([128, 8, 128]), op=ALU.mult), deps=[last])
        return f

    def phase_p1(l, xsrc_reg, xsrc_meta):
        nonlocal phase_done
        if skip_phase():
            return
        P.reset_barrier()
        state["wstg_free"] = [None, None]
        pd = []
        Wq = wview(0, 8, NQKV)
        Wqb = wview(8 * NQKV, 2, 768)
        Wkvb = wview(8 * NQKV + 2 * 768, 1, 1024)
        wt = load_w(Wq, w_in[l][:, 0:NQKV], deps=pd)
        wt += load_w(Wqb, c_wqb[l], deps=pd)
        wt += load_w(Wkvb, c_wkvb[l], deps=pd)
        gt = []
        gt.append(P.dma("sync", gT[:], attn_g[l].rearrange("(k p) -> p k", p=128), dsem_misc, deps=pd))
        col = 0
        for nm, rep in [("a_q_norm_g", 8), ("a_k_norm_g", 2), ("b_q_norm_g", 8), ("b_k_norm_g", 8)]:
            for r in range(rep):
                gt.append(P.dma("sync", gains[:, col:col + 64], bcast_row(g64[nm][l:l + 1, :], 64), dsem_misc, deps=pd))
                col += 64
        gt.append(P.dma("sync", gains[:, col:col + 256], bcast_row(c_qa_g[l:l + 1, :], 256), dsem_misc, deps=pd)); col += 256
        gt.append(P.dma("sync", gains[:, col:col + 128], bcast_row(c_kva_g[l:l + 1, :], 128), dsem_misc, deps=pd)); col += 128
        for r in range(4):
            gt.append(P.dma("sync", gains[:, col:col + 192], bcast_row(c_q_g[l:l + 1, :], 192), dsem_misc, deps=pd)); col += 192
        for r in range(4):
            gt.append(P.dma("sync", gains[:, col:col + 192], bcast_row(c_k_g[l:l + 1, :], 192), dsem_misc, deps=pd)); col += 192
        setup = wt + [gt[-1]]
        G1 = gains[:, 0:640]
        G2 = gains[:, 640:1664]
        GCQA = gains[:, 1664:1920]
        GCKVA = gains[:, 1920:2048]

        xt = [work(0, [128, 1024]), work(1024, [128, 1024])]
        hT = work(2048, [128, 8, 128], BF16)
        z = work(2560, [128, NQKV])
        tmp = work(5312, [128, NQKV])
        zb = work(8064, [128, NQKV], BF16)
        ssq = work(9440, [128, 40])
        rst = work(9480, [128, 40])
        rope = work(9520, [128, 256])
        cqn = work(9776, [128, 384], BF16)
        cT = work(9968, [128, 3, 128], BF16)
        zcb = work(11696, [128, 2, 4, 128], BF16)
        zrb = work(12208, [128, 2, 256], BF16)
        vst = work(12464, [128, 1152], BF16)
        kst = WBIG[:, 40 * 1024:40 * 1024 + 11 * 512].rearrange("p (b t) -> p b t", t=512)
        qst = WBIG[:, 40 * 1024 + 11 * 512:40 * 1024 + 25 * 512].rearrange("p (b t) -> p b t", t=512)

        tile_list = []
        for s in range(2):
            nt = NSEQ[s] // 128
            gsz = min(4, nt)
            for t in range(nt):
                tile_list.append((s, OFF[s] + t * 128, 128, t % gsz, (t % gsz) == gsz - 1, t * 128, gsz))
        tile_list.append((None, 0, NMT, 0, True, 0, 1))

        prev_tile_done = {"tok": None}
        xfree = [None, None]
        st_prev = {"k": None, "q": None, "v": None}
        for ti, (s, row0, ntok, gcol, glast, tok0, gsz) in enumerate(tile_list):
            i = ti % 2
            X = xt[i]
            ptd = prev_tile_done["tok"]
            ld = []
            if s is None:
                mz = P.op("gpsimd", lambda e, X=X: e.memset(X, 0.0), deps=[xfree[i]] + pd)
                ld.append(P.dma("sync", X[0:NMT, :], xsrc_meta, dsem_x[i], deps=[mz]))
                ld.append(P.dma("sync", rope[0:NMT, 0:128], ropeA[TOK:TALL, :], dsem_x[i], deps=[ptd] + pd))
                ld.append(P.dma("sync", rope[0:NMT, 128:256], ropeC[TOK:TALL, :], dsem_x[i], deps=[ptd] + pd))
            else:
                ld.append(P.dma("sync", X, xsrc_reg[row0:row0 + 128, :], dsem_x[i], deps=[xfree[i]] + pd))
                ld.append(P.dma("sync", rope[:, 0:128], ropeA[row0:row0 + 128, :], dsem_x[i], deps=[ptd] + pd))
                ld.append(P.dma("sync", rope[:, 128:256], ropeC[row0:row0 + 128, :], dsem_x[i], deps=[ptd] + pd))
            ldt = ld[-1]
            f = rmsnorm_T(X, gT, hT, 0, [ldt, ptd] + setup, 5312)
            xfree[i] = f
            isq = (s is None) or (tok0 < RQ[s])
            hA0 = 0 if isq else 8
            hB0 = 0 if isq else 8
            hC0 = 0 if isq else 4
            cA0 = 64 * hA0
            cB0 = 768 + 64 * hB0
            nA, nB, nC = 10 - hA0, 16 - hB0, 8 - hC0
            if isq:
                colch = [(0, 512), (512, 256), (768, 512), (1280, 512), (1792, 512), (2304, 448)]
            else:
                colch = [(512, 256), (1280, 512), (1792, 512), (2560, 192)]
            ev = []
            for bi, (c0, w) in enumerate(colch):
                mm = None
                for k in range(8):
                    mm = P.op("tensor", lambda e, bi=bi, c0=c0, w=w, k=k: e.matmul(
                        ps[:, bi, 0:w], lhsT=hT[:, k, :], rhs=Wq[:, k, c0:c0 + w], start=(k == 0), stop=(k == 7)),
                        deps=[f, ptd])
                ev.append(P.op("scalar", lambda e, bi=bi, c0=c0, w=w: e.activation(
                    out=z[:, c0:c0 + w], in_=ps[:, bi, 0:w], func=AF.Copy), deps=[mm, ptd]))
            evl = ev[-1]
            if isq:
                a1 = P.op("gpsimd", lambda e: e.tensor_tensor(out=tmp[:, :], in0=z[:, :], in1=z[:, :], op=ALU.mult), deps=[evl, f])
            else:
                a1 = [P.op("gpsimd", lambda e, lo=lo, hi=hi: e.tensor_tensor(out=tmp[:, lo:hi], in0=z[:, lo:hi], in1=z[:, lo:hi], op=ALU.mult), deps=[evl, f])
                      for (lo, hi) in ((cA0, 640), (cB0, 1792), (2560, 2688))]
            r1 = P.op("vector", lambda e: e.tensor_reduce(out=ssq[:, hA0:10], in_=tmp[:, cA0:640].rearrange("p (h d) -> p h d", d=64), axis=AX.X, op=ALU.add), deps=[a1, ptd])
            r2 = P.op("vector", lambda e: e.tensor_reduce(out=ssq[:, 10 + hB0:26], in_=tmp[:, cB0:1792].rearrange("p (h d) -> p h d", d=64), axis=AX.X, op=ALU.add), deps=[a1])
            r3 = P.op("vector", lambda e: e.tensor_reduce(out=ssq[:, 26:27], in_=tmp[:, 2304:2560], axis=AX.X, op=ALU.add), deps=[a1]) if isq else None
            r4 = P.op("vector", lambda e: e.tensor_reduce(out=ssq[:, 27:28], in_=tmp[:, 2560:2688], axis=AX.X, op=ALU.add), deps=[a1])
            if isq:
                s1 = P.op("scalar", lambda e: e.activation(out=rst[:, 0:26], in_=ssq[:, 0:26], func=AF.Sqrt, scale=1.0 / 64, bias=EPS), deps=[r1, r2, ptd])
                s2 = P.op("scalar", lambda e: e.activation(out=rst[:, 26:27], in_=ssq[:, 26:27], func=AF.Sqrt, scale=1.0 / 256, bias=EPS), deps=[r3])
            else:
                s1 = P.op("scalar", lambda e: e.activation(out=rst[:, 8:10], in_=ssq[:, 8:10], func=AF.Sqrt, scale=1.0 / 64, bias=EPS), deps=[r1, r2, ptd])
                s2 = P.op("scalar", lambda e: e.activation(out=rst[:, 18:26], in_=ssq[:, 18:26], func=AF.Sqrt, scale=1.0 / 64, bias=EPS), deps=[r1, r2, ptd])
            s3 = P.op("scalar", lambda e: e.activation(out=rst[:, 27:28], in_=ssq[:, 27:28], func=AF.Sqrt, scale=1.0 / 128, bias=EPS), deps=[r4])
            if isq:
                rc = P.op("vector", lambda e: e.reciprocal(out=rst[:, 0:28], in_=rst[:, 0:28]), deps=[s1, s2, s3])
            else:
                rc = [P.op("vector", lambda e, lo=lo, hi=hi: e.reciprocal(out=rst[:, lo:hi], in_=rst[:, lo:hi]), deps=[s1, s2, s3])
                      for (lo, hi) in ((8, 10), (18, 26), (27, 28))]
            n1 = P.op("vector", lambda e: e.tensor_tensor(out=z[:, cA0:640].rearrange("p (h d) -> p h d", d=64), in0=z[:, cA0:640].rearrange("p (h d) -> p h d", d=64), in1=rst[:, hA0:10].unsqueeze(2).to_broadcast([128, nA, 64]), op=ALU.mult), deps=[rc, a1])
            n2 = P.op("vector", lambda e: e.tensor_tensor(out=z[:, cB0:1792].rearrange("p (h d) -> p h d", d=64), in0=z[:, cB0:1792].rearrange("p (h d) -> p h d", d=64), in1=rst[:, 10 + hB0:26].unsqueeze(2).to_broadcast([128, nB, 64]), op=ALU.mult), deps=[rc, a1])
            n3 = P.op("vector", lambda e: e.tensor_scalar(out=z[:, 2304:2560], in0=z[:, 2304:2560], scalar1=rst[:, 26:27], scalar2=None, op0=ALU.mult), deps=[rc, a1]) if isq else None
            n4 = P.op("vector", lambda e: e.tensor_scalar(out=z[:, 2560:2688], in0=z[:, 2560:2688], scalar1=rst[:, 27:28], scalar2=None, op0=ALU.mult), deps=[rc, a1])
            g1 = P.op("gpsimd", lambda e: e.tensor_tensor(out=z[:, cA0:640], in0=z[:, cA0:640], in1=G1[:, cA0:640], op=ALU.mult), deps=[n1])
            g2 = P.op("gpsimd", lambda e: e.tensor_tensor(out=zb[:, cB0:1792], in0=z[:, cB0:1792], in1=G2[:, cB0 - 768:1024], op=ALU.mult), deps=[n2, ptd])
            g3 = P.op("gpsimd", lambda e: e.tensor_tensor(out=cqn[:, 0:256], in0=z[:, 2304:2560], in1=GCQA, op=ALU.mult), deps=[n3, ptd]) if isq else None
            g4 = P.op("gpsimd", lambda e: e.tensor_tensor(out=cqn[:, 256:384], in0=z[:, 2560:2688], in1=GCKVA, op=ALU.mult), deps=[n4, ptd])
            zv = z[:, cA0:640].rearrange("p (h r a d) -> p h r a d", r=2, a=2, d=16)
            tv = tmp[:, cA0:640].rearrange("p (h r a d) -> p h r a d", r=2, a=2, d=16)
            sinv = rope[:, 64:128].rearrange("p (r a d) -> p r a d", r=2, a=2)
            ra = P.op("vector", lambda e: e.tensor_tensor(out=tmp[:, 768 + cA0:1408].rearrange("p (h d) -> p h d", d=64), in0=z[:, cA0:640].rearrange("p (h d) -> p h d", d=64), in1=rope[:, 0:64].unsqueeze(1).to_broadcast([128, nA, 64]), op=ALU.mult), deps=[g1, ldt, r2, r3, r4])
            rb = []
            for hh in range(2):
                rb.append(P.op("vector", lambda e, hh=hh: e.tensor_tensor(
                    out=tv[:, :, :, hh, :], in0=zv[:, :, :, 1 - hh, :],
                    in1=sinv[:, :, hh, :].unsqueeze(1).to_broadcast([128, nA, 2, 16]),
                    op=ALU.mult), deps=[g1, ldt, r1]))
            rd = P.op("vector", lambda e: e.tensor_tensor(out=zb[:, cA0:640], in0=tmp[:, 768 + cA0:1408], in1=tmp[:, cA0:640], op=ALU.add), deps=[ra] + rb + [ptd])
            v1 = P.op("scalar", lambda e: e.activation(out=vst[:, 0:128], in_=z[:, 640:768], func=AF.Copy), deps=[evl, st_prev["v"]])
            v2 = P.op("scalar", lambda e: e.activation(out=vst[:, 128:640], in_=z[:, 1792:2304], func=AF.Copy), deps=[evl])
            tr = None
            k0c = 0 if isq else 2
            for k in range(k0c, 3):
                tr = P.op("tensor", lambda e, k=k: e.transpose(out=pst[:, k * 128:(k + 1) * 128], in_=cqn[:, k * 128:(k + 1) * 128], identity=ident[:]), deps=[g3, g4, f])
            ctc = P.op("vector", lambda e: e.tensor_copy(out=cT[:, k0c:3, :], in_=pst[:, k0c * 128:384].rearrange("p (k t) -> p k t", t=128)), deps=[tr, ptd])
            mm = None
            for j in range(2 if isq else 0):
                for k in range(2):
                    mm = P.op("tensor", lambda e, j=j, k=k: e.matmul(ps[:, j, 0:384], lhsT=cT[:, k, :], rhs=Wqb[:, k, j * 384:(j + 1) * 384], start=(k == 0), stop=(k == 1)), deps=[ctc, evl])
            for j in range(2):
                mm = P.op("tensor", lambda e, j=j: e.matmul(ps[:, 2 + j, :], lhsT=cT[:, 2, :], rhs=Wkvb[:, 0, j * 512:(j + 1) * 512], start=True, stop=True), deps=[ctc, evl])
            zqk = WORK[:, 10160:10160 + 1536]
            zq = zqk[:, 0:768]
            zk = zqk[:, 768:1536]
            e1 = P.op("scalar", lambda e: e.activation(out=zq.rearrange("p (j c) -> p j c", c=384), in_=ps[:, 0:2, 0:384], func=AF.Copy), deps=[mm, ptd]) if isq else None
            kvv = ps[:, 2:4, :].rearrange("p j (h c) -> p (j h) c", c=256)
            e2 = P.op("scalar", lambda e: e.activation(out=zk.rearrange("p (h c) -> p h c", c=192)[:, :, 0:128], in_=kvv[:, :, 0:128], func=AF.Copy), deps=[mm, ptd])
            e3 = P.op("scalar", lambda e: e.activation(out=vst[:, 640:1152].rearrange("p (h c) -> p h c", c=128), in_=kvv[:, :, 128:256], func=AF.Copy), deps=[mm])
            e4 = P.op("gpsimd", lambda e: e.tensor_copy(out=zk.rearrange("p (h c) -> p h c", c=192)[:, :, 128:192], in_=z[:, 2688:2752].unsqueeze(1).to_broadcast([128, 4, 64])), deps=[evl, ptd, g4])
            zc0 = 192 * hC0
            zqs = zqk[:, zc0:1536]
            a2 = P.op("gpsimd", lambda e: e.tensor_tensor(out=tmp[:, zc0:1536], in0=zqs, in1=zqs, op=ALU.mult), deps=[e1, e2, e4, rd])
            r5 = P.op("vector", lambda e: e.tensor_reduce(out=ssq[:, 28 + hC0:36], in_=tmp[:, zc0:1536].rearrange("p (h d) -> p h d", d=192), axis=AX.X, op=ALU.add), deps=[a2])
            s5 = P.op("scalar", lambda e: e.activation(out=rst[:, 28 + hC0:36], in_=ssq[:, 28 + hC0:36], func=AF.Sqrt, scale=1.0 / 192, bias=EPS), deps=[r5])
            rc5 = P.op("vector", lambda e: e.reciprocal(out=rst[:, 28 + hC0:36], in_=rst[:, 28 + hC0:36]), deps=[s5])
            n5 = P.op("vector", lambda e: e.tensor_tensor(out=zqs.rearrange("p (h d) -> p h d", d=192), in0=zqs.rearrange("p (h d) -> p h d", d=192), in1=rst[:, 28 + hC0:36].unsqueeze(2).to_broadcast([128, nC, 192]), op=ALU.mult), deps=[rc5, a2])
            g5 = P.op("gpsimd", lambda e: e.tensor_tensor(out=zqs, in0=zqs, in1=gains[:, 2048 + zc0:3584], op=ALU.mult), deps=[n5])
            zqk4 = zqk.rearrange("p (h d) -> p h d", d=192)[:, hC0:8, :]
            c1 = P.op("scalar", lambda e: e.activation(out=zcb[:, :, :, :].rearrange("p a h d -> p (a h) d")[:, hC0:8, :], in_=zqk4[:, :, 0:128], func=AF.Copy), deps=[g5, ptd])
            rp = zqk4[:, :, 128:192].rearrange("p h (a d) -> p h a d", a=2)
            t64 = tmp[:, 0:512].rearrange("p (h d) -> p h d", d=64)[:, hC0:8, :]
            u64 = tmp[:, 512:1024].rearrange("p (h a d) -> p h a d", a=2, d=32)[:, hC0:8, :, :]
            sinc = rope[:, 192:256].rearrange("p (a d) -> p a d", a=2)
            qa_ = P.op("vector", lambda e: e.tensor_tensor(out=t64, in0=zqk4[:, :, 128:192], in1=rope[:, 128:192].unsqueeze(1).to_broadcast([128, nC, 64]), op=ALU.mult), deps=[g5, ldt, r5])
            qb_ = []
            for hh in range(2):
                qb_.append(P.op("vector", lambda e, hh=hh: e.tensor_tensor(out=u64[:, :, hh, :], in0=rp[:, :, 1 - hh, :], in1=sinc[:, hh, :].unsqueeze(1).to_broadcast([128, nC, 32]), op=ALU.mult), deps=[g5, ldt, r5]))
            qd_ = P.op("vector", lambda e: e.tensor_tensor(out=zrb[:, :, :].rearrange("p a (h d) -> p (a h) d", d=64)[:, hC0:8, :], in0=t64, in1=tmp[:, 512:1024].rearrange("p (h d) -> p h d", d=64)[:, hC0:8, :], op=ALU.add), deps=[qa_] + qb_ + [ptd])
            srcs = [("k", 0, zb[:, 512:640])]
            for b in range(4):
                srcs.append(("k", 1 + b, zb[:, 1280 + b * 128:1280 + (b + 1) * 128]))
            for b in range(4):
                srcs.append(("k", 5 + b, zcb[:, 1, b, :]))
            for b in range(2):
                srcs.append(("k", 9 + b, zrb[:, 1, b * 128:(b + 1) * 128]))
            for b in range(4):
                srcs.append(("q", b, zb[:, b * 128:(b + 1) * 128]))
            for b in range(4):
                srcs.append(("q", 4 + b, zb[:, 768 + b * 128:768 + (b + 1) * 128]))
            for b in range(4):
                srcs.append(("q", 8 + b, zcb[:, 0, b, :]))
            for b in range(2):
                srcs.append(("q", 12 + b, zrb[:, 0, b * 128:(b + 1) * 128]))
            if not isq:
                srcs = [x_ for x_ in srcs if x_[0] == "k"]
            alld = [rd, g2, c1, qd_]
            cp_last = ctc
            cps = []
            for r0 in range(0, len(srcs), 8):
                grp = srcs[r0:r0 + 8]
                tr = None
                for j, (dst, blk, src) in enumerate(grp):
                    tr = P.op("tensor", lambda e, j=j, src=src: e.transpose(out=pst[:, j * 128:(j + 1) * 128], in_=src, identity=ident[:]), deps=alld + [cp_last])
                j = 0
                while j < len(grp):
                    j2 = j
                    while j2 + 1 < len(grp) and grp[j2 + 1][0] == grp[j][0] and grp[j2 + 1][1] == grp[j2][1] + 1:
                        j2 += 1
                    dstt = kst if grp[j][0] == "k" else qst
                    b0 = grp[j][1]
                    nb = j2 - j + 1
                    cp_last = P.op("vector", lambda e, dstt=dstt, b0=b0, nb=nb, j=j: e.tensor_copy(
                        out=dstt[:, b0:b0 + nb, gcol * 128:(gcol + 1) * 128],
                        in_=pst[:, j * 128:(j + nb) * 128].rearrange("p (b t) -> p b t", t=128)),
                        deps=[tr, st_prev["k"], st_prev["q"]])
                    cps.append(cp_last)
                    j = j2 + 1
            prev_tile_done["tok"] = [cp_last, qd_, rd, g2, c1, v1, v2, e3]
            if s is None:
                for s4 in range(2):
                    sk = P.dma("gpsimd", kvm[s4][0:1408, :].rearrange("(b p) t -> p b t", p=128), kst[:, :, s4 * 16:(s4 + 1) * 16], dsem_st, deps=cps)
                    sq_ = P.dma("gpsimd", qloc[s4][:, NSEQ[s4]:NSEQ[s4] + NMETA].rearrange("(b p) t -> p b t", p=128), qst[:, :, s4 * 16:(s4 + 1) * 16], dsem_st, deps=cps)
                    sv = P.dma("gpsimd", bass.AP(kvm[s4].tensor, 1408 * NMETA, [[1152, NMETA], [1, 1152]]), vst[s4 * 16:(s4 + 1) * 16, :], dsem_st, deps=[v1, v2, e3])
                st_prev["k"], st_prev["q"], st_prev["v"] = sk, sq_, sv
            else:
                sv = P.dma("gpsimd", bass.AP(kvl[s].tensor, 1408 * NSEQ[s] + tok0 * 1152, [[1152, 128], [1, 1152]]), vst[:, :], dsem_st, deps=[v1, v2, e3])
                st_prev["v"] = sv
                if glast:
                    gw = gsz * 128
                    g0 = tok0 - (gsz - 1) * 128
                    sk = P.dma("gpsimd", kvl[s][0:1408, g0:g0 + gw].rearrange("(b p) t -> p b t", p=128), kst[:, :, 0:gw], dsem_st, deps=cps)
                    if g0 < RQ[s]:
                        sq_ = P.dma("gpsimd", qloc[s][:, g0:g0 + gw].rearrange("(b p) t -> p b t", p=128), qst[:, :, 0:gw], dsem_st, deps=cps)
                        st_prev["q"] = sq_
                    st_prev["k"] = sk
        phase_done = [prev_tile_done["tok"], (dsem_st.h, dsem_st.count)]

    def phase_att(l, s, qcol0, qn, qquarter, is_meta):
        nonlocal phase_done
        if skip_phase():
            return
        P.reset_barrier()
        state["wstg_free"] = [None, None]
        pd = []
        N = NSEQ[s]
        Rq = RQ[s]
        KBs = KB_[s]
        if is_meta:
            qchunks = [(0, NMETA)]
            TQ = NMETA
        else:
            qchunks = chunks(qn, QC)
            TQ = qn
        NCH = len(qchunks)
        kbuf = [WBIG[:, (i * 2) * RMAX:(i * 2 + 2) * RMAX].rearrange("p (m t) -> p m t", t=RMAX) for i in range(3)]
        vo = 6 * RMAX
        VT = max(1, RMAX // 128)
        vbuf = [WBIG[:, vo + i * VT * 130: vo + (i + 1) * VT * 130].rearrange("p (t c) -> p t c", c=130) for i in range(3)]
        qo = vo + 3 * VT * 130
        qbuf = [WBIG[:, qo + i * 4 * TQM: qo + (i + 1) * 4 * TQM].rearrange("p (m t) -> p m t", t=TQM) for i in range(2)]
        po = qo + 8 * TQM
        pbuf = WBIG[:, po:po + 4 * 512].rearrange("p (i t) -> p i t", t=512)
        assert po + 2048 + 1536 <= 64 * 1024
        oacc = work(0, [128, 4, NCH, 512])
        sacc = work(2 * NCH * 512, [1, 2, NCH, 512])
        assert 4 * NCH * 512 <= 12800
        tmpA = WORK[:, 12800:13312]
        tmpB = WORK[:, 13312:13824]
        rcp = WORK[:, 13824:14336]
        ostA = WBIG[:, po + 2048:po + 2048 + 512]
        ostB = WBIG[:, po + 2560:po + 2560 + 1024].bitcast(F32)

        t_ones = [P.op("gpsimd", lambda e, i=i: e.memset(vbuf[i][:, :, 128:129], 1.0), deps=pd) for i in range(3)]
        passes = [("A", 0), ("A", 1)] + [("B", h) for h in range(4)] + [("C", h) for h in range(4)]
        S = {"step": 0, "unit": 0, "pv_tok": {}, "blk": 0, "exp_tok": {}, "ep_free": None, "ost_free": [None, None],
             "bank6": None, "last_evac": {}, "evac_by_unit": {}, "pending": [], "last_pv": None}
        slot_free = [None, None, None]
        qb_free = [None, None]
        kblocks = [(c0, w, c0 // Rq, "g") for (c0, w) in chunks(N, KBs)] + [(N, NMETA, 4, "m")]

        for pi, (kind, idx) in enumerate(passes):
            qi = pi % 2
            Q = qbuf[qi]
            nm = {"A": 4, "B": 2, "C": 1}[kind]
            qz = P.op("gpsimd", lambda e, Q=Q: e.memset(Q[:, :, :], 0.0), deps=[qb_free[qi]] + pd)
            qsrc0 = N if is_meta else qcol0
            qt = []
            if kind == "A":
                for m in range(4):
                    h = idx * 4 + m
                    qt.append(P.dma("sync", Q[idx * 64:(idx + 1) * 64, m, 0:TQ], qloc[s][h * 64:(h + 1) * 64, qsrc0:qsrc0 + TQ], dsem_q[qi], deps=[qz]))
            elif kind == "B":
                for m in range(2):
                    mp = idx * 2 + m
                    qt.append(P.dma("sync", Q[0:64, m, 0:TQ], qloc[s][512 + mp * 64:512 + (mp + 1) * 64, qsrc0:qsrc0 + TQ], dsem_q[qi], deps=[qz]))
                    if is_meta:
                        qt.append(P.dma("sync", Q[0:64, m, TQ:2 * TQ], qloc[s][512 + mp * 64:512 + (mp + 1) * 64, qsrc0:qsrc0 + TQ], dsem_q[qi], deps=[qz]))
                        qt.append(P.dma("sync", Q[64:64 + NAUG, m, 0:2 * TQ], qaug[s][idx, :, N:N + 2 * NMETA], dsem_q[qi], deps=[qz]))
                    else:
                        qt.append(P.dma("sync", Q[64:64 + NAUG, m, 0:TQ], qaug[s][idx, :, qcol0:qcol0 + TQ], dsem_q[qi], deps=[qz]))
            else:
                qt.append(P.dma("sync", Q[:, 0, 0:TQ], qloc[s][1024 + idx * 128:1024 + (idx + 1) * 128, qsrc0:qsrc0 + TQ], dsem_q[qi], deps=[qz]))
                hh = idx % 2
                qt.append(P.dma("sync", Q[hh * 64:(hh + 1) * 64, 1, 0:TQ], qloc[s][1536 + idx * 64:1536 + (idx + 1) * 64, qsrc0:qsrc0 + TQ], dsem_q[qi], deps=[qz]))
            qtok = qt[-1]
            blocks = []
            for (c0, w, bq, bk) in kblocks:
                if kind == "B" and bk == "g" and (not is_meta) and bq == qquarter:
                    blocks.append((c0, w, bq, "l", 0))
                    blocks.append((c0, w, bq, "l", 1))
                else:
                    blocks.append((c0, w, bq, bk, None))
            first = {}
            for (kc0, nkeys, bq, bk, lm) in blocks:
                si = S["blk"] % 3
                S["blk"] += 1
                KB, VB = kbuf[si], vbuf[si]
                sem = dsem_kv[si]
                dps = [slot_free[si]] + pd + [t_ones[si]]
                ntb = max(1, nkeys // 128)
                if bk == "m":
                    ksrc, kcs = kvm[s], 0
                else:
                    ksrc, kcs = kvl[s], kc0
                nrow = ksrc.shape[1]

                def vload(c0, w, dcol):
                    base = 1408 * nrow + kcs * 1152
                    if bk == "m":
                        return P.dma("sync", VB[0:NMETA, 0, dcol:dcol + w], bass.AP(ksrc.tensor, base + c0, [[1152, NMETA], [1, w]]), sem, deps=dps)
                    return P.dma("sync", VB[:, 0:ntb, dcol:dcol + w], bass.AP(ksrc.tensor, base + c0, [[1152, 128], [128 * 1152, ntb], [1, w]]), sem, deps=dps)
                kt = []
                if kind == "A":
                    kt.append(P.dma("sync", KB[:, 0, 0:nkeys], ksrc[0:128, kcs:kcs + nkeys], sem, deps=dps))
                    kt.append(vload(idx * 64, 64, 64))
                elif kind == "B":
                    for m in range(2):
                        mp = idx * 2 + (m if bk != "l" else lm)
                        kt.append(P.dma("sync", KB[0:64, m, 0:nkeys], ksrc[128 + mp * 64:128 + (mp + 1) * 64, kcs:kcs + nkeys], sem, deps=dps))
                        if bk == "l":
                            rel = kc0 - bq * Rq
                            kt.append(P.dma("sync", KB[64:64 + NAUG, m, 0:nkeys], kaugl[s][idx, m, :, rel:rel + nkeys], sem, deps=dps))
                        else:
                            kt.append(P.dma("sync", KB[64:64 + NAUG, m, 0:nkeys], kaug[s][idx, :, kc0:kc0 + nkeys], sem, deps=dps))
                    kt.append(vload(128 + idx * 128, 128, 0))
                else:
                    kt.append(P.dma("sync", KB[:, 0, 0:nkeys], ksrc[640 + idx * 128:640 + (idx + 1) * 128, kcs:kcs + nkeys], sem, deps=dps))
                    kt.append(P.dma("sync", KB[:, 1, 0:nkeys], ksrc[1152 + (idx // 2) * 128:1152 + (idx // 2 + 1) * 128, kcs:kcs + nkeys], sem, deps=dps))
                    kt.append(vload(640 + idx * 128, 128, 0))
                ktok = kt[-1]
                ktiles = [(0, NMETA)] if bk == "m" else [(t * 128, 128) for t in range(ntb)]
                for m in ([lm] if bk == "l" else list(range(nm))):
                    for ci, (q0, qw) in enumerate(qchunks):
                        qrel = (qcol0 - qquarter * Rq + q0) if not is_meta else 0
                        krel = kc0 - bq * Rq if bk != "m" else 0
                        fb = (m, ci) not in first
                        first[(m, ci)] = True
                        run_unit(S, kind, m, ci, q0, qw, is_meta, bk, KB, VB, Q, ktiles, ktok, qtok,
                                 oacc, sacc, pbuf, tmpA, tmpB, fb, qrel, krel)
                pipe_flush(S)
                slot_free[si] = S["last_pv"]
            qb_free[qi] = S["last_pv"]
            for m in range(nm):
                for ci, (q0, qw) in enumerate(qchunks):
                    M = 64 if kind == "A" else 128
                    if kind == "A":
                        lrow_ap = oacc[64:65, m, ci, 0:qw]
                        onesl = ones_f[64:65, 0:M]
                    else:
                        lrow_ap = sacc[0:1, m, ci, 0:qw]
                        onesl = ones_f[0:1, 0:M]
                    evac = S["last_evac"][(m, ci)]
                    bc = P.op("tensor", lambda e, lrow_ap=lrow_ap, onesl=onesl, M=M, qw=qw: e.matmul(
                        ps[0:M, 6, 0:qw], lhsT=onesl, rhs=lrow_ap, start=True, stop=True), deps=[evac, S["bank6"]])
                    r1 = P.op("vector", lambda e, M=M, qw=qw: e.reciprocal(out=rcp[0:M, 0:qw], in_=ps[0:M, 6, 0:qw]), deps=[bc, S["ep_free"]])
                    S["bank6"] = r1
                    dcol = (N if is_meta else qcol0) + q0
                    if kind == "B":
                        ost, key = ostB, 0
                        dst = oB[s][(idx * 2 + m) * 128:(idx * 2 + m + 1) * 128, dcol:dcol + qw]
                    elif kind == "A":
                        ost, key = ostA, 1
                        h = idx * 4 + m
                        dst = oA[s][h * 64:(h + 1) * 64, dcol:dcol + qw]
                    else:
                        ost, key = ostA, 1
                        dst = oC[s][idx * 128:(idx + 1) * 128, dcol:dcol + qw]
                    r2 = P.op("vector", lambda e, M=M, qw=qw, m=m, ci=ci, ost=ost: e.tensor_tensor(
                        out=ost[0:M, 0:qw], in0=oacc[0:M, m, ci, 0:qw], in1=rcp[0:M, 0:qw], op=ALU.mult),
                        deps=[r1, S["ost_free"][key]])
                    S["ep_free"] = r2
                    stt = P.dma("gpsimd", dst, ost[0:M, 0:qw], dsem_o, deps=[r2])
                    S["ost_free"][key] = stt
        phase_done = [S["last_pv"], (dsem_o.h, dsem_o.count), S["ep_free"]]

    PIPE_D = 2

    def pipe_push(S, back):
        S["pending"].append(back)
        while len(S["pending"]) > PIPE_D:
            S["pending"].pop(0)()

    def pipe_flush(S):
        while S["pending"]:
            S["pending"].pop(0)()

    def run_unit(S, kind, m, ci, q0, qw, is_meta, bk, KB, VB, Q, ktiles, ktok, qtok,
                 oacc, sacc, pbuf, tmpA, tmpB, first_block, qrel, krel):
        scale = {"A": 0.125, "B": 0.125, "C": 192 ** -0.5}[kind]
        shift = {"A": -8.0, "B": -8.0, "C": -(192 ** 0.5)}[kind]
        u = S["unit"]
        S["unit"] += 1
        ob = 3 + (u % 2)
        sum_ps = ps[0:1, 5, :] if (u % 2 == 0) else pstf[0:1, :]
        M = 65 if kind == "A" else 128
        nt = len(ktiles)
        KA = 64 + NAUG
        for ti, (k0, kw) in enumerate(ktiles):
            i = S["step"]
            S["step"] += 1
            sbk = i % 3
            pslot = i % 4
            diag = False
            var = 0
            if kind == "B":
                if bk == "m" and is_meta:
                    diag = True
                elif bk == "l":
                    if krel + k0 + kw <= qrel:
                        var = 0
                    elif krel + k0 >= qrel + qw:
                        var = 1
                    else:
                        diag = True
            sfree = S["exp_tok"].get(i - 3)

            def qk(bank, variant, xd=(), k0=k0, kw=kw):
                xd = list(xd)
                if kind == "A":
                    return P.op("tensor", lambda e: e.matmul(ps[0:kw, bank, 0:qw], lhsT=KB[:, 0, k0:k0 + kw], rhs=Q[:, m, q0:q0 + qw], start=True, stop=True), deps=[ktok, qtok, sfree] + xd)
                if kind == "C":
                    P.op("tensor", lambda e: e.matmul(ps[0:kw, bank, 0:qw], lhsT=KB[:, 0, k0:k0 + kw], rhs=Q[:, 0, q0:q0 + qw], start=True, stop=False), deps=[ktok, qtok, sfree] + xd)
                    return P.op("tensor", lambda e: e.matmul(ps[0:kw, bank, 0:qw], lhsT=KB[:, 1, k0:k0 + kw], rhs=Q[:, 1, q0:q0 + qw], start=False, stop=True), deps=[])
                if bk == "l":
                    kslice = KB[0:KA, variant, k0:k0 + kw]
                    qcol = q0
                elif bk == "m" and is_meta:
                    kslice = KB[0:KA, m, k0:k0 + kw]
                    qcol = q0 + (NMETA if variant == 1 else 0)
                else:
                    kslice = KB[0:KA, m, k0:k0 + kw]
                    qcol = q0
                return P.op("tensor", lambda e: e.matmul(ps[0:kw, bank, 0:qw], lhsT=kslice, rhs=Q[0:KA, m, qcol:qcol + qw], start=True, stop=True), deps=[ktok, qtok, sfree] + xd)
            pfree = S["pv_tok"].get(i - 4)
            if not diag:
                mm = qk(sbk, var)
                ex = P.op("scalar", lambda e: e.activation(out=pbuf[0:kw, pslot, 0:qw], in_=ps[0:kw, sbk, 0:qw], func=AF.Exp, scale=scale, bias=shift), deps=[mm, pfree])
            else:
                mm0 = qk(sbk, 0)
                mm1 = qk(6, 1, [S.get("diag_free"), S.get("bank6")])
                cA = P.op("scalar", lambda e: e.activation(out=tmpA[0:kw, 0:qw], in_=ps[0:kw, sbk, 0:qw], func=AF.Copy), deps=[mm0, S.get("diag_free")])
                mn = P.op("vector", lambda e: e.tensor_tensor(out=tmpB[0:kw, 0:qw], in0=ps[0:kw, 6, 0:qw], in1=tmpA[0:kw, 0:qw], op=ALU.min), deps=[cA, mm1, S.get("diag_free2")])
                S["diag_free"] = mn
                S["bank6"] = mn
                ex = P.op("scalar", lambda e: e.activation(out=pbuf[0:kw, pslot, 0:qw], in_=tmpB[0:kw, 0:qw], func=AF.Exp, scale=scale, bias=shift), deps=[mn, pfree])
                S["diag_free2"] = ex
            S["exp_tok"][i] = ex

            def back(i=i, ti=ti, k0=k0, kw=kw, pslot=pslot, ex=ex):
                if kind == "A":
                    lhs = VB[0:kw, k0 // 128, 64:129]
                else:
                    lhs = VB[0:kw, k0 // 128, 0:128]
                evac_dep = S["evac_by_unit"].get(u - 2) if ti == 0 else None
                pv = P.op("tensor", lambda e: e.matmul(ps[0:M, ob, 0:qw], lhsT=lhs, rhs=pbuf[0:kw, pslot, 0:qw], start=(ti == 0), stop=(ti == nt - 1)), deps=[ex, evac_dep])
                if kind != "A":
                    pv = P.op("tensor", lambda e: e.matmul(sum_ps[:, 0:qw], lhsT=ones_bf[0:kw, 0:1], rhs=pbuf[0:kw, pslot, 0:qw], start=(ti == 0), stop=(ti == nt - 1)), deps=[])
                S["pv_tok"][i] = pv
                S["last_pv"] = pv
                if ti != nt - 1:
                    return
                prev = S["last_evac"].get((m, ci))
                if first_block:
                    ev = P.op("vector", lambda e: e.tensor_copy(out=oacc[0:M, m, ci, 0:qw], in_=ps[0:M, ob, 0:qw]), deps=[pv, S["ep_free"]])
                    if kind != "A":
                        ev = P.op("vector", lambda e: e.tensor_copy(out=sacc[0:1, m, ci, 0:qw], in_=sum_ps[:, 0:qw]), deps=[pv, S["ep_free"]])
                else:
                    ev = P.op("vector", lambda e: e.tensor_tensor(out=oacc[0:M, m, ci, 0:qw], in0=ps[0:M, ob, 0:qw], in1=oacc[0:M, m, ci, 0:qw], op=ALU.add), deps=[pv, prev])
                    if kind != "A":
                        ev = P.op("vector", lambda e: e.tensor_tensor(out=sacc[0:1, m, ci, 0:qw], in0=sum_ps[:, 0:qw], in1=sacc[0:1, m, ci, 0:qw], op=ALU.add), deps=[pv, prev])
                S["last_evac"][(m, ci)] = ev
                S["evac_by_unit"][u] = ev
            pipe_push(S, back)

    def phase_p5(l, groups, xsrc_reg, xsrc_meta, dst_fn):
        nonlocal phase_done
        if skip_phase():
            return
        P.reset_barrier()
        state["wstg_free"] = [None, None]
        pd = []
        tl0 = P.dma("sync", lrow[0:1, 262:264], lamc, dsem_misc, deps=pd)
        tl1 = P.dma("sync", lamt[:, 1:2], bcast_row(lamc[0:1, 1:2], 1), dsem_misc, deps=pd)
        Wg = wview(0, 8, 3072)
        Wb = [wview(8 * 3072 + i * 4 * 1024, 4, 1024) for i in range(3)]
        Wo = wview(8 * 3072 + 12 * 1024, 8, 1024)
        wt = load_w(Wg, w_in[l][:, NQKV:5824], deps=pd)
        for i in range(3):
            wt += load_w(Wb[i], w_br[i][l], deps=pd)
        wt += load_w(Wo, w_out[l], deps=pd)
        t1 = P.dma("sync", gT[:], attn_g[l].rearrange("(k p) -> p k", p=128), dsem_misc, deps=pd)
        t1 = P.dma("sync", gT2[:], mlp_g[l].rearrange("(k p) -> p k", p=128), dsem_misc, deps=pd)
        t1 = P.dma("sync", bgT[:], b_gate[l].rearrange("(k p) -> p k", p=128), dsem_misc, deps=pd)
        t1 = P.dma("sync", gsub[:], b_sub[l].rearrange("(p o) -> p o", o=1), dsem_misc, deps=pd)
        for j, nm in enumerate(["b_lambda_q1", "b_lambda_k1", "b_lambda_q2", "b_lambda_k2"]):
            t1 = P.dma("sync", lrow[0:1, j * 64:(j + 1) * 64], g64[nm][l:l + 1, :], dsem_misc, deps=pd)
        la = P.op("vector", lambda e: e.tensor_tensor(out=lrow[0:1, 0:64], in0=lrow[0:1, 0:64], in1=lrow[0:1, 64:128], op=ALU.mult), deps=[t1])
        lb = P.op("vector", lambda e: e.tensor_tensor(out=lrow[0:1, 128:192], in0=lrow[0:1, 128:192], in1=lrow[0:1, 192:256], op=ALU.mult), deps=[t1])
        lc = P.op("vector", lambda e: e.tensor_reduce(out=lrow[0:1, 256:257], in_=lrow[0:1, 0:64], axis=AX.X, op=ALU.add), deps=[la])
        ld_ = P.op("vector", lambda e: e.tensor_reduce(out=lrow[0:1, 257:258], in_=lrow[0:1, 128:192], axis=AX.X, op=ALU.add), deps=[lb])
        le = P.op("scalar", lambda e: e.activation(out=lrow[0:1, 258:260], in_=lrow[0:1, 256:258], func=AF.Exp), deps=[lc, ld_])
        lf = P.op("vector", lambda e: e.tensor_tensor(out=lrow[0:1, 260:261], in0=lrow[0:1, 259:260], in1=lrow[0:1, 258:259], op=ALU.subtract), deps=[le])
        lg = P.op("vector", lambda e: e.tensor_scalar(out=lrow[0:1, 261:262], in0=lrow[0:1, 260:261], scalar1=lrow[0:1, 262:263], scalar2=None, op0=ALU.add), deps=[lf, tl0])
        lh = P.op("tensor", lambda e: e.matmul(ps[:, 6, 0:1], lhsT=ones_f[0:1, :], rhs=lrow[0:1, 261:262], start=True, stop=True), deps=[lg] + pd)
        li = P.op("vector", lambda e: e.tensor_copy(out=lamt[:, 0:1], in_=ps[:, 6, 0:1]), deps=[lh])
        lj = P.op("vector", lambda e: e.tensor_scalar(out=gsub[:], in0=gsub[:], scalar1=lamt[:, 1:2], scalar2=None, op0=ALU.mult), deps=[t1, tl1])
        setup = wt + [li, lj]

        GW = 256
        xt = work(0, [128, 2, 1024])
        hT = work(2048, [128, 8, GW], BF16)
        gTt = work(3072, [128, 24, GW], BF16)
        oAs = work(6144, [128, 4, GW], BF16)
        oCs = work(6656, [128, 4, GW], BF16)
        oBs = work(7168, [128, 8, GW])
        dd = work(9216, [128, GW])
        sqv = work(9472, [128, GW])
        obn = work(9728, [128, 4, GW], BF16)
        mT = work(10240, [128, 8, GW], BF16)
        macc = work(11264, [128, GW])
        mtmp = work(11520, [128, GW])
        x1t = work(11776, [128, 1024])
        prevg = None
        st1 = None
        for gi, (s, row0, ntok, tok0) in enumerate(groups):
            ntile = (ntok + 127) // 128
            pg = prevg
            ld = []
            if s is None:
                mz = P.op("gpsimd", lambda e: e.memset(xt[:, 0, :], 0.0), deps=[pg] + pd)
                ld.append(P.dma("sync", xt[0:NMT, 0, :], xsrc_meta, dsem_x[0], deps=[mz]))
            else:
                for t in range(ntile):
                    ld.append(P.dma("sync", xt[:, t, :], xsrc_reg[row0 + t * 128:row0 + (t + 1) * 128, :], dsem_x[0], deps=[pg] + pd))
            fl = []
            for t in range(ntile):
                fl.append(rmsnorm_T(xt[:, t, :], gT, hT, t * 128, [ld[-1], pg] + setup + fl, 11776))
            W_ = ntile * 128
            ol = []
            if s is None:
                mz2 = P.op("gpsimd", lambda e: e.memset(WORK[:, 6144:9216], 0.0), deps=[pg] + pd)
                for s4 in range(2):
                    c0 = NSEQ[s4]
                    ol.append(P.dma("sync", oAs[:, :, s4 * 16:(s4 + 1) * 16], oA[s4][:, c0:c0 + NMETA].rearrange("(b p) t -> p b t", p=128), dsem_x[1], deps=[mz2]))
                    ol.append(P.dma("sync", oCs[:, :, s4 * 16:(s4 + 1) * 16], oC[s4][:, c0:c0 + NMETA].rearrange("(b p) t -> p b t", p=128), dsem_x[1], deps=[mz2]))
                    ol.append(P.dma("sync", oBs[:, :, s4 * 16:(s4 + 1) * 16], oB[s4][:, c0:c0 + NMETA].rearrange("(b p) t -> p b t", p=128), dsem_x[1], deps=[mz2]))
            else:
                ol.append(P.dma("sync", oAs[:, :, 0:ntok], oA[s][:, tok0:tok0 + ntok].rearrange("(b p) t -> p b t", p=128), dsem_x[1], deps=[pg] + pd))
                ol.append(P.dma("sync", oCs[:, :, 0:ntok], oC[s][:, tok0:tok0 + ntok].rearrange("(b p) t -> p b t", p=128), dsem_x[1], deps=[pg] + pd))
                ol.append(P.dma("sync", oBs[:, :, 0:ntok], oB[s][:, tok0:tok0 + ntok].rearrange("(b p) t -> p b t", p=128), dsem_x[1], deps=[pg] + pd))
            olt = ol[-1]
            lastb = None
            for h in range(4):
                d1 = P.op("vector", lambda e, h=h: e.scalar_tensor_tensor(out=dd[:, 0:W_], in0=oBs[:, 2 * h + 1, 0:W_], scalar=lamt[:, 0:1], in1=oBs[:, 2 * h, 0:W_], op0=ALU.mult, op1=ALU.add), deps=[olt, lastb] + setup)
                d2 = P.op("scalar", lambda e: e.activation(out=sqv[:, 0:W_], in_=dd[:, 0:W_], func=AF.Square), deps=[d1, lastb])
                d3 = P.op("tensor", lambda e: e.matmul(ps[:, 5, 0:W_], lhsT=mean_f[:, :], rhs=sqv[:, 0:W_], start=True, stop=True), deps=[d2, lastb] + fl)
                d4 = P.op("scalar", lambda e: e.activation(out=sqv[:, 0:W_], in_=ps[:, 5, 0:W_], func=AF.Sqrt, bias=EPS), deps=[d3])
                d5 = P.op("vector", lambda e: e.reciprocal(out=sqv[:, 0:W_], in_=sqv[:, 0:W_]), deps=[d4])
                d6 = P.op("vector", lambda e: e.tensor_tensor(out=dd[:, 0:W_], in0=dd[:, 0:W_], in1=sqv[:, 0:W_], op=ALU.mult), deps=[d5])
                lastb = P.op("vector", lambda e, h=h: e.tensor_scalar(out=obn[:, h, 0:W_], in0=dd[:, 0:W_], scalar1=gsub[:, 0:1], scalar2=None, op0=ALU.mult), deps=[d6, pg])
            gacts = []
            for j in range(24):
                bank = j % 3
                mm = None
                for k in range(8):
                    mm = P.op("tensor", lambda e, j=j, k=k, bank=bank: e.matmul(ps[:, bank, 0:W_], lhsT=Wg[:, k, j * 128:(j + 1) * 128], rhs=hT[:, k, 0:W_], start=(k == 0), stop=(k == 7)), deps=fl + [gacts[j - 3] if j >= 3 else None, pg])
                gacts.append(P.op("scalar", lambda e, j=j, bank=bank: e.activation(out=gTt[:, j, 0:W_], in_=ps[:, bank, 0:W_], func=AF.Sigmoid, bias=bgT[:, j:j + 1]), deps=[mm, pg, lastb]))
            gates_done = gacts[-1]
            srcs = [oAs, obn, oCs]
            ml = None
            for j in range(8):
                mms = []
                for br in range(3):
                    mm = None
                    for k in range(4):
                        mm = P.op("tensor", lambda e, j=j, br=br, k=k: e.matmul(ps[:, br, 0:W_], lhsT=Wb[br][:, k, j * 128:(j + 1) * 128], rhs=srcs[br][:, k, 0:W_], start=(k == 0), stop=(k == 3)), deps=[gates_done, olt, lastb, ml])
                    mms.append(mm)
                a0 = P.op("vector", lambda e, j=j: e.tensor_tensor(out=macc[:, 0:W_], in0=ps[:, 0, 0:W_], in1=gTt[:, j, 0:W_], op=ALU.mult), deps=[mms[0], gates_done, ml])
                a1 = P.op("vector", lambda e, j=j: e.tensor_tensor(out=mtmp[:, 0:W_], in0=ps[:, 1, 0:W_], in1=gTt[:, 8 + j, 0:W_], op=ALU.mult), deps=[mms[1], ml])
                a2 = P.op("vector", lambda e: e.tensor_tensor(out=macc[:, 0:W_], in0=macc[:, 0:W_], in1=mtmp[:, 0:W_], op=ALU.add), deps=[a0, a1])
                a3 = P.op("vector", lambda e, j=j: e.tensor_tensor(out=mtmp[:, 0:W_], in0=ps[:, 2, 0:W_], in1=gTt[:, 16 + j, 0:W_], op=ALU.mult), deps=[mms[2], a2])
                ml = P.op("vector", lambda e, j=j: e.tensor_tensor(out=mT[:, j, 0:W_], in0=macc[:, 0:W_], in1=mtmp[:, 0:W_], op=ALU.add), deps=[a3, pg])
            last = ml
            for t in range(ntile):
                mm = None
                for c in range(2):
                    for k in range(8):
                        mm = P.op("tensor", lambda e, t=t, c=c, k=k: e.matmul(ps[:, 3 + c, :], lhsT=mT[:, k, t * 128:(t + 1) * 128], rhs=Wo[:, k, c * 512:(c + 1) * 512], start=(k == 0), stop=(k == 7)), deps=[ml, last])
                nrow = min(128, ntok - t * 128)
                ad = P.op("vector", lambda e, t=t: e.tensor_tensor(out=x1t[:, :], in0=ps[:, 3:5, :].rearrange("p c n -> p (c n)"), in1=xt[:, t, :], op=ALU.add), deps=[mm, st1])
                st1 = P.dma("gpsimd", x1buf[row0 + t * 128:row0 + t * 128 + nrow, :], x1t[0:nrow, :], dsem_st, deps=[ad])
                last = ad
            prevg = [last, st1]
        P.reset_barrier()
        state["wstg_free"] = [None, None]
        pd = []
        Wu = wview(0, 8, 4096)
        Wd = wview(8 * 4096, 32, 1024)
        wt = load_w(Wu, w_up[l], deps=pd)
        wt += load_w(Wd, w_down[l], deps=pd)
        setup = wt
        xt = work(0, [128, 2, 1024])
        hT = work(2048, [128, 8, GW], BF16)
        aT = work(3072, [128, 32, GW], BF16)
        rl = [work(7168, [128, GW]), work(7424, [128, GW])]
        yt = work(7680, [128, 1024])
        prevg = None
        sto = None
        for gi, (s, row0, ntok, tok0) in enumerate(groups):
            ntile = (ntok + 127) // 128
            pg = prevg
            ld = []
            if s is None:
                mz = P.op("gpsimd", lambda e: e.memset(xt[:, 0, :], 0.0), deps=[pg] + pd)
                ld.append(P.dma("sync", xt[0:NMT, 0, :], x1buf[TOK:TALL, :], dsem_x[0], deps=[mz]))
            else:
                for t in range(ntile):
                    ld.append(P.dma("sync", xt[:, t, :], x1buf[row0 + t * 128:row0 + (t + 1) * 128, :], dsem_x[0], deps=[pg] + pd))
            fl = []
            for t in range(ntile):
                fl.append(rmsnorm_T(xt[:, t, :], gT2, hT, t * 128, [ld[-1], pg] + setup + fl, 8704))
            W_ = ntile * 128
            rq = []
            sq_all = []
            for j in range(32):
                bank = j % 3
                mm = None
                for k in range(8):
                    mm = P.op("tensor", lambda e, j=j, k=k, bank=bank: e.matmul(ps[:, bank, 0:W_], lhsT=Wu[:, k, j * 128:(j + 1) * 128], rhs=hT[:, k, 0:W_], start=(k == 0), stop=(k == 7)), deps=fl + [rq[j - 3] if j >= 3 else None, pg])
                r_ = P.op("scalar", lambda e, j=j, bank=bank: e.activation(out=rl[j % 2][:, 0:W_], in_=ps[:, bank, 0:W_], func=AF.Relu), deps=[mm, sq_all[j - 2] if j >= 2 else None, pg])
                rq.append(r_)
                sq_all.append(P.op("gpsimd" if j % 2 else "vector", lambda e, j=j: e.tensor_tensor(out=aT[:, j, 0:W_], in0=rl[j % 2][:, 0:W_], in1=rl[j % 2][:, 0:W_], op=ALU.mult), deps=[r_, pg]))
            last = None
            for t in range(ntile):
                mm = None
                for c in range(2):
                    for k in range(32):
                        mm = P.op("tensor", lambda e, t=t, c=c, k=k: e.matmul(ps[:, 3 + c, :], lhsT=aT[:, k, t * 128:(t + 1) * 128], rhs=Wd[:, k, c * 512:(c + 1) * 512], start=(k == 0), stop=(k == 31)), deps=[sq_all[-1], sq_all[-2], last])
                nrow = min(128, ntok - t * 128)
                ad = P.op("vector", lambda e, t=t: e.tensor_tensor(out=yt[:, :], in0=ps[:, 3:5, :].rearrange("p c n -> p (c n)"), in1=xt[:, t, :], op=ALU.add), deps=[mm, sto])
                sto = P.dma("gpsimd", dst_fn(gi, t, nrow), yt[0:nrow, :], dsem_st, deps=[ad])
                last = ad
            prevg = [last, sto]
        phase_done = [prevg, (dsem_st.h, dsem_st.count)]

    GW = 256
    l = 0
    phase_p1(l, x_in, meta_in)
    for s in range(2):
        for (g0, gn) in chunks(RQ[s], QG_[s]):
            phase_att(l, s, g0, gn, 0, False)
        phase_att(l, s, NSEQ[s], NMETA, 4, True)
    groups = []
    yoff = [0, RQ[0]]
    for s in range(2):
        for (o, w) in chunks(RQ[s], GW):
            groups.append((s, OFF[s] + o, w, o))
    groups.append((None, TOK, NMT, 0))

    def dst_fn(gi, t, nrow, groups=groups, yoff=yoff):
        s, row0, ntok, tok0 = groups[gi]
        if s is None:
            return y_out[TOKQ:TOKQ + NMT, :]
        r = yoff[s] + tok0 + t * 128
        return y_out[r:r + nrow, :]
    phase_p5(l, groups, x_in, meta_in, dst_fn)
    P.reset_barrier()
    fin = P.dma("sync", y_out[0:1, 0:128], ident_in[0:1, :], dsem_misc) if stop < 10 ** 9 else None
    P.wait_only("sync", [fin])

    with nc.allow_non_contiguous_dma(reason="small strided param/V loads"), nc.Block() as block:
        P.replay(block)
    es.close()
    return nc


def _local_order(c, N):
    r = c % 4
    Rq = N // 4
    order = [r] + [g for g in range(4) if g != r]
    return np.concatenate([np.arange(g * Rq, (g + 1) * Rq) for g in order]), order


def _tables(NP, NS, c):
    f32 = np.float32
    NSEQ = [NP, NS]
    TOK = NP + NS

    def rope_tab(pos, d):
        half = d // 2
        inv = (f32(THETA) ** (f32(-2.0) * np.arange(half, dtype=f32) / f32(d))).astype(f32)
        ang = (pos.astype(f32)[:, None] * inv[None, :]).astype(f32)
        return np.cos(ang).astype(f32), np.sin(ang).astype(f32)

    ropeA = np.zeros((TOK + 2 * NMETA, 128), f32)
    ropeC = np.zeros((TOK + 2 * NMETA, 128), f32)

    def fill(rows, row, col, lin):
        cr, sr = rope_tab(row, 32)
        cc, sc = rope_tab(col, 32)
        ropeA[rows, 0:64] = np.concatenate([cr, cr, cc, cc], axis=1)
        ropeA[rows, 64:128] = np.concatenate([-sr, sr, -sc, sc], axis=1)
        cl, sl = rope_tab(lin, 64)
        ropeC[rows, 0:64] = np.concatenate([cl, cl], axis=1)
        ropeC[rows, 64:128] = np.concatenate([-sl, sl], axis=1)
    off = 0
    m = np.arange(NMETA)
    slopes = [2.0 ** (-8.0 * (h + 1) / 4) for h in range(4)]
    qaug, kaug, kaugl = [], [], []
    for s in range(2):
        N = NSEQ[s]
        Rq = N // 4
        n, order = _local_order(c, N)
        fill(slice(off, off + N), (n // GRID_W).astype(f32), (n % GRID_W).astype(f32), (NMETA + n).astype(f32))
        off += N
        fill(slice(TOK + s * NMETA, TOK + (s + 1) * NMETA), np.full(NMETA, -1.0, f32), m.astype(f32), m.astype(f32))
        pos = NMETA + n
        lq = np.arange(N) // Rq
        gq = np.array(order)[lq]
        irel = np.arange(N) % Rq
        qa = np.zeros((4, NAUG, N + 2 * NMETA), np.float64)
        ka = np.zeros((4, NAUG, N + NMETA), np.float64)
        kl = np.zeros((4, 2, NAUG, Rq), np.float64)
        for h in range(4):
            c8 = 8.0 * slopes[h]
            hi, lo = pos // 128, pos % 128
            for b in range(5):
                r0 = 4 * b
                if b < 4:
                    sig = np.sign(gq - order[b]).astype(np.float64)
                else:
                    sig = np.ones(N)
                qa[h, r0 + 0, 0:N] = sig * (-c8 * 128.0) * hi
                qa[h, r0 + 1, 0:N] = sig * (-c8) * lo
                qa[h, r0 + 2, 0:N] = sig
                qa[h, r0 + 3, 0:N] = sig
                for rep in range(2):
                    sg = -1.0 if (b < 4 or rep == 1) else 1.0
                    cs = slice(N + rep * NMETA, N + (rep + 1) * NMETA)
                    qa[h, r0 + 0, cs] = 0.0
                    qa[h, r0 + 1, cs] = sg * (-c8) * m
                    qa[h, r0 + 2, cs] = sg
                    qa[h, r0 + 3, cs] = sg
                if b < 4:
                    sel = np.where(lq == b)[0]
                    ka[h, r0 + 0, sel] = 1.0
                    ka[h, r0 + 1, sel] = 1.0
                    ka[h, r0 + 2, sel] = c8 * 128.0 * hi[sel]
                    ka[h, r0 + 3, sel] = c8 * lo[sel]
                else:
                    ka[h, r0 + 0, N:N + NMETA] = 1.0
                    ka[h, r0 + 1, N:N + NMETA] = 1.0
                    ka[h, r0 + 2, N:N + NMETA] = 0.0
                    ka[h, r0 + 3, N:N + NMETA] = c8 * m
            qa[h, 20, 0:N] = (-c8 * 128.0) * (irel // 128)
            qa[h, 21, 0:N] = (-c8) * (irel % 128)
            qa[h, 22, 0:N] = 1.0
            qa[h, 23, 0:N] = 1.0
            j = np.arange(Rq)
            for v, sg in ((0, 1.0), (1, -1.0)):
                kl[h, v, 20, :] = sg
                kl[h, v, 21, :] = sg
                kl[h, v, 22, :] = sg * c8 * 128.0 * (j // 128)
                kl[h, v, 23, :] = sg * c8 * (j % 128)
        qaug.append(qa.reshape(4 * NAUG, -1).astype(f32))
        kaug.append(ka.reshape(4 * NAUG, -1).astype(f32))
        kaugl.append(kl.reshape(8 * NAUG, -1).astype(f32))
    return ropeA, ropeC, qaug, kaug, kaugl


_CACHE = {}


def run(inputs, NP, NS, **kw):
    key = (NP, NS, tuple(sorted(kw.items())))
    if key not in _CACHE:
        _CACHE[key] = build_program(NP, NS, **kw)
    nc = _CACHE[key]
    xcur = [np.asarray(inputs["x_prompt"], dtype=np.float32), np.asarray(inputs["x_sample"], dtype=np.float32)]
    meta = np.asarray(inputs["meta_tokens"], dtype=np.float32)
    mcur = [[meta, meta], [meta, meta]]
    PN = ["b_gate", "attn_norm_g", "mlp_norm_g", "a_q_norm_g", "a_k_norm_g", "b_q_norm_g", "b_k_norm_g",
          "b_lambda_q1", "b_lambda_k1", "b_lambda_q2", "b_lambda_k2", "b_subln_g", "c_q_a_norm_g",
          "c_kv_a_norm_g", "c_q_norm_g", "c_k_norm_g"]
    tabs = [_tables(NP, NS, c) for c in range(NCORE)]
    perms = [(_local_order(c, NP)[0], _local_order(c, NS)[0]) for c in range(NCORE)]
    RqP, RqS = NP // 4, NS // 4
    for l in range(DEPTH):
        shared = {}
        shared["params"] = np.ascontiguousarray(np.concatenate([np.asarray(inputs[k], dtype=np.float32)[l:l + 1] for k in PN], axis=1))
        for k in ["w_in", "w_up", "w_down", "w_out", "w_branch_a", "w_branch_b", "w_branch_c", "c_w_q_b", "c_w_kv_b"]:
            shared[k] = np.ascontiguousarray(np.asarray(inputs[k], dtype=np.float32)[l])
        lam_init = 0.8 - 0.6 * math.exp(-0.3 * l)
        shared["lamc"] = np.array([[-lam_init, 1.0 - lam_init]], np.float32)
        shared["ident_in"] = np.eye(128, dtype=np.float32)
        in_maps = []
        for c in range(NCORE):
            b = c // 4
            ropeA, ropeC, qaug, kaug, kaugl = tabs[c]
            d = dict(shared)
            d["x_in"] = np.ascontiguousarray(np.concatenate([xcur[0][b][perms[c][0]], xcur[1][b][perms[c][1]]], axis=0))
            d["meta_in"] = np.ascontiguousarray(np.concatenate([mcur[b][0], mcur[b][1]], axis=0))
            d["ropeA"], d["ropeC"] = ropeA, ropeC
            for s in range(2):
                d["qaug%d" % s] = qaug[s]
                d["kaug%d" % s] = kaug[s]
                d["kaugl%d" % s] = kaugl[s]
            in_maps.append(d)
        res = run_bass_kernel_spmd(nc, in_maps, core_ids=list(range(NCORE)))
        y_prompt = np.zeros((2, NP, D), np.float32)
        y_sample = np.zeros((2, NS, D), np.float32)
        mnew = [[None, None], [None, None]]
        for c in range(NCORE):
            b, r = c // 4, c % 4
            y = res.results[c]["y_out"]
            y_prompt[b, r * RqP:(r + 1) * RqP] = y[0:RqP]
            y_sample[b, r * RqS:(r + 1) * RqS] = y[RqP:RqP + RqS]
            if r == 0:
                mnew[b][0] = np.ascontiguousarray(y[RqP + RqS:RqP + RqS + NMETA])
                mnew[b][1] = np.ascontiguousarray(y[RqP + RqS + NMETA:RqP + RqS + 2 * NMETA])
        xcur = [y_prompt, y_sample]
        mcur = mnew
    return xcur[0], xcur[1]


def kernel(**inputs):
    return run(inputs, inputs["x_prompt"].shape[1], inputs["x_sample"].shape[1])
```
